# Optimizing a Trainium2 kernel written in Bass

```python
import math
import jax, jax.numpy as jnp
from jax import lax
import numpy as np

D_MODEL = 1024
BATCH = 4
SEQ = 4096
DEPTH = 4

HEAD_DIM = 64
NORM_EPS = 1e-6
GMLP_GROUPS = 8
GMLP_WIDTH = GMLP_GROUPS * HEAD_DIM
GMLP_CHUNK = 128
FOX_HEADS = 8
FOX_WIDTH = FOX_HEADS * HEAD_DIM
FOX_Q_BLOCK = 128
MOBA_HEADS = 8
MOBA_WIDTH = MOBA_HEADS * HEAD_DIM
MOBA_BLOCK = 256
MOBA_TOPK = 3
MOBA_Q_CHUNK = 16
ROPE_THETA = 500000.0
ROPE_DIM = HEAD_DIM // 4
SSM_HEADS = 12
SSM_HEAD_DIM = 64
SSM_WIDTH = SSM_HEADS * SSM_HEAD_DIM
SSM_GROUPS = 2
SSM_STATE = 128
SSM_CONV = 4
SSM_CHUNK = 128
SSM_CONV_DIM = SSM_WIDTH + 2 * SSM_GROUPS * SSM_STATE
N_BRANCH = 4
D_FF = 4 * D_MODEL
IN_SPLITS = (2 * GMLP_WIDTH, 3 * FOX_WIDTH, FOX_HEADS, 3 * MOBA_WIDTH,
             SSM_WIDTH, SSM_CONV_DIM, SSM_HEADS, N_BRANCH * D_MODEL)
IN_COLS = sum(IN_SPLITS)

kernel_name = 'conditioned_hybrid_gmlp_fox_moba_ssd_trunk'


def split_cols(t, sizes):
    outs, start = [], 0
    for s in sizes:
        outs.append(t[..., start:start + s])
        start += s
    return outs


def rms_norm(x, w):
    xf = x.astype(jnp.float32)
    xn = xf * lax.rsqrt(jnp.mean(xf * xf, axis=-1, keepdims=True) + NORM_EPS)
    return xn.astype(x.dtype) * w


def layer_norm(x, w, b):
    xf = x.astype(jnp.float32)
    xc = xf - jnp.mean(xf, axis=-1, keepdims=True)
    xn = xc * lax.rsqrt(jnp.mean(xc * xc, axis=-1, keepdims=True) + NORM_EPS)
    return xn.astype(x.dtype) * w + b


def partial_rotary(x):
    S = x.shape[1]
    half = ROPE_DIM // 2
    inv_freq = ROPE_THETA ** (-jnp.arange(half, dtype=jnp.float32) / half)
    ang = jnp.arange(S, dtype=jnp.float32)[:, None] * inv_freq[None, :]
    cos = jnp.cos(ang)[None, :, None, :]
    sin = jnp.sin(ang)[None, :, None, :]
    xr = x[..., :ROPE_DIM].astype(jnp.float32)
    x1, x2 = xr[..., :half], xr[..., half:]
    rot = jnp.concatenate([x1 * cos - x2 * sin, x2 * cos + x1 * sin], axis=-1).astype(x.dtype)
    return jnp.concatenate([rot, x[..., ROPE_DIM:]], axis=-1)


def chunked_gmlp(u, v, ln_w, ln_b, w_s, b_s):
    Bn, S, _ = v.shape
    vn = layer_norm(v, ln_w, ln_b).reshape(Bn, S // GMLP_CHUNK, GMLP_CHUNK, GMLP_GROUPS, HEAD_DIM)
    causal = jnp.tril(jnp.ones((GMLP_CHUNK, GMLP_CHUNK), dtype=bool))
    ws = jnp.where(causal[None], w_s, 0.0)
    mixed = jnp.einsum('gts,bnsgc->bntgc', ws, vn) + b_s.T[None, None, :, :, None]
    return u * mixed.reshape(Bn, S, GMLP_WIDTH)


def forgetting_attention(q, k, v, f_logit, f_bias):
    Bn, S, H, Dh = q.shape
    log_f = jax.nn.log_sigmoid((f_logit + f_bias).astype(jnp.float32))
    cum = jnp.cumsum(log_f, axis=1).transpose(0, 2, 1)
    qh, kh, vh = (t.transpose(0, 2, 1, 3) for t in (q, k, v))
    scale = Dh ** -0.5
    key_pos = jnp.arange(S)

    def block(i):
        start = i * FOX_Q_BLOCK
        qb = lax.dynamic_slice_in_dim(qh, start, FOX_Q_BLOCK, axis=2)
        cq = lax.dynamic_slice_in_dim(cum, start, FOX_Q_BLOCK, axis=2)
        s = jnp.einsum('bhqd,bhkd->bhqk', qb, kh, preferred_element_type=jnp.float32) * scale
        s = s + (cq[..., :, None] - cum[..., None, :])
        q_pos = start + jnp.arange(FOX_Q_BLOCK)
        s = jnp.where(q_pos[:, None] >= key_pos[None, :], s, -jnp.inf)
        p = jax.nn.softmax(s, axis=-1).astype(v.dtype)
        return jnp.einsum('bhqk,bhkd->bhqd', p, vh)

    out = lax.map(block, jnp.arange(S // FOX_Q_BLOCK))
    return out.transpose(1, 0, 3, 2, 4).reshape(Bn, S, H * Dh)


def moba_attention(q, k, v):
    Bn, S, H, Dh = q.shape
    nb = -(-S // MOBA_BLOCK)
    pad = nb * MOBA_BLOCK - S
    topk = min(MOBA_TOPK, nb)
    qh = q.transpose(0, 2, 1, 3)
    padw = ((0, 0), (0, pad), (0, 0), (0, 0))
    kp = jnp.pad(k, padw).transpose(0, 2, 1, 3).reshape(Bn, H, nb, MOBA_BLOCK, Dh)
    vp = jnp.pad(v, padw).transpose(0, 2, 1, 3).reshape(Bn, H, nb, MOBA_BLOCK, Dh)
    k_mean = jnp.mean(kp.astype(jnp.float32), axis=3)
    scale = Dh ** -0.5
    blk = jnp.arange(nb)
    in_blk = jnp.arange(MOBA_BLOCK)
    b_idx = jnp.arange(Bn)[:, None, None, None]
    h_idx = jnp.arange(H)[None, :, None, None]

    def chunk(i):
        start = i * MOBA_Q_CHUNK
        own = start // MOBA_BLOCK
        qc = lax.dynamic_slice_in_dim(qh, start, MOBA_Q_CHUNK, axis=2)
        q_pos = start + jnp.arange(MOBA_Q_CHUNK)
        gate = jnp.einsum('bhqd,bhnd->bhqn', qc.astype(jnp.float32), k_mean)
        gate = jnp.where(blk < own, gate, -jnp.inf)
        _, sel = lax.top_k(gate, topk)
        valid = sel < own
        k_sel = kp[b_idx, h_idx, sel]
        v_sel = vp[b_idx, h_idx, sel]
        s_sel = jnp.einsum('bhqd,bhqntd->bhqnt', qc, k_sel, preferred_element_type=jnp.float32) * scale
        s_sel = jnp.where(valid[..., None], s_sel, -jnp.inf).reshape(Bn, H, MOBA_Q_CHUNK, topk * MOBA_BLOCK)
        k_own = lax.dynamic_index_in_dim(kp, own, axis=2, keepdims=False)
        v_own = lax.dynamic_index_in_dim(vp, own, axis=2, keepdims=False)
        s_own = jnp.einsum('bhqd,bhtd->bhqt', qc, k_own, preferred_element_type=jnp.float32) * scale
        key_pos = own * MOBA_BLOCK + in_blk
        s_own = jnp.where(key_pos[None, :] <= q_pos[:, None], s_own, -jnp.inf)
        p = jax.nn.softmax(jnp.concatenate([s_sel, s_own], axis=-1), axis=-1).astype(v.dtype)
        p_sel = p[..., :topk * MOBA_BLOCK].reshape(Bn, H, MOBA_Q_CHUNK, topk, MOBA_BLOCK)
        p_own = p[..., topk * MOBA_BLOCK:]
        return (jnp.einsum('bhqnt,bhqntd->bhqd', p_sel, v_sel)
                + jnp.einsum('bhqt,bhtd->bhqd', p_own, v_own))

    out = lax.map(chunk, jnp.arange(S // MOBA_Q_CHUNK))
    return out.transpose(1, 0, 3, 2, 4).reshape(Bn, S, H * Dh)


def causal_depthwise_conv(x, w, b):
    K, C = w.shape
    y = lax.conv_general_dilated(x, w[:, None, :].astype(x.dtype), window_strides=(1,),
                                 padding=[(K - 1, 0)], dimension_numbers=('NWC', 'WIO', 'NWC'),
                                 feature_group_count=C)
    return y + b


def segsum(a):
    T = a.shape[-1]
    cs = jnp.cumsum(a, axis=-1)
    d = cs[..., :, None] - cs[..., None, :]
    return jnp.where(jnp.tril(jnp.ones((T, T), dtype=bool)), d, -jnp.inf)


def ssd_chunked_scan(x, dt, a, bm, cm):
    Bn, S, H, P = x.shape
    G, N = bm.shape[-2:]
    R = H // G
    nc, L = S // SSM_CHUNK, SSM_CHUNK
    xdt = (x.astype(jnp.float32) * dt[..., None]).reshape(Bn, nc, L, G, R, P)
    adt = (dt * a).reshape(Bn, nc, L, G, R).transpose(0, 3, 4, 1, 2)
    bc = bm.astype(jnp.float32).reshape(Bn, nc, L, G, N)
    cc = cm.astype(jnp.float32).reshape(Bn, nc, L, G, N)
    a_cum = jnp.cumsum(adt, axis=-1)
    decay_in = jnp.exp(segsum(adt))
    y_diag = jnp.einsum('bclgn,bcsgn,bgrcls,bcsgrp->bclgrp', cc, bc, decay_in, xdt)
    decay_to_end = jnp.exp(a_cum[..., -1:] - a_cum)
    states = jnp.einsum('bcsgn,bgrcs,bcsgrp->bcgrpn', bc, decay_to_end, xdt)
    chunk_decay = jnp.exp(a_cum[..., -1])

    def step(h, inp):
        st, dec = inp
        return h * dec[..., None, None] + st, h

    h0 = jnp.zeros((Bn, G, R, P, N), jnp.float32)
    _, prev = lax.scan(step, h0, (states.transpose(1, 0, 2, 3, 4, 5), chunk_decay.transpose(3, 0, 1, 2)))
    y_off = jnp.einsum('bclgn,cbgrpn,bgrcl->bclgrp', cc, prev, jnp.exp(a_cum))
    return (y_diag + y_off).reshape(Bn, S, H, P)


def mamba2_mixer(z, xbc, dt_raw, conv_w, conv_b, dt_bias, a_log, d_skip, norm_w):
    Bn, S, _ = z.shape
    xbc = jax.nn.silu(causal_depthwise_conv(xbc, conv_w, conv_b))
    xs, bm, cm = split_cols(xbc, (SSM_WIDTH, SSM_GROUPS * SSM_STATE, SSM_GROUPS * SSM_STATE))
    xs = xs.reshape(Bn, S, SSM_HEADS, SSM_HEAD_DIM)
    bm = bm.reshape(Bn, S, SSM_GROUPS, SSM_STATE)
    cm = cm.reshape(Bn, S, SSM_GROUPS, SSM_STATE)
    dt = jax.nn.softplus((dt_raw + dt_bias).astype(jnp.float32))
    a = -jnp.exp(a_log.astype(jnp.float32))
    y = ssd_chunked_scan(xs, dt, a, bm, cm) + d_skip.astype(jnp.float32)[:, None] * xs.astype(jnp.float32)
    y = y.reshape(Bn, S, SSM_WIDTH).astype(z.dtype) * jax.nn.silu(z)
    yg = rms_norm(y.reshape(Bn, S, SSM_GROUPS, SSM_WIDTH // SSM_GROUPS),
                  norm_w.reshape(SSM_GROUPS, SSM_WIDTH // SSM_GROUPS))
    return yg.reshape(Bn, S, SSM_WIDTH)


def hybrid_mixer(h, w_in, gmlp_ln_w, gmlp_ln_b, gmlp_ws, gmlp_bs, fox_f_bias,
                 ssm_conv_w, ssm_conv_b, ssm_dt_bias, ssm_a_log, ssm_d, ssm_norm_w,
                 w_branch_a, w_branch_b, w_branch_c, w_branch_d, w_out):
    Bn, S, _ = h.shape
    proj = h @ w_in
    uv, fox_qkv, fox_f, moba_qkv, ssm_z, ssm_xbc, ssm_dt, gate_logits = split_cols(proj, IN_SPLITS)
    u, v = jnp.split(jax.nn.gelu(uv), 2, axis=-1)
    y_a = chunked_gmlp(u, v, gmlp_ln_w, gmlp_ln_b, gmlp_ws, gmlp_bs)
    qf, kf, vf = [t.reshape(Bn, S, FOX_HEADS, HEAD_DIM) for t in jnp.split(fox_qkv, 3, axis=-1)]
    y_b = forgetting_attention(qf, kf, vf, fox_f, fox_f_bias)
    qm, km, vm = [t.reshape(Bn, S, MOBA_HEADS, HEAD_DIM) for t in jnp.split(moba_qkv, 3, axis=-1)]
    y_c = moba_attention(partial_rotary(qm), partial_rotary(km), vm)
    y_d = mamba2_mixer(ssm_z, ssm_xbc, ssm_dt, ssm_conv_w, ssm_conv_b, ssm_dt_bias,
                       ssm_a_log, ssm_d, ssm_norm_w)
    g = jax.nn.sigmoid(gate_logits).reshape(Bn, S, N_BRANCH, D_MODEL)
    merged = (g[:, :, 0] * (y_a @ w_branch_a) + g[:, :, 1] * (y_b @ w_branch_b)
              + g[:, :, 2] * (y_c @ w_branch_c) + g[:, :, 3] * (y_d @ w_branch_d))
    return merged @ w_out


def setup_inputs(seed: int = 0) -> dict:
    key = jax.random.key(seed)
    ks = jax.random.split(key, 28)
    L, D = DEPTH, D_MODEL
    f32 = jnp.float32

    def nrm(k, shape, scale):
        return jax.random.normal(k, shape, f32) * scale

    dt_init = jnp.exp(jax.random.uniform(ks[12], (L, SSM_HEADS), f32, math.log(1e-3), math.log(1e-1)))
    return {
        'x': nrm(ks[0], (BATCH, SEQ, D), 1.0),
        'c': nrm(ks[1], (BATCH, D), 1.0),
        'ada_w': nrm(ks[2], (L, D, 6 * D), 0.5 * D ** -0.5),
        'ada_b': nrm(ks[3], (L, 6 * D), 0.02),
        'norm_mix_w': 1.0 + nrm(ks[4], (L, D), 0.02),
        'w_in': nrm(ks[5], (L, D, IN_COLS), D ** -0.5),
        'gmlp_ln_w': 1.0 + nrm(ks[6], (L, GMLP_WIDTH), 0.02),
        'gmlp_ln_b': nrm(ks[7], (L, GMLP_WIDTH), 0.02),
        'gmlp_ws': nrm(ks[8], (L, GMLP_GROUPS, GMLP_CHUNK, GMLP_CHUNK), GMLP_CHUNK ** -0.5),
        'gmlp_bs': 1.0 + nrm(ks[9], (L, GMLP_GROUPS, GMLP_CHUNK), 0.1),
        'fox_f_bias': jax.random.uniform(ks[10], (L, FOX_HEADS), f32, 1.0, 6.0),
        'ssm_conv_w': nrm(ks[11], (L, SSM_CONV, SSM_CONV_DIM), SSM_CONV ** -0.5),
        'ssm_conv_b': nrm(ks[13], (L, SSM_CONV_DIM), 0.02),
        'ssm_dt_bias': dt_init + jnp.log(-jnp.expm1(-dt_init)),
        'ssm_a_log': jnp.log(jax.random.uniform(ks[14], (L, SSM_HEADS), f32, 1.0, 16.0)),
        'ssm_d': 1.0 + nrm(ks[15], (L, SSM_HEADS), 0.1),
        'ssm_norm_w': 1.0 + nrm(ks[16], (L, SSM_WIDTH), 0.02),
        'w_branch_a': nrm(ks[17], (L, GMLP_WIDTH, D), GMLP_WIDTH ** -0.5),
        'w_branch_b': nrm(ks[18], (L, FOX_WIDTH, D), FOX_WIDTH ** -0.5),
        'w_branch_c': nrm(ks[19], (L, MOBA_WIDTH, D), MOBA_WIDTH ** -0.5),
        'w_branch_d': nrm(ks[20], (L, SSM_WIDTH, D), SSM_WIDTH ** -0.5),
        'w_out': nrm(ks[21], (L, D, D), D ** -0.5),
        'norm_mlp_w': 1.0 + nrm(ks[22], (L, D), 0.02),
        'mlp_w1': nrm(ks[23], (L, D, D_FF), D ** -0.5),
        'mlp_w2': nrm(ks[24], (L, D_FF, D), D_FF ** -0.5),
        'final_norm_w': 1.0 + nrm(ks[25], (D,), 0.02),
    }


def reference(x, c, ada_w, ada_b, norm_mix_w, w_in, gmlp_ln_w, gmlp_ln_b, gmlp_ws, gmlp_bs,
              fox_f_bias, ssm_conv_w, ssm_conv_b, ssm_dt_bias, ssm_a_log, ssm_d, ssm_norm_w,
              w_branch_a, w_branch_b, w_branch_c, w_branch_d, w_out, norm_mlp_w, mlp_w1, mlp_w2,
              final_norm_w):
    c_act = jax.nn.silu(c)
    for l in range(DEPTH):
        mod = c_act @ ada_w[l] + ada_b[l]
        sh1, sc1, g1, sh2, sc2, g2 = [m[:, None, :] for m in jnp.split(mod, 6, axis=-1)]
        h = rms_norm(x, norm_mix_w[l]) * (1.0 + sc1) + sh1
        x = x + g1 * hybrid_mixer(h, w_in[l], gmlp_ln_w[l], gmlp_ln_b[l], gmlp_ws[l], gmlp_bs[l],
                                  fox_f_bias[l], ssm_conv_w[l], ssm_conv_b[l], ssm_dt_bias[l],
                                  ssm_a_log[l], ssm_d[l], ssm_norm_w[l], w_branch_a[l],
                                  w_branch_b[l], w_branch_c[l], w_branch_d[l], w_out[l])
        h = rms_norm(x, norm_mlp_w[l]) * (1.0 + sc2) + sh2
        x = x + g2 * (jnp.square(jax.nn.relu(h @ mlp_w1[l])) @ mlp_w2[l])
    return rms_norm(x, final_norm_w)
```

```python
import math
from contextlib import ExitStack

import numpy as np
import ml_dtypes
import concourse.bass as bass
import concourse.mybir as mybir
from concourse.bass_utils import run_bass_kernel_spmd

F32 = mybir.dt.float32
BF16 = mybir.dt.bfloat16
AF = mybir.ActivationFunctionType
ALU = mybir.AluOpType
AX = mybir.AxisListType

D = 1024
KC = 8
EPS = 1e-6
NEG = -30000.0
IN_COLS = 10260
C_U, C_V, C_FQ, C_FK, C_FV, C_FF = 0, 512, 1024, 1536, 2048, 2560
C_MQ, C_MK, C_MV = 2568, 3080, 3592
C_Z, C_XBC, C_DT, C_G = 4104, 4872, 6152, 6164


class Buf:
    __slots__ = ("name", "w", "r")

    def __init__(self, name=""):
        self.name = name
        self.w = None
        self.r = {}


class Sched:
    LIMIT = 30000
    NSLOT = 8

    def __init__(self, nc, stack):
        self.nc = nc
        self.stack = stack
        self.eng = dict(pe=nc.tensor, act=nc.scalar, dve=nc.vector, pool=nc.gpsimd, sp=nc.sync)
        self.cur = {}
        self.seen = {e: {} for e in self.eng}
        self.slots = {}
        self.slot_i = {}
        self.nsem = 0
        self.ninst = {e: 0 for e in self.eng}

    def _newsem(self, tag):
        self.nsem += 1
        return self.stack.enter_context(self.nc.semaphore(f"s{tag}_{self.nsem}"))

    def _wait(self, e, deps):
        eng = self.eng[e]
        best = {}
        for d in deps:
            if d is None:
                continue
            sem, val, src = d
            if src == "pe" and e == "pe":
                continue
            if self.seen[e].get(sem.num, 0) >= val:
                continue
            if best.get(sem.num, (None, 0))[1] < val:
                best[sem.num] = (sem, val)
        for sem, val in best.values():
            eng.wait_ge(sem, val)
            self.seen[e][sem.num] = val
            self.ninst[e] += 1

    def _deps(self, reads, writes):
        deps = []
        for b in reads:
            deps.append(b.w)
        for b in writes:
            deps.append(b.w)
            deps.extend(b.r.values())
        return deps

    def _record(self, ticket, reads, writes):
        sem = ticket[0]
        for b in reads:
            b.r[sem.num] = ticket
        for b in writes:
            b.w = ticket
            b.r = {}

    def op(self, e, fn, reads=(), writes=()):
        self._wait(e, self._deps(reads, writes))
        c = self.cur.get(e)
        if c is None or c[1] >= self.LIMIT:
            c = [self._newsem(e), 0]
            self.cur[e] = c
        ins = fn(self.eng[e])
        ins.then_inc(c[0], 1)
        c[1] += 1
        self.ninst[e] += 1
        t = (c[0], c[1], e)
        self._record(t, reads, writes)
        return t

    def dma(self, q, out, in_, reads=(), writes=(), **kw):
        if q not in self.slots:
            self.slots[q] = [None] * self.NSLOT
            self.slot_i[q] = 0
        i = self.slot_i[q]
        self.slot_i[q] = (i + 1) % self.NSLOT
        sl = self.slots[q][i]
        deps = self._deps(reads, writes)
        if sl is not None and sl[1] > 0:
            deps.append((sl[0], sl[1] * 16, "dma"))
        if sl is None or sl[1] * 16 >= self.LIMIT:
            self._wait(q, deps)
            deps = []
            sl = [self._newsem(f"d{q}{i}"), 0]
            self.slots[q][i] = sl
        self._wait(q, deps)
        ins = self.eng[q].dma_start(out=out, in_=in_, **kw)
        ins.then_inc(sl[0], 16)
        sl[1] += 1
        self.ninst[q] += 1
        t = (sl[0], sl[1] * 16, "dma")
        self._record(t, reads, writes)
        return t

    def _all(self):
        deps = []
        for q, sl in self.slots.items():
            for s in sl:
                if s is not None and s[1] > 0:
                    deps.append((s[0], s[1] * 16, "dma"))
        for e, c in self.cur.items():
            deps.append((c[0], c[1], e))
        return deps

    def barrier(self):
        deps = self._all()
        for e in self.eng:
            self._wait(e, deps)

    def finish(self):
        self._wait("sp", self._all())


class Rot:
    def __init__(self, items):
        self.items = items
        self.i = 0

    def next(self):
        it = self.items[self.i]
        self.i = (self.i + 1) % len(self.items)
        return it


def build(S, depth, dbg=False):
    NT = S // 128
    NG = S // 512
    nc = bass.Bass("TRN2", target_bir_lowering=False)

    def din(name, shape, dt=F32):
        return nc.dram_tensor(name, list(shape), dt, kind="ExternalInput").ap()

    def dscr(name, shape, dt):
        return nc.dram_tensor(name, list(shape), dt, kind="ExternalOutput" if dbg else "Internal").ap()

    L = depth
    x_in = din("x", [S, D])
    cT_in = din("cT", [128, KC])
    ada_w = din("ada_w", [L, D, 6 * D]); ada_b = din("ada_b", [L, 6 * D])
    norm_mix_w = din("norm_mix_w", [L, D]); norm_mlp_w = din("norm_mlp_w", [L, D])
    final_norm_w = din("final_norm_w", [D])
    w_in = din("w_in", [L, D, IN_COLS])
    w_mqk = din("w_mqk", [L, D, 1024])
    gmlp_ln_w = din("gmlp_ln_w", [L, 512]); gmlp_ln_b = din("gmlp_ln_b", [L, 512])
    gmlp_wsT = din("gmlp_wsT", [L, 8, 128, 128]); gmlp_bs = din("gmlp_bs", [L, 8, 128])
    fox_f_bias = din("fox_f_bias", [L, 8])
    conv_wT = din("conv_wT", [L, 128, 10, 4]); conv_b = din("conv_b", [L, 128, 10])
    dt_bias = din("dt_bias", [L, 12]); a_log = din("a_log", [L, 12])
    ssm_d_rep = din("ssm_d_rep", [L, 128, 6]); ssm_norm_w = din("ssm_norm_w", [L, 128, 6])
    w_br = din("w_br", [L, 2304, D])
    w_out = din("w_out", [L, D, D])
    w1 = din("mlp_w1", [L, D, 4 * D]); w2 = din("mlp_w2", [L, 4 * D, D])
    c_ident = din("c_ident", [128, 128])
    c_tri = din("c_tri", [128, 128])
    c_m01 = din("c_m01", [128, 128])
    c_rope = din("c_rope", [2, 40, S])
    c_onehot = din("c_onehot", [32, S])
    c_e0 = din("c_e0", [128, 128])
    c_rot = din("c_rot", [64, 64])
    c_sel = din("c_sel", [64, 12 * 32])
    c_selb = din("c_selb", [64, 12 * 128])
    c_ownneg = din("c_ownneg", [128, NT * 16])
    c_own0 = din("c_own0", [128, NT * 16])

    out = nc.dram_tensor("out", [S, D], F32, kind="ExternalOutput").ap()

    xs = dscr("xs", [S, D], F32)
    uT = dscr("uT", [4, 128, S], BF16)
    vg = dscr("vg", [S, 512], F32)
    qTf = dscr("qTf", [8, 64, S], BF16); kTf = dscr("kTf", [8, 64, S], BF16); vf = dscr("vf", [S, 512], BF16)
    fT = dscr("fT", [8, S], F32)
    qTm = dscr("qTm", [8, 64, S], BF16); kTm = dscr("kTm", [8, 64, S], BF16); vm = dscr("vm", [S, 512], BF16)
    szT = dscr("szT", [6, 128, S], BF16)
    xbcT = dscr("xbcT", [10, 128, S], F32)
    dtT = dscr("dtT", [12, S], F32); dtm = dscr("dtm", [S, 12], F32)
    gT = dscr("gT", [32, 128, S], BF16)
    yT = dscr("yT", [18, 128, S], BF16)
    ysT = dscr("ysT", [12, 64, S], F32)
    hidT = dscr("hidT", [32, 128, S], BF16)
    xsTd = dscr("xsTd", [6, 128, S], BF16)

    with ExitStack() as top:
        Sd = Sched(nc, top)

        uid = [0]

        def alloc(stack, name, shape, dt):
            uid[0] += 1
            t = stack.enter_context(nc.sbuf_tensor(f"{name}_{uid[0]}", list(shape), dt))
            return t, Buf(name)

        banks = []
        for i in range(8):
            t = top.enter_context(nc.psum_tensor(f"bank{i}", [128, 512], F32))
            banks.append((t, Buf(f"bank{i}")))
        accrot = Rot(banks[0:2])
        bankrot = Rot(banks[2:8])

        ident, b_ident = alloc(top, "ident", [128, 128], F32)
        identb, b_identb = alloc(top, "identb", [128, 128], BF16)
        trib, b_trib = alloc(top, "trib", [128, 128], BF16)
        ones32, b_ones32 = alloc(top, "ones32", [128, 128], F32)
        e0, b_e0 = alloc(top, "e0", [128, 128], F32)
        sel, b_sel = alloc(top, "sel", [64, 12 * 32], BF16)
        ca, b_ca = alloc(top, "ca", [128, KC], F32)
        modb, b_modb = alloc(top, "modb", [128, 6 * D], F32)
        wmod, b_wmod = alloc(top, "wmod", [128, 2 * D], F32)
        consts_r = [b_ident, b_identb, b_trib, b_ones32, b_e0, b_sel]

        Sd.dma("sp", ident[:], c_ident, writes=[b_ident])
        Sd.dma("pool", identb[:], c_ident, writes=[b_identb])
        Sd.dma("pool", trib[:], c_tri, writes=[b_trib])
        Sd.dma("sp", e0[:], c_e0, writes=[b_e0])
        Sd.dma("pool", sel[:], c_sel, writes=[b_sel])
        Sd.op("dve", lambda e: e.memset(ones32[:], 1.0), writes=[b_ones32])
        with ExitStack() as st:
            ct, b_ct = alloc(st, "ct", [128, KC], F32)
            Sd.dma("sp", ct[:], cT_in, writes=[b_ct])
            Sd.op("act", lambda e: e.activation(out=ca[:], in_=ct[:], func=AF.Silu), reads=[b_ct], writes=[b_ca])
            b_x0 = Buf()
            for i in range(NT):
                Sd.dma("sp", xs[i * 128:(i + 1) * 128, :], x_in[i * 128:(i + 1) * 128, :], writes=[b_x0])
            Sd.barrier()

        dummy, b_dummy = alloc(top, "dummy", [128, 512], BF16)
        Sd.op("pool", lambda e: e.memset(dummy[:], 1.0), writes=[b_dummy])

        def pe_warm(n=24):
            for _ in range(n):
                bk, bb = bankrot.next()
                Sd.op("pe", lambda e: e.matmul(bk[:, :], lhsT=identb[:], rhs=dummy[:], start=True, stop=True),
                      reads=[b_identb, b_dummy], writes=[bb])

        def adaln(l):
            with ExitStack() as st:
                cact_rep, b_cact = alloc(st, "cact_rep", [128, KC, 128], F32)
                Sd.op("dve", lambda e: e.tensor_copy(out=cact_rep[:], in_=ca[:].unsqueeze(2).to_broadcast([128, KC, 128])),
                      reads=[b_ca], writes=[b_cact])
                awt = [alloc(st, f"awt{i}", [128, KC, 512], F32) for i in range(2)]
                awr = Rot(awt)
                abb, b_abb = alloc(st, "abb", [128, 6 * D], F32)
                nw, b_nw = alloc(st, "nw", [128, 2 * D], F32)
                Sd.dma("sp", abb[:], ada_b[l].partition_broadcast(128), writes=[b_abb])
                Sd.dma("sp", nw[:, 0:D], norm_mix_w[l].partition_broadcast(128), writes=[b_nw])
                Sd.dma("sp", nw[:, D:2 * D], norm_mlp_w[l].partition_broadcast(128), writes=[b_nw])
                for g in range(12):
                    w, bw = awr.next()
                    Sd.dma("sp", w[:], ada_w[l][:, g * 512:(g + 1) * 512].rearrange("(kc p) n -> p kc n", p=128), writes=[bw])
                    bk, bb = bankrot.next()
                    for kc in range(KC):
                        Sd.op("pe", lambda e, kc=kc: e.matmul(bk[:, :], lhsT=cact_rep[:, kc, :], rhs=w[:, kc, :],
                                                               start=(kc == 0), stop=(kc == KC - 1)),
                              reads=[b_cact, bw], writes=[bb])
                    Sd.op("dve", lambda e: e.tensor_tensor(out=modb[:, g * 512:(g + 1) * 512], in0=bk[:, :],
                                                           in1=abb[:, g * 512:(g + 1) * 512], op=ALU.add),
                          reads=[bb, b_abb], writes=[b_modb])
                for j, sc_off in enumerate((1 * D, 4 * D)):
                    Sd.op("dve", lambda e, j=j, sc_off=sc_off: e.scalar_tensor_tensor(
                        out=wmod[:, j * D:(j + 1) * D], in0=modb[:, sc_off:sc_off + D], scalar=1.0,
                        in1=nw[:, j * D:(j + 1) * D], op0=ALU.add, op1=ALU.mult),
                        reads=[b_modb, b_nw], writes=[b_wmod])
                Sd.barrier()

        def norm_stage(st, wm_ap, sh_ap, hT, b_hT, final=False):
            xt = Rot([alloc(st, f"nx{i}", [128, D], F32) for i in range(3)])
            h1 = Rot([alloc(st, f"nh{i}", [128, D], F32) for i in range(3)])
            h2 = Rot([alloc(st, f"ng{i}", [128, D], F32) for i in range(3)])
            junk, b_junk = alloc(st, "njunk", [128, D], BF16)
            stat = Rot([alloc(st, f"nst{i}", [128, 4], F32) for i in range(3)])
            for i in range(NT):
                x_t, bx = xt.next()
                Sd.dma("sp", x_t[:], xs[i * 128:(i + 1) * 128, :], writes=[bx])
                s_t, bs = stat.next()
                Sd.op("act", lambda e: e.activation(out=junk[:], in_=x_t[:], func=AF.Square, accum_out=s_t[:, 0:1]),
                      reads=[bx], writes=[b_junk, bs])
                Sd.op("dve", lambda e: e.tensor_scalar(out=s_t[:, 1:2], in0=s_t[:, 0:1], scalar1=1.0 / D, scalar2=EPS,
                                                       op0=ALU.mult, op1=ALU.add), reads=[bs], writes=[bs])
                Sd.op("act", lambda e: e.activation(out=s_t[:, 2:3], in_=s_t[:, 1:2], func=AF.Sqrt), reads=[bs], writes=[bs])
                Sd.op("dve", lambda e: e.reciprocal(out=s_t[:, 3:4], in_=s_t[:, 2:3]), reads=[bs], writes=[bs])
                a_t, ba = h1.next()
                Sd.op("dve", lambda e: e.scalar_tensor_tensor(out=a_t[:], in0=x_t[:], scalar=s_t[:, 3:4], in1=wm_ap,
                                                              op0=ALU.mult, op1=ALU.mult),
                      reads=[bx, bs, b_wmod], writes=[ba])
                if final:
                    Sd.dma("sp", out[i * 128:(i + 1) * 128, :], a_t[:], reads=[ba])
                    continue
                g_t, bg = h2.next()
                Sd.op("pool", lambda e: e.tensor_tensor(out=g_t[:], in0=a_t[:], in1=sh_ap, op=ALU.add),
                      reads=[ba, b_modb], writes=[bg])
                for half in range(2):
                    bk, bb = bankrot.next()
                    for q in range(4):
                        kc = half * 4 + q
                        Sd.op("pe", lambda e, kc=kc, q=q: e.transpose(out=bk[:, q * 128:(q + 1) * 128],
                                                                      in_=g_t[:, kc * 128:(kc + 1) * 128], identity=ident[:]),
                              reads=[bg, b_ident], writes=[bb])
                    eng = "act" if half == 0 else "dve"
                    if eng == "act":
                        Sd.op("act", lambda e: e.copy(out=hT[:, half * 4:half * 4 + 4, i * 128:(i + 1) * 128],
                                                      in_=bk[:, :].rearrange("p (q t) -> p q t", q=4)),
                              reads=[bb], writes=[b_hT])
                    else:
                        Sd.op("dve", lambda e: e.tensor_copy(out=hT[:, half * 4:half * 4 + 4, i * 128:(i + 1) * 128],
                                                             in_=bk[:, :].rearrange("p (q t) -> p q t", q=4)),
                              reads=[bb], writes=[b_hT])

        def proj_fm(wts, hT, b_hT, w_ap, ncols, evac, nk=KC):
            for g0 in range(0, ncols, 512):
                n = min(512, ncols - g0)
                w, bw = wts.next()
                Sd.dma("pool", w[:, :, 0:n], w_ap[:, g0:g0 + n].rearrange("(kc p) n -> p kc n", p=128), writes=[bw])
                for tg in range(NG):
                    for c0 in range(0, n, 128):
                        rows = min(128, n - c0)
                        bk, bb = bankrot.next()
                        for kc in range(nk):
                            Sd.op("pe", lambda e, kc=kc: e.matmul(bk[0:rows, :], lhsT=w[:, kc, c0:c0 + rows],
                                                                   rhs=hT[:, kc, tg * 512:(tg + 1) * 512],
                                                                   start=(kc == 0), stop=(kc == nk - 1)),
                                  reads=[bw, b_hT], writes=[bb])
                        evac(bk, bb, rows, (g0 + c0) // 128, tg)

        def proj_tm(wts, hT, b_hT, w_ap, ncols, evac):
            w, bw = wts.next()
            assert ncols <= 512
            Sd.dma("pool", w[:, :, 0:ncols], w_ap.rearrange("(kc p) n -> p kc n", p=128), writes=[bw])
            for i in range(NT):
                bk, bb = bankrot.next()
                for kc in range(KC):
                    Sd.op("pe", lambda e, kc=kc: e.matmul(bk[:, 0:ncols], lhsT=hT[:, kc, i * 128:(i + 1) * 128],
                                                           rhs=w[:, kc, 0:ncols], start=(kc == 0), stop=(kc == KC - 1)),
                          reads=[bw, b_hT], writes=[bb])
                evac(bk, bb, i)

        def mixer_proj(l, st, hT, b_hT):
            stg = Rot([alloc(st, f"stg{i}", [128, 512], BF16) for i in range(3)])
            stg32 = Rot([alloc(st, f"stgf{i}", [128, 512], F32) for i in range(3)])
            wts = Rot([alloc(st, f"pw{i}", [128, KC, 512], BF16) for i in range(2)])
            wl = w_in[l]

            def ev_act(dst_fn, func, scale=1.0, f32=False):
                def ev(bk, bb, rows, cc, tg):
                    s_t, bs = (stg32 if f32 else stg).next()
                    Sd.op("act", lambda e: e.activation(out=s_t[0:rows, :], in_=bk[0:rows, :], func=func, scale=scale),
                          reads=[bb], writes=[bs])
                    Sd.dma("sp", dst_fn(cc, tg, rows), s_t[0:rows, :], reads=[bs])
                return ev

            def ev_copy(dst_fn, f32=False):
                def ev(bk, bb, rows, cc, tg):
                    s_t, bs = (stg32 if f32 else stg).next()
                    Sd.op("dve", lambda e: e.tensor_copy(out=s_t[0:rows, :], in_=bk[0:rows, :]), reads=[bb], writes=[bs])
                    Sd.dma("sp", dst_fn(cc, tg, rows), s_t[0:rows, :], reads=[bs])
                return ev

            def tgs(tg):
                return slice(tg * 512, (tg + 1) * 512)

            proj_fm(wts, hT, b_hT, wl[:, C_U:C_U + 512], 512,
                    ev_act(lambda cc, tg, rows: uT[cc, :, tgs(tg)], AF.Gelu_apprx_tanh))
            proj_fm(wts, hT, b_hT, wl[:, C_FQ:C_FQ + 512], 512,
                    ev_act(lambda cc, tg, rows: qTf[2 * cc:2 * cc + 2, :, tgs(tg)].rearrange("h d t -> (h d) t"), AF.Copy, 0.125))
            proj_fm(wts, hT, b_hT, wl[:, C_FK:C_FK + 512], 512,
                    ev_copy(lambda cc, tg, rows: kTf[2 * cc:2 * cc + 2, :, tgs(tg)].rearrange("h d t -> (h d) t")))
            proj_fm(wts, hT, b_hT, wl[:, C_FF:C_FF + 8], 8,
                    ev_copy(lambda cc, tg, rows: fT[:, tgs(tg)], f32=True))
            wm = w_mqk[l]
            proj_fm(wts, hT, b_hT, wm[:, 0:512], 512,
                    ev_act(lambda cc, tg, rows: qTm[2 * cc:2 * cc + 2, :, tgs(tg)].rearrange("h d t -> (h d) t"), AF.Copy, 0.125))
            proj_fm(wts, hT, b_hT, wm[:, 512:1024], 512,
                    ev_copy(lambda cc, tg, rows: kTm[2 * cc:2 * cc + 2, :, tgs(tg)].rearrange("h d t -> (h d) t")))
            proj_fm(wts, hT, b_hT, wl[:, C_Z:C_Z + 768], 768,
                    ev_act(lambda cc, tg, rows: szT[cc, :, tgs(tg)], AF.Silu))
            proj_fm(wts, hT, b_hT, wl[:, C_XBC:C_XBC + 1280], 1280,
                    ev_copy(lambda cc, tg, rows: xbcT[cc, :, tgs(tg)], f32=True))
            proj_fm(wts, hT, b_hT, wl[:, C_DT:C_DT + 12], 12,
                    ev_copy(lambda cc, tg, rows: dtT[:, tgs(tg)], f32=True))
            proj_fm(wts, hT, b_hT, wl[:, C_G:C_G + 4096], 4096,
                    ev_act(lambda cc, tg, rows: gT[cc, :, tgs(tg)], AF.Sigmoid))

            def ev_tm(dst, n, func=None, f32=False):
                def ev(bk, bb, i):
                    s_t, bs = (stg32 if f32 else stg).next()
                    if func is None:
                        Sd.op("dve", lambda e: e.tensor_copy(out=s_t[:, 0:n], in_=bk[:, 0:n]), reads=[bb], writes=[bs])
                    else:
                        Sd.op("act", lambda e: e.activation(out=s_t[:, 0:n], in_=bk[:, 0:n], func=func), reads=[bb], writes=[bs])
                    Sd.dma("sp", dst[i * 128:(i + 1) * 128, :], s_t[:, 0:n], reads=[bs])
                return ev
            proj_tm(wts, hT, b_hT, wl[:, C_V:C_V + 512], 512, ev_tm(vg, 512, AF.Gelu_apprx_tanh, True))
            proj_tm(wts, hT, b_hT, wl[:, C_FV:C_FV + 512], 512, ev_tm(vf, 512))
            proj_tm(wts, hT, b_hT, wl[:, C_MV:C_MV + 512], 512, ev_tm(vm, 512))
            proj_tm(wts, hT, b_hT, wl[:, C_DT:C_DT + 12], 12, ev_tm(dtm, 12, None, True))

        def gmlp(l):
            with ExitStack() as st:
                wsT, b_wsT = alloc(st, "wsT", [128, 8, 128], F32)
                wsm, b_wsm = alloc(st, "wsm", [128, 8, 128], BF16)
                m01, b_m01 = alloc(st, "m01", [128, 128], F32)
                bsb, b_bsb = alloc(st, "bsb", [128, 4, 128], F32)
                lnw, b_lnw = alloc(st, "lnw", [128, 512], F32)
                lnb, b_lnb = alloc(st, "lnb", [128, 512], F32)
                Sd.dma("sp", wsT[:], gmlp_wsT[l].rearrange("g s t -> s g t"), writes=[b_wsT])
                Sd.dma("sp", m01[:], c_m01, writes=[b_m01])
                for c in range(4):
                    for hh in range(2):
                        Sd.dma("sp", bsb[hh * 64:(hh + 1) * 64, c, :], gmlp_bs[l, 2 * c + hh].partition_broadcast(64), writes=[b_bsb])
                Sd.dma("sp", lnw[:], gmlp_ln_w[l].partition_broadcast(128), writes=[b_lnw])
                Sd.dma("sp", lnb[:], gmlp_ln_b[l].partition_broadcast(128), writes=[b_lnb])
                Sd.op("dve", lambda e: e.tensor_tensor(out=wsm[:], in0=wsT[:], in1=m01[:].unsqueeze(1).to_broadcast([128, 8, 128]),
                                                       op=ALU.mult), reads=[b_wsT, b_m01], writes=[b_wsm])
                vt = Rot([alloc(st, f"gv{i}", [128, 512], F32) for i in range(4)])
                ut = Rot([alloc(st, f"gu{i}", [128, 4, 128], BF16) for i in range(4)])
                v1 = Rot([alloc(st, f"gw{i}", [128, 512], F32) for i in range(4)])
                v2 = Rot([alloc(st, f"gx{i}", [128, 512], F32) for i in range(4)])
                vn = Rot([alloc(st, f"gn{i}", [128, 512], BF16) for i in range(4)])
                tt = Rot([alloc(st, f"gt{i}", [128, 512], F32) for i in range(4)])
                yo = Rot([alloc(st, f"gy{i}", [128, 4, 128], BF16) for i in range(4)])
                junk, b_junk = alloc(st, "gjunk", [128, 512], BF16)
                stat = Rot([alloc(st, f"gs{i}", [128, 8], F32) for i in range(4)])
                ld = {}

                def gloads(i):
                    ts = slice(i * 128, (i + 1) * 128)
                    v_t, bv = vt.next()
                    u_t, bu = ut.next()
                    Sd.dma("sp", v_t[:], vg[ts, :], writes=[bv])
                    Sd.dma("sp", u_t[:], uT[:, :, ts].rearrange("c p t -> p c t"), writes=[bu])
                    ld[i] = (v_t, bv, u_t, bu)
                gloads(0); gloads(1)
                for i in range(NT):
                    ts = slice(i * 128, (i + 1) * 128)
                    if i + 2 < NT:
                        gloads(i + 2)
                    v_t, bv, u_t, bu = ld.pop(i)
                    s, bs = stat.next()
                    Sd.op("dve", lambda e: e.tensor_reduce(out=s[:, 0:1], in_=v_t[:], axis=AX.X, op=ALU.add), reads=[bv], writes=[bs])
                    Sd.op("act", lambda e: e.activation(out=junk[:], in_=v_t[:], func=AF.Square, accum_out=s[:, 1:2]),
                          reads=[bv], writes=[b_junk, bs])
                    Sd.op("dve", lambda e: e.tensor_scalar(out=s[:, 2:3], in0=s[:, 0:1], scalar1=-1.0 / 512, scalar2=None, op0=ALU.mult),
                          reads=[bs], writes=[bs])
                    Sd.op("dve", lambda e: e.tensor_tensor(out=s[:, 3:4], in0=s[:, 2:3], in1=s[:, 2:3], op=ALU.mult), reads=[bs], writes=[bs])
                    Sd.op("dve", lambda e: e.scalar_tensor_tensor(out=s[:, 4:5], in0=s[:, 1:2], scalar=1.0 / 512, in1=s[:, 3:4],
                                                                  op0=ALU.mult, op1=ALU.subtract), reads=[bs], writes=[bs])
                    Sd.op("dve", lambda e: e.tensor_scalar(out=s[:, 5:6], in0=s[:, 4:5], scalar1=EPS, scalar2=None, op0=ALU.add),
                          reads=[bs], writes=[bs])
                    Sd.op("act", lambda e: e.activation(out=s[:, 6:7], in_=s[:, 5:6], func=AF.Sqrt), reads=[bs], writes=[bs])
                    Sd.op("dve", lambda e: e.reciprocal(out=s[:, 7:8], in_=s[:, 6:7]), reads=[bs], writes=[bs])
                    a1, ba1 = v1.next()
                    Sd.op("dve", lambda e: e.tensor_scalar(out=a1[:], in0=v_t[:], scalar1=s[:, 2:3], scalar2=s[:, 7:8],
                                                           op0=ALU.add, op1=ALU.mult), reads=[bv, bs], writes=[ba1])
                    a2, ba2 = v2.next()
                    Sd.op("pool", lambda e: e.tensor_tensor(out=a2[:], in0=a1[:], in1=lnw[:], op=ALU.mult), reads=[ba1, b_lnw], writes=[ba2])
                    n_t, bn = vn.next()
                    Sd.op("pool", lambda e: e.tensor_tensor(out=n_t[:], in0=a2[:], in1=lnb[:], op=ALU.add), reads=[ba2, b_lnb], writes=[bn])
                    bk, bb = bankrot.next()
                    for g in range(8):
                        po = (g % 2) * 64
                        Sd.op("pe", lambda e, g=g, po=po: e.matmul(bk[po:po + 64, (g // 2) * 128:(g // 2 + 1) * 128],
                                                                   lhsT=n_t[:, g * 64:(g + 1) * 64], rhs=wsm[:, g, :],
                                                                   start=True, stop=True),
                              reads=[bn, b_wsm], writes=[bb])
                    t_t, bt = tt.next()
                    Sd.op("dve", lambda e: e.tensor_tensor(out=t_t[:], in0=bk[:, :], in1=bsb[:].rearrange("p c t -> p (c t)"), op=ALU.add),
                          reads=[bb, b_bsb], writes=[bt])
                    y_t, by = yo.next()
                    Sd.op("pool", lambda e: e.tensor_tensor(out=y_t[:].rearrange("p c t -> p (c t)"), in0=t_t[:],
                                                            in1=u_t[:].rearrange("p c t -> p (c t)"), op=ALU.mult),
                          reads=[bt, bu], writes=[by])
                    Sd.dma("sp", yT[0:4, :, ts].rearrange("c p t -> p c t"), y_t[:], reads=[by])
                Sd.barrier()

        def decay_prep(st, srcT, b_src, nh, tagp):
            cum, b_cum = alloc(st, tagp + "cum", [nh, S], F32)
            one1, b_one1 = alloc(st, tagp + "one1", [nh, 1], F32)
            Sd.op("dve", lambda e: e.memset(one1[:], 1.0), writes=[b_one1])
            Sd.op("dve", lambda e: e.tensor_tensor_scan(out=cum[:], data0=one1[:, 0:1].to_broadcast([nh, S]), data1=srcT[:],
                                                        initial=0.0, op0=ALU.mult, op1=ALU.add),
                  reads=[b_one1, b_src], writes=[b_cum])
            a32, b_a32 = alloc(st, tagp + "a32", [nh, S], F32)
            AH, b_AH = alloc(st, tagp + "AH", [64, S], BF16)
            Sd.op("pool", lambda e: e.memset(AH[:], 0.0), writes=[b_AH])
            for I in range(NG):
                cs = slice(I * 512, (I + 1) * 512)
                Sd.op("dve", lambda e, cs=cs, I=I: e.tensor_scalar(out=a32[:, cs], in0=cum[:, cs], scalar1=cum[:, I * 512:I * 512 + 1],
                                                                   scalar2=None, op0=ALU.subtract),
                      reads=[b_cum], writes=[b_a32])
            Sd.op("dve", lambda e: e.tensor_copy(out=AH[0:nh, :], in_=a32[:]), reads=[b_a32], writes=[b_AH])
            Sd.op("dve", lambda e: e.tensor_tensor(out=AH[32:32 + nh, :], in0=a32[:], in1=AH[0:nh, :], op=ALU.subtract),
                  reads=[b_a32, b_AH], writes=[b_AH])
            cumT, b_cumT = alloc(st, tagp + "cumT", [128, NT, nh], F32)
            bk, bb = bankrot.next()
            for i in range(NT):
                Sd.op("pe", lambda e, i=i: e.transpose(out=bk[:, i * nh:(i + 1) * nh], in_=cum[:, i * 128:(i + 1) * 128],
                                                       identity=ident[0:nh, 0:nh]),
                      reads=[b_cum, b_ident], writes=[bb])
            Sd.op("dve", lambda e: e.tensor_copy(out=cumT[:].rearrange("p i h -> p (i h)"), in_=bk[:, 0:NT * nh]), reads=[bb], writes=[b_cumT])
            refb, b_refb = alloc(st, tagp + "refb", [128, NG, nh], F32)
            bk, bb = bankrot.next()
            for I in range(NG):
                Sd.op("pe", lambda e, I=I: e.matmul(bk[:, I * nh:(I + 1) * nh], lhsT=e0[:], rhs=cumT[:, 4 * I, :], start=True, stop=True),
                      reads=[b_e0, b_cumT], writes=[bb])
            Sd.op("dve", lambda e: e.tensor_copy(out=refb[:].rearrange("p i h -> p (i h)"), in_=bk[:, 0:NG * nh]), reads=[bb], writes=[b_refb])
            fb, b_fb = alloc(st, tagp + "fb", [128, NG, NT, nh], F32)
            for I in range(NG):
                Sd.op("dve", lambda e, I=I: e.tensor_tensor(out=fb[:, I, :, :], in0=refb[:, I, :].unsqueeze(1).to_broadcast([128, NT, nh]),
                                                            in1=cumT[:], op=ALU.subtract),
                      reads=[b_refb, b_cumT], writes=[b_fb])
            return AH, b_AH, fb, b_fb

        def attn_core(st, kind, l, AH=None, b_AH=None, fb=None, b_fb=None):
            qsrc, ksrc, vsrc, ybase = (qTf, kTf, vf, 4) if kind == "fox" else (qTm, kTm, vm, 8)
            Qa = Rot([alloc(st, f"Qa{i}", [96, S], BF16) for i in range(2)])
            Ka = Rot([alloc(st, f"Ka{i}", [96, S], BF16) for i in range(2)])
            Va = Rot([alloc(st, f"Va{i}", [128, NT, 65], BF16) for i in range(2)])
            PT = Rot([alloc(st, f"PT{i}", [128, 512], BF16) for i in range(5)])
            osb = Rot([alloc(st, f"osb{i}", [65, 512], F32) for i in range(2)])
            rsb = Rot([alloc(st, f"rsb{i}", [65, 512], F32) for i in range(2)])
            yst = Rot([alloc(st, f"yst{i}", [64, 512], BF16) for i in range(2)])
            rhl = Rot([alloc(st, f"rhl{i}", [97, 512], BF16) for i in range(2)])
            for (h_t0, bh0) in rhl.items:
                Sd.op("pool", lambda e, h_t0=h_t0: e.memset(h_t0[64:97, :], 0.0), writes=[bh0])
            onesb, b_onesb = alloc(st, "onesb", [128, 64], BF16)
            Sd.op("pool", lambda e: e.memset(onesb[:], 1.0), writes=[b_onesb])
            if kind == "moba":
                rope, b_rope = alloc(st, "rope", [40, 2, S], F32)
                Sd.dma("sp", rope[:, 0, :], c_rope[0], writes=[b_rope])
                Sd.dma("sp", rope[:, 1, :], c_rope[1], writes=[b_rope])
                oneh, b_oneh = alloc(st, "oneh", [32, S], BF16)
                Sd.dma("pool", oneh[:], c_onehot, writes=[b_oneh])
                ownneg, b_ownneg = alloc(st, "ownneg", [128, NT * 16], F32)
                own0, b_own0 = alloc(st, "own0", [128, NT * 16], F32)
                Sd.dma("sp", ownneg[:], c_ownneg, writes=[b_ownneg])
                Sd.dma("sp", own0[:], c_own0, writes=[b_own0])
                rt1 = Rot([alloc(st, f"rt1{i}", [40, 512], F32) for i in range(3)])
                rt2 = Rot([alloc(st, f"rt2{i}", [40, 512], F32) for i in range(3)])
                rotm, b_rotm = alloc(st, "rotm", [64, 64], BF16)
                Sd.dma("pool", rotm[:], c_rot, writes=[b_rotm])
                kmT, b_kmT = alloc(st, "kmT", [64, 16], F32)
                kmb, b_kmb = alloc(st, "kmb", [64, 32], BF16)
                G0, b_G0 = alloc(st, "G0", [128, NT, 16], F32)
                G1, b_G1 = alloc(st, "G1", [128, NT, 16], F32)
                EQ, b_EQ = alloc(st, "EQ", [128, NT, 16], F32)
                MB, b_MB = alloc(st, "MB", [128, NT, 32], F32)
                mx, b_mx = alloc(st, "mx", [128, NT], F32)
                Sd.op("pool", lambda e: e.memset(MB[:], 0.0), writes=[b_MB])
            for (q_t, bq), (k_t, bkk) in zip(Qa.items, Ka.items):
                Sd.op("pool", lambda e, k_t=k_t: e.memset(k_t[64:96, :], 0.0), writes=[bkk])
                if kind == "fox":
                    Sd.op("pool", lambda e, k_t=k_t: e.memset(k_t[64:66, :], 1.0), writes=[bkk])
                else:
                    Sd.op("dve", lambda e, k_t=k_t: e.tensor_copy(out=k_t[64:96, :], in_=oneh[:]), reads=[b_oneh], writes=[bkk])
            for (v_t, bv) in Va.items:
                Sd.op("pool", lambda e, v_t=v_t: e.memset(v_t[:, :, 64:65], 1.0), writes=[bv])

            pb, pbb = banks[7]
            wrot = Rot(banks[2:7])
            hc = {}

            def prolog_steps(h):
                steps = []
                q_t, bq = Qa.next()
                k_t, bkk = Ka.next()
                v_t, bv = Va.next()
                hc[h] = (q_t, bq, k_t, bkk, v_t, bv)

                def loads():
                    Sd.dma("sp", q_t[0:64, :], qsrc[h], writes=[bq])
                    Sd.dma("sp", k_t[0:64, :], ksrc[h], writes=[bkk])
                    Sd.dma("sp", v_t[:, :, 0:64], vsrc[:, h * 64:(h + 1) * 64].rearrange("(i p) d -> p i d", p=128), writes=[bv])
                steps.append(loads)
                if kind == "fox":
                    for I in range(NG):
                        def f(I=I):
                            Sd.op("pe", lambda e: e.matmul(pb[0:32, :], lhsT=sel[0:64, h * 32:(h + 1) * 32], rhs=AH[0:64, I * 512:(I + 1) * 512],
                                                           start=True, stop=True), reads=[b_sel, b_AH], writes=[pbb])
                            Sd.op("dve", lambda e: e.tensor_copy(out=q_t[64:96, I * 512:(I + 1) * 512], in_=pb[0:32, :]), reads=[pbb], writes=[bq])
                        steps.append(f)
                    return steps
                for (T, bT) in ((q_t, bq), (k_t, bkk)):
                    for I in range(NG):
                        def f(T=T, bT=bT, I=I):
                            cs = slice(I * 512, (I + 1) * 512)
                            Sd.op("pe", lambda e: e.matmul(pb[0:64, :], lhsT=rotm[:, :], rhs=T[0:64, cs], start=True, stop=True),
                                  reads=[b_rotm, bT], writes=[pbb])
                            t1, b1 = rt1.next()
                            t2, b2 = rt2.next()
                            Sd.op("dve", lambda e: e.tensor_tensor(out=t1[:, :], in0=pb[0:40, :], in1=rope[:, 1, cs], op=ALU.mult), reads=[pbb, b_rope], writes=[b1])
                            Sd.op("pool", lambda e: e.tensor_tensor(out=t2[:, :], in0=T[0:40, cs], in1=rope[:, 0, cs], op=ALU.mult), reads=[bT, b_rope], writes=[b2])
                            Sd.op("dve", lambda e: e.tensor_tensor(out=T[0:40, cs], in0=t1[:, :], in1=t2[:, :], op=ALU.add), reads=[b1, b2], writes=[bT])
                        steps.append(f)
                nblk = S // 256

                def kmean():
                    Sd.op("dve", lambda e: e.tensor_reduce(out=kmT[:, 0:nblk], in_=k_t[0:64, :].rearrange("p (n t) -> p n t", t=256), axis=AX.X, op=ALU.add),
                          reads=[bkk], writes=[b_kmT])
                    Sd.op("dve", lambda e: e.tensor_copy(out=kmb[:, 0:nblk], in_=kmT[:, 0:nblk]), reads=[b_kmT], writes=[b_kmb])
                    Sd.op("dve", lambda e: e.tensor_tensor(out=kmb[:, 16:16 + nblk], in0=kmT[:, 0:nblk], in1=kmb[:, 0:nblk], op=ALU.subtract),
                          reads=[b_kmT, b_kmb], writes=[b_kmb])
                steps.append(kmean)
                for i0 in range(0, NT, 8):
                    def f(i0=i0):
                        for i in range(i0, min(i0 + 8, NT)):
                            Sd.op("pe", lambda e, i=i: e.matmul(pb[:, i * 16:i * 16 + nblk], lhsT=q_t[0:64, i * 128:(i + 1) * 128], rhs=kmb[:, 0:nblk],
                                                                start=True, stop=False), reads=[bq, b_kmb], writes=[pbb])
                            Sd.op("pe", lambda e, i=i: e.matmul(pb[:, i * 16:i * 16 + nblk], lhsT=q_t[0:64, i * 128:(i + 1) * 128], rhs=kmb[:, 16:16 + nblk],
                                                                start=False, stop=True), reads=[bq, b_kmb], writes=[pbb])
                    steps.append(f)

                def tk0():
                    if nblk < 16:
                        Sd.op("dve", lambda e: e.memset(G0[:], 0.0), writes=[b_G0])
                    Sd.op("dve", lambda e: e.tensor_tensor(out=G0[:, :, 0:nblk], in0=pb[:, 0:NT * 16].rearrange("p (i n) -> p i n", n=16)[:, :, 0:nblk],
                                                           in1=ownneg[:].rearrange("p (i n) -> p i n", n=16)[:, :, 0:nblk], op=ALU.add),
                          reads=[pbb, b_ownneg], writes=[b_G0])
                    if nblk < 16:
                        Sd.op("dve", lambda e: e.memset(G0[:, :, nblk:16], -3e9), writes=[b_G0])
                steps.append(tk0)
                for r in range(2):
                    def f(r=r):
                        src, bsrc = (G0, b_G0) if r == 0 else (G1, b_G1)
                        Sd.op("dve", lambda e: e.tensor_reduce(out=mx[:], in_=src[:], axis=AX.X, op=ALU.max), reads=[bsrc], writes=[b_mx])
                        Sd.op("dve", lambda e: e.tensor_tensor(out=EQ[:], in0=src[:], in1=mx[:].unsqueeze(2).to_broadcast([128, NT, 16]),
                                                               op=ALU.is_equal), reads=[bsrc, b_mx], writes=[b_EQ])
                        Sd.op("dve", lambda e: e.scalar_tensor_tensor(out=G1[:].rearrange("p i n -> p (i n)"), in0=EQ[:].rearrange("p i n -> p (i n)"),
                                                                      scalar=-2e9, in1=src[:].rearrange("p i n -> p (i n)"),
                                                                      op0=ALU.mult, op1=ALU.add),
                              reads=[b_EQ, bsrc], writes=[b_G1])
                    steps.append(f)

                def tk3():
                    Sd.op("dve", lambda e: e.tensor_reduce(out=mx[:], in_=G1[:], axis=AX.X, op=ALU.max), reads=[b_G1], writes=[b_mx])
                    Sd.op("dve", lambda e: e.tensor_tensor(out=EQ[:], in0=G0[:], in1=mx[:].unsqueeze(2).to_broadcast([128, NT, 16]), op=ALU.is_ge),
                          reads=[b_G0, b_mx], writes=[b_EQ])
                    Sd.op("dve", lambda e: e.scalar_tensor_tensor(out=MB[:, :, 0:16], in0=EQ[:], scalar=-1.0,
                                                                  in1=own0[:].rearrange("p (i n) -> p i n", n=16), op0=ALU.add, op1=ALU.mult),
                          reads=[b_EQ, b_own0], writes=[b_MB])
                steps.append(tk3)
                for I in range(NG):
                    def f(I=I):
                        for q in range(4):
                            i = 4 * I + q
                            Sd.op("pe", lambda e, i=i, q=q: e.transpose(out=pb[0:32, q * 128:(q + 1) * 128], in_=MB[:, i, :], identity=ident[:]),
                                  reads=[b_MB, b_ident], writes=[pbb])
                        Sd.op("act", lambda e: e.copy(out=q_t[64:96, I * 512:(I + 1) * 512], in_=pb[0:32, :]), reads=[pbb], writes=[bq])
                    steps.append(f)
                return steps

            items = [(h, I, j) for h in range(8) for I in range(NG) for j in range(4 * I + 4)]
            nper = len(items) // 8
            ic = {}
            gc = {}
            deferred = []

            def stageA(idx):
                h, I, j = items[idx]
                q_t, bq, k_t, bkk, v_t, bv = hc[h]
                m = j - 4 * I
                c0 = max(m, 0) * 128
                sb_, sbb = wrot.next()
                if m < 0:
                    Sd.op("pe", lambda e: e.matmul(sb_[:, 0:512], lhsT=k_t[:, j * 128:(j + 1) * 128],
                                                   rhs=q_t[:, I * 512:(I + 1) * 512], start=True, stop=True),
                          reads=[bkk, bq], writes=[sbb])
                else:
                    Sd.op("pe", lambda e: e.matmul(sb_[:, c0:c0 + 128], lhsT=k_t[:, j * 128:(j + 1) * 128],
                                                   rhs=q_t[:, I * 512 + c0:I * 512 + c0 + 128], start=True, stop=False),
                          reads=[bkk, bq], writes=[sbb])
                    Sd.op("pe", lambda e: e.matmul(sb_[:, c0:c0 + 128], lhsT=identb[:], rhs=trib[:], start=False, stop=True),
                          reads=[b_identb, b_trib], writes=[sbb])
                    if c0 + 128 < 512:
                        Sd.op("pe", lambda e: e.matmul(sb_[:, c0 + 128:512], lhsT=k_t[:, j * 128:(j + 1) * 128],
                                                       rhs=q_t[:, I * 512 + c0 + 128:(I + 1) * 512], start=True, stop=True),
                              reads=[bkk, bq], writes=[sbb])
                p_t, bp = PT.next()
                if kind == "fox":
                    Sd.op("act", lambda e: e.activation(out=p_t[:, c0:512], in_=sb_[:, c0:512], func=AF.Exp, bias=fb[:, I, j, h:h + 1]),
                          reads=[sbb, b_fb], writes=[bp])
                else:
                    Sd.op("act", lambda e: e.activation(out=p_t[:, c0:512], in_=sb_[:, c0:512], func=AF.Exp), reads=[sbb], writes=[bp])
                ic[idx] = (p_t, bp, c0)

            def stageB(idx):
                h, I, j = items[idx]
                q_t, bq, k_t, bkk, v_t, bv = hc[h]
                nj = 4 * I + 4
                if j == 0:
                    gc[(h, I)] = accrot.next()
                ob, obb = gc[(h, I)]
                p_t, bp, c0 = ic.pop(idx)
                Sd.op("pe", lambda e: e.matmul(ob[0:65, c0:512], lhsT=v_t[:, j, :], rhs=p_t[:, c0:512],
                                               start=(j == 0), stop=(j == nj - 1)),
                      reads=[bv, bp], writes=[obb])
                if j == nj - 1:
                    o_t, bo = osb.next()
                    r_t, br = rsb.next()
                    Sd.op("act", lambda e: e.copy(out=o_t[:, :], in_=ob[0:65, :]), reads=[obb], writes=[bo])
                    Sd.op("dve", lambda e: e.reciprocal(out=r_t[64:65, :], in_=o_t[64:65, :]), reads=[bo], writes=[br])

                    h_t, bh_ = rhl.next()
                    Sd.op("dve", lambda e: e.tensor_copy(out=h_t[64:65, :], in_=r_t[64:65, :]), reads=[br], writes=[bh_])
                    Sd.op("dve", lambda e: e.tensor_tensor(out=h_t[96:97, :], in0=r_t[64:65, :], in1=h_t[64:65, :], op=ALU.subtract),
                          reads=[br, bh_], writes=[bh_])

                    def E2():
                        b2, b2b = wrot.next()
                        Sd.op("pe", lambda e: e.matmul(b2[0:64, :], lhsT=onesb[64:97, 0:64], rhs=h_t[64:97, :], start=True, stop=True),
                              reads=[b_onesb, bh_], writes=[b2b])
                        y_t, by = yst.next()
                        Sd.op("dve", lambda e: e.tensor_tensor(out=y_t[:, :], in0=o_t[0:64, :], in1=b2[0:64, :], op=ALU.mult), reads=[bo, b2b], writes=[by])
                        Sd.dma("sp", yT[ybase + h // 2, (h % 2) * 64:(h % 2) * 64 + 64, I * 512:(I + 1) * 512], y_t[:, :], reads=[by])
                    deferred.append([4, E2])

            LA = 3
            for f in prolog_steps(0):
                f()
            n = len(items)
            for idx in range(min(LA, n)):
                stageA(idx)
            pend = []
            for idx in range(n):
                h = items[idx][0]
                loc = idx - h * nper
                if h + 1 < 8:
                    if loc == 0:
                        pend = prolog_steps(h + 1)
                        pend.pop(0)()
                    elif pend and (nper < 60 or (loc >= 8 and loc % 2 == 0) or loc >= nper - LA - 3):
                        pend.pop(0)()
                        while pend and loc >= nper - LA - 3:
                            pend.pop(0)()
                if idx + LA < n:
                    stageA(idx + LA)
                stageB(idx)
                for d in list(deferred):
                    d[0] -= 1
                    if d[0] <= 0:
                        d[1]()
                        deferred.remove(d)
            for d in deferred:
                d[1]()

        def fox(l):
            with ExitStack() as st:
                f_t, bf_ = alloc(st, "fraw", [8, S], F32)
                nfb, b_nfb = alloc(st, "nfb", [8, 1], F32)
                Sd.dma("sp", f_t[:], fT, writes=[bf_])
                Sd.dma("sp", nfb[:], fox_f_bias[l].rearrange("(h o) -> h o", o=1), writes=[b_nfb])
                Sd.op("dve", lambda e: e.tensor_scalar(out=nfb[:], in0=nfb[:], scalar1=-1.0, scalar2=None, op0=ALU.mult), reads=[b_nfb], writes=[b_nfb])
                Sd.op("act", lambda e: e.activation(out=f_t[:], in_=f_t[:], func=AF.Exp, bias=nfb[:, 0:1], scale=-1.0), reads=[bf_, b_nfb], writes=[bf_])
                Sd.op("act", lambda e: e.activation(out=f_t[:], in_=f_t[:], func=AF.Ln, bias=1.0), reads=[bf_], writes=[bf_])
                Sd.op("dve", lambda e: e.tensor_scalar(out=f_t[:], in0=f_t[:], scalar1=-1.0, scalar2=None, op0=ALU.mult), reads=[bf_], writes=[bf_])
                AH, b_AH, fb, b_fb = decay_prep(st, f_t, bf_, 8, "f")
                attn_core(st, "fox", l, AH, b_AH, fb, b_fb)
                Sd.barrier()

        def moba(l):
            with ExitStack() as st:
                attn_core(st, "moba", l)
                Sd.barrier()

        def ssd(l):
            with ExitStack() as st:
                AHp, b_AHp = alloc(st, "sAHp", [64, S], BF16)
                fbp, b_fbp = alloc(st, "sfbp", [128, NG, NT, 12], F32)
                dtk, b_dtk = alloc(st, "dtk", [128, NT, 12], F32)
                with ExitStack() as st2:
                    dT, b_dT = alloc(st2, "dT", [12, S], F32)
                    dtb, b_dtb = alloc(st2, "dtb", [12, 1], F32)
                    alg, b_alg = alloc(st2, "alg", [12, 1], F32)
                    Sd.dma("sp", dT[:], dtT, writes=[b_dT])
                    Sd.dma("sp", dtb[:], dt_bias[l].rearrange("(h o) -> h o", o=1), writes=[b_dtb])
                    Sd.dma("sp", alg[:], a_log[l].rearrange("(h o) -> h o", o=1), writes=[b_alg])
                    Sd.op("act", lambda e: e.activation(out=alg[:], in_=alg[:], func=AF.Exp), reads=[b_alg], writes=[b_alg])
                    Sd.op("act", lambda e: e.activation(out=dT[:], in_=dT[:], func=AF.Exp, bias=dtb[:, 0:1]), reads=[b_dT, b_dtb], writes=[b_dT])
                    Sd.op("act", lambda e: e.activation(out=dT[:], in_=dT[:], func=AF.Ln, bias=1.0), reads=[b_dT], writes=[b_dT])
                    Sd.op("dve", lambda e: e.tensor_scalar(out=dT[:], in0=dT[:], scalar1=alg[:, 0:1], scalar2=-1.0, op0=ALU.mult, op1=ALU.mult),
                          reads=[b_dT, b_alg], writes=[b_dT])
                    AH0, b_AH0, fb0, b_fb0 = decay_prep(st2, dT, b_dT, 12, "s")
                    Sd.op("pool", lambda e: e.tensor_copy(out=AHp[:], in_=AH0[:]), reads=[b_AH0], writes=[b_AHp])
                    Sd.op("pool", lambda e: e.tensor_copy(out=fbp[:].rearrange("p a b c -> p (a b c)"), in_=fb0[:].rearrange("p a b c -> p (a b c)")),
                          reads=[b_fb0], writes=[b_fbp])
                    dtbb, b_dtbb = alloc(st2, "dtbb", [128, 12], F32)
                    Sd.dma("sp", dtk[:], dtm.rearrange("(i p) h -> p i h", p=128), writes=[b_dtk])
                    Sd.dma("sp", dtbb[:], dt_bias[l].partition_broadcast(128), writes=[b_dtbb])
                    Sd.op("dve", lambda e: e.tensor_tensor(out=dtk[:], in0=dtk[:], in1=dtbb[:].unsqueeze(1).to_broadcast([128, NT, 12]), op=ALU.add),
                          reads=[b_dtk, b_dtbb], writes=[b_dtk])
                    Sd.op("act", lambda e: e.activation(out=dtk[:], in_=dtk[:], func=AF.Exp), reads=[b_dtk], writes=[b_dtk])
                    Sd.op("act", lambda e: e.activation(out=dtk[:], in_=dtk[:], func=AF.Ln, bias=1.0), reads=[b_dtk], writes=[b_dtk])
                    Sd.barrier()
                AH, b_AH, fb, b_fb = AHp, b_AHp, fbp, b_fbp
                BC, b_BC = alloc(st, "BC", [128, 4, S], BF16)
                xdtm, b_xdtm = alloc(st, "xdtm", [128, NT, 768], BF16)
                with ExitStack() as st2:
                    cw, b_cw = alloc(st2, "cw", [128, 10, 4], F32)
                    cb, b_cb = alloc(st2, "cb", [128, 10], F32)
                    Sd.dma("sp", cw[:], conv_wT[l], writes=[b_cw])
                    Sd.dma("sp", cb[:], conv_b[l], writes=[b_cb])
                    xin = Rot([alloc(st2, f"cxi{i}", [128, S], F32) for i in range(1)])
                    acc = Rot([alloc(st2, f"cac{i}", [128, S], F32) for i in range(1)])
                    xsf, b_xsf = alloc(st2, "xsf", [128, S], F32)
                    xsb = Rot([alloc(st2, f"xsb{i}", [128, S], BF16) for i in range(1)])
                    for c in range(10):
                        xi, bxi = xin.next()
                        ac, bac = acc.next()
                        Sd.dma("sp", xi[:], xbcT[c], writes=[bxi])
                        Sd.op("dve", lambda e, c=c: e.tensor_scalar(out=ac[:], in0=xi[:], scalar1=cw[:, c, 3:4], scalar2=None, op0=ALU.mult),
                              reads=[bxi, b_cw], writes=[bac])
                        for sh in (1, 2, 3):
                            Sd.op("dve", lambda e, c=c, sh=sh: e.scalar_tensor_tensor(out=ac[:, sh:S], in0=xi[:, 0:S - sh], scalar=cw[:, c, 3 - sh:4 - sh],
                                                                                     in1=ac[:, sh:S], op0=ALU.mult, op1=ALU.add),
                                  reads=[bxi, b_cw, bac], writes=[bac])
                        if c < 6:
                            Sd.op("act", lambda e, c=c: e.activation(out=xsf[:], in_=ac[:], func=AF.Silu, bias=cb[:, c:c + 1]), reads=[bac, b_cb], writes=[b_xsf])
                            xb_, bxb_ = xsb.next()
                            Sd.op("pool", lambda e: e.tensor_copy(out=xb_[:], in_=xsf[:]), reads=[b_xsf], writes=[bxb_])
                            Sd.dma("sp", xsTd[c], xb_[:], reads=[bxb_])
                            for I in range(NG):
                                bk, bb = bankrot.next()
                                for q in range(4):
                                    i = 4 * I + q
                                    Sd.op("pe", lambda e, i=i, q=q: e.transpose(out=bk[:, q * 128:(q + 1) * 128], in_=xsf[:, i * 128:(i + 1) * 128], identity=ident[:]),
                                          reads=[b_xsf, b_ident], writes=[bb])
                                Sd.op("dve", lambda e, I=I, c=c: e.tensor_tensor(
                                    out=xdtm[:, 4 * I:4 * I + 4, c * 128:(c + 1) * 128].rearrange("p i (h d) -> p i h d", h=2),
                                    in0=bk[:, :].rearrange("p (i h d) -> p i h d", i=4, h=2),
                                    in1=dtk[:, 4 * I:4 * I + 4, 2 * c:2 * c + 2].unsqueeze(3).to_broadcast([128, 4, 2, 64]), op=ALU.mult),
                                    reads=[bb, b_dtk], writes=[b_xdtm])
                        else:
                            Sd.op("act", lambda e, c=c: e.activation(out=BC[:, c - 6, :], in_=ac[:], func=AF.Silu, bias=cb[:, c:c + 1]), reads=[bac, b_cb], writes=[b_BC])
                    Sd.barrier()
                selb, b_selb = alloc(st, "selb", [64, 12 * 128], BF16)
                trib4, b_trib4 = alloc(st, "trib4", [128, 512], F32)
                Sd.dma("pool", selb[:], c_selb, writes=[b_selb])
                for q in range(4):
                    Sd.dma("sp", trib4[:, q * 128:(q + 1) * 128], c_tri, writes=[b_trib4])
                m01b, b_m01b = alloc(st, "m01b", [128, 128], BF16)
                Sd.dma("pool", m01b[:], c_m01, writes=[b_m01b])
                ARs = [alloc(st, f"AR_{hh}", [128, 512], F32) for hh in range(6)]
                ARDs = [alloc(st, f"ARD_{hh}", [128, 512], F32) for hh in range(6)]
                Gs = Rot([alloc(st, f"Gs{i}", [128, 512], BF16) for i in range(3)])
                DT_ = Rot([alloc(st, f"DTt{i}", [128, 512], F32) for i in range(3)])
                WT = Rot([alloc(st, f"WT{i}", [128, 512], BF16) for i in range(4)])
                yo = Rot([alloc(st, f"syo{i}", [128, 512], F32) for i in range(2)])
                yoff = [alloc(st, f"yoff{i}", [128, 512], F32) for i in range(3)]
                xsc = Rot([alloc(st, f"xsc{i}", [128, 6, 64], BF16) for i in range(3)])
                Vb = Rot([alloc(st, f"Vb{i}", [128, 512], F32) for i in range(2)])
                U, b_U = alloc(st, "Utab", [128, NG, NT, 12], F32)
                Sd.op("act", lambda e: e.activation(out=U[:].rearrange("p a b c -> p (a b c)"), in_=fb[:].rearrange("p a b c -> p (a b c)"), func=AF.Exp),
                      reads=[b_fb], writes=[b_U])
                accb = banks[0:3]
                gbank = Rot(banks[3:6])
                misc = Rot(banks[6:8])
                groups = [(g, I) for g in range(2) for I in range(NG)]

                def ar_setup(gi):
                    g, I = groups[gi]
                    for hh in range(6):
                        h = 6 * g + hh
                        bk, bb = misc.next()
                        Sd.op("pe", lambda e: e.matmul(bk[:, :], lhsT=selb[0:64, h * 128:(h + 1) * 128], rhs=AH[0:64, I * 512:(I + 1) * 512],
                                                       start=True, stop=True), reads=[b_selb, b_AH], writes=[bb])
                        ar, bar = ARs[hh]
                        ard, bard = ARDs[hh]
                        Sd.op("dve", lambda e: e.tensor_copy(out=ar[:], in_=bk[:, :]), reads=[bb], writes=[bar])
                        Sd.op("pool", lambda e: e.tensor_tensor(out=ard[:], in0=ar[:], in1=trib4[:], op=ALU.add), reads=[bar, b_trib4], writes=[bard])

                for gi, (g, I) in enumerate(groups):
                    ar_setup(gi)
                    gst = {}

                    def emitG(j):
                        m = j - 4 * I
                        c0 = max(m, 0) * 128
                        gb, gbb = gbank.next()
                        Sd.op("pe", lambda e: e.matmul(gb[:, c0:512], lhsT=BC[:, g, j * 128:(j + 1) * 128],
                                                       rhs=BC[:, 2 + g, I * 512 + c0:(I + 1) * 512], start=True, stop=True),
                              reads=[b_BC], writes=[gbb])
                        gst[j] = (gb, gbb, c0, m)

                    def evacG(j):
                        gb, gbb, c0, m = gst[j]
                        gs, bgs = Gs.next()
                        if m < 0:
                            Sd.op("act", lambda e: e.copy(out=gs[:, :], in_=gb[:, :]), reads=[gbb], writes=[bgs])
                        else:
                            Sd.op("dve", lambda e: e.tensor_tensor(out=gs[:, c0:c0 + 128], in0=gb[:, c0:c0 + 128], in1=m01b[:], op=ALU.mult),
                                  reads=[gbb, b_m01b], writes=[bgs])
                            if c0 + 128 < 512:
                                Sd.op("act", lambda e: e.copy(out=gs[:, c0 + 128:512], in_=gb[:, c0 + 128:512]), reads=[gbb], writes=[bgs])
                        gst[j] = (gs, bgs, c0, m)

                    noff = 4 * I
                    if noff > 0:
                        emitG(0); evacG(0)
                        if noff > 1:
                            emitG(1); evacG(1)
                        for j in range(noff):
                            if j + 2 < noff:
                                emitG(j + 2)
                            gs, bgs, c0, m = gst[j]
                            x_t, bxs = xsc.next()
                            Sd.op("dve", lambda e: e.tensor_tensor(out=x_t[:], in0=xdtm[:, j, g * 384:(g + 1) * 384].rearrange("p (h d) -> p h d", h=6),
                                                                   in1=U[:, I, j, 6 * g:6 * g + 6].unsqueeze(2).to_broadcast([128, 6, 64]), op=ALU.mult),
                                  reads=[b_xdtm, b_U], writes=[bxs])
                            for b3 in range(3):
                                ob, obb = accb[b3]
                                Sd.op("pe", lambda e: e.matmul(ob[:, :], lhsT=x_t[:, 2 * b3:2 * b3 + 2, :].rearrange("p h d -> p (h d)"), rhs=gs[:, :],
                                                               start=(j == 0), stop=(j == noff - 1)),
                                      reads=[bxs, bgs], writes=[obb])
                            if j + 2 < noff:
                                evacG(j + 2)
                        for b3 in range(3):
                            ob, obb = accb[b3]
                            yf, byf = yoff[b3]
                            Sd.op("act", lambda e: e.copy(out=yf[:, :], in_=ob[:, :]), reads=[obb], writes=[byf])
                    j0 = 4 * I
                    nj = 4 * I + 4
                    emitG(j0); evacG(j0)
                    emitG(j0 + 1); evacG(j0 + 1)
                    for j in range(j0, nj):
                        if j + 2 < nj:
                            emitG(j + 2)
                        gs, bgs, c0, m = gst[j]
                        for hh in range(6):
                            h = 6 * g + hh
                            ar, bar = ARs[hh]
                            ard, bard = ARDs[hh]
                            d_t, bd = DT_.next()
                            Sd.op("act", lambda e: e.activation(out=d_t[:, c0:c0 + 128], in_=ard[:, c0:c0 + 128], func=AF.Exp, bias=fb[:, I, j, h:h + 1]),
                                  reads=[bard, b_fb], writes=[bd])
                            if c0 + 128 < 512:
                                Sd.op("act", lambda e: e.activation(out=d_t[:, c0 + 128:512], in_=ar[:, c0 + 128:512], func=AF.Exp, bias=fb[:, I, j, h:h + 1]),
                                      reads=[bar, b_fb], writes=[bd])
                            w_t, bw = WT.next()
                            Sd.op("dve", lambda e: e.tensor_tensor(out=w_t[:, c0:512], in0=gs[:, c0:512], in1=d_t[:, c0:512], op=ALU.mult),
                                  reads=[bgs, bd], writes=[bw])
                            ob, obb = accb[hh // 2]
                            po = (hh % 2) * 64
                            Sd.op("pe", lambda e: e.matmul(ob[po:po + 64, c0:512], lhsT=xdtm[:, j, h * 64:(h + 1) * 64], rhs=w_t[:, c0:512],
                                                           start=(j == j0), stop=(j == nj - 1)),
                                  reads=[b_xdtm, bw], writes=[obb])
                        if j + 2 < nj:
                            evacG(j + 2)
                    for b3 in range(3):
                        ob, obb = accb[b3]
                        y_t, by = yo.next()
                        if noff == 0:
                            Sd.op("act", lambda e: e.copy(out=y_t[:, :], in_=ob[:, :]), reads=[obb], writes=[by])
                        else:
                            v_t, bvb = Vb.next()
                            for hf in range(2):
                                ar, bar = ARs[2 * b3 + hf]
                                Sd.op("act", lambda e: e.activation(out=v_t[hf * 64:(hf + 1) * 64, :], in_=ar[hf * 64:(hf + 1) * 64, :], func=AF.Exp),
                                      reads=[bar], writes=[bvb])
                            yf, byf = yoff[b3]
                            Sd.op("pool", lambda e: e.tensor_tensor(out=v_t[:, :], in0=v_t[:, :], in1=yf[:, :], op=ALU.mult), reads=[bvb, byf], writes=[bvb])
                            Sd.op("dve", lambda e: e.tensor_tensor(out=y_t[:, :], in0=ob[:, :], in1=v_t[:, :], op=ALU.add), reads=[obb, bvb], writes=[by])
                        c = 3 * g + b3
                        Sd.dma("sp", ysT[2 * c:2 * c + 2, :, I * 512:(I + 1) * 512].rearrange("h d t -> (h d) t"), y_t[:, :], reads=[by])
                Sd.barrier()

        def ssd_pass2(l):
            with ExitStack() as st:
                xsl = Rot([alloc(st, f"xsl{i}", [128, 512], BF16) for i in range(6)])
                dcol, b_dcol = alloc(st, "dcol", [128, 6], F32)
                nwc, b_nwc = alloc(st, "nwc", [128, 6], F32)
                Sd.dma("sp", dcol[:], ssm_d_rep[l], writes=[b_dcol])
                Sd.dma("sp", nwc[:], ssm_norm_w[l], writes=[b_nwc])
                ysl = Rot([alloc(st, f"ysl{i}", [128, 512], F32) for i in range(6)])
                szl = Rot([alloc(st, f"szl{i}", [128, 512], BF16) for i in range(6)])
                y1 = Rot([alloc(st, f"y1{i}", [128, 512], F32) for i in range(2)])
                y2 = [alloc(st, f"y2{i}", [128, 512], F32) for i in range(6)]
                sq = Rot([alloc(st, f"sq{i}", [128, 512], F32) for i in range(2)])
                rs = Rot([alloc(st, f"rs{i}", [128, 512], F32) for i in range(2)])
                y3 = Rot([alloc(st, f"y3{i}", [128, 512], BF16) for i in range(2)])
                ld = {}

                def ploads(u):
                    I, g = divmod(u, 2)
                    cs = slice(I * 512, (I + 1) * 512)
                    for cc in range(3):
                        c = 3 * g + cc
                        ys_t, bys = ysl.next()
                        xs_t, bxs = xsl.next()
                        sz_t, bsz = szl.next()
                        Sd.dma("sp", xs_t[:], xsTd[c, :, cs], writes=[bxs])
                        Sd.dma("sp", ys_t[:], ysT[2 * c:2 * c + 2, :, cs].rearrange("h d t -> (h d) t"), writes=[bys])
                        Sd.dma("sp", sz_t[:], szT[c, :, cs], writes=[bsz])
                        ld[(u, cc)] = (ys_t, bys, xs_t, bxs, sz_t, bsz)
                ploads(0)
                for I in range(NG):
                    cs = slice(I * 512, (I + 1) * 512)
                    for g in range(2):
                        u = 2 * I + g
                        if u + 1 < 2 * NG:
                            ploads(u + 1)
                        sb_, sbb = accrot.next()
                        for cc in range(3):
                            c = 3 * g + cc
                            ys_t, bys, xs_t, bxs, sz_t, bsz = ld.pop((u, cc))
                            a_t, ba = y1.next()
                            Sd.op("dve", lambda e, c=c: e.scalar_tensor_tensor(out=a_t[:], in0=xs_t[:], scalar=dcol[:, c:c + 1], in1=ys_t[:],
                                                                               op0=ALU.mult, op1=ALU.add), reads=[bxs, b_dcol, bys], writes=[ba])
                            y2t, by2 = y2[c]
                            Sd.op("pool", lambda e: e.tensor_tensor(out=y2t[:], in0=a_t[:], in1=sz_t[:], op=ALU.mult), reads=[ba, bsz], writes=[by2])
                            q_, bq_ = sq.next()
                            Sd.op("act", lambda e: e.activation(out=q_[:], in_=y2t[:], func=AF.Square), reads=[by2], writes=[bq_])
                            Sd.op("pe", lambda e, cc=cc: e.matmul(sb_[:, :], lhsT=ones32[:], rhs=q_[:], start=(cc == 0), stop=(cc == 2)),
                                  reads=[b_ones32, bq_], writes=[sbb])
                        r_t, br = rs.next()
                        Sd.op("dve", lambda e: e.tensor_scalar(out=r_t[:], in0=sb_[:, :], scalar1=1.0 / 384, scalar2=EPS, op0=ALU.mult, op1=ALU.add),
                              reads=[sbb], writes=[br])
                        Sd.op("act", lambda e: e.activation(out=r_t[:], in_=r_t[:], func=AF.Sqrt), reads=[br], writes=[br])
                        Sd.op("dve", lambda e: e.reciprocal(out=r_t[:], in_=r_t[:]), reads=[br], writes=[br])
                        for cc in range(3):
                            c = 3 * g + cc
                            y2t, by2 = y2[c]
                            o_t, bo = y3.next()
                            Sd.op("dve", lambda e, c=c: e.scalar_tensor_tensor(out=o_t[:], in0=y2t[:], scalar=nwc[:, c:c + 1], in1=r_t[:],
                                                                               op0=ALU.mult, op1=ALU.mult), reads=[by2, b_nwc, br], writes=[bo])
                            Sd.dma("sp", yT[12 + c, :, cs], o_t[:], reads=[bo])
                Sd.barrier()

        def resid_evac(st_rot, bk, bb, gofs, xt, bx, cg):
            t_t, bt = st_rot.next()
            Sd.op("dve", lambda e: e.tensor_tensor(out=t_t[:], in0=bk[:, :], in1=modb[:, gofs + cg * 512:gofs + (cg + 1) * 512], op=ALU.mult),
                  reads=[bb, b_modb], writes=[bt])
            Sd.op("pool", lambda e: e.tensor_tensor(out=xt[:, cg * 512:(cg + 1) * 512], in0=xt[:, cg * 512:(cg + 1) * 512], in1=t_t[:], op=ALU.add),
                  reads=[bt, bx], writes=[bx])

        def merge(l):
            with ExitStack() as st:
                wbr, b_wbr = alloc(st, "wbr", [128, 18, D], BF16)
                wo, b_wo = alloc(st, "wo", [128, KC, D], BF16)
                for k0 in range(0, 18, 6):
                    Sd.dma("pool", wbr[:, k0:k0 + 6, :], w_br[l][k0 * 128:(k0 + 6) * 128, :].rearrange("(kc p) n -> p kc n", p=128), writes=[b_wbr])
                Sd.dma("pool", wo[:], w_out[l].rearrange("(kc p) n -> p kc n", p=128), writes=[b_wo])
                ssd_pass2(l)
                yt = Rot([alloc(st, f"myt{i}", [128, 18, 512], BF16) for i in range(2)])
                gt = Rot([alloc(st, f"mgt{i}", [128, 4, 512], BF16) for i in range(4)])
                mt = Rot([alloc(st, f"mmt{i}", [128, 512], F32) for i in range(3)])
                tt = Rot([alloc(st, f"mtt{i}", [128, 512], F32) for i in range(4)])
                mg = Rot([alloc(st, f"mmg{i}", [128, KC, 512], BF16) for i in range(2)])
                xt = Rot([alloc(st, f"mxt{i}", [128, D], F32) for i in range(8)])
                rr = Rot([alloc(st, f"mrr{i}", [128, 512], F32) for i in range(3)])
                kranges = [(0, 4), (4, 8), (8, 12), (12, 18)]
                ld = {}
                gld = {}

                def mloads(tg):
                    cs = slice(tg * 512, (tg + 1) * 512)
                    y_t, by = yt.next()
                    for k0 in range(0, 18, 6):
                        Sd.dma("sp", y_t[:, k0:k0 + 6, :], yT[k0:k0 + 6, :, cs].rearrange("c p t -> p c t"), writes=[by])
                    xl = []
                    for q in range(4):
                        i = 4 * tg + q
                        x_t, bx = xt.next()
                        Sd.dma("sp", x_t[:], xs[i * 128:(i + 1) * 128, :], writes=[bx])
                        xl.append((x_t, bx))
                    ld[tg] = (y_t, by, xl)

                def gloads(k):
                    tg, c = divmod(k, 8)
                    cs = slice(tg * 512, (tg + 1) * 512)
                    g_t, bg = gt.next()
                    Sd.dma("sp", g_t[:], gT[:, :, cs].rearrange("(i c) p t -> c p i t", c=8)[c], writes=[bg])
                    gld[k] = (g_t, bg)
                mloads(0)
                gloads(0); gloads(1)
                for tg in range(NG):
                    cs = slice(tg * 512, (tg + 1) * 512)
                    if tg + 1 < NG:
                        mloads(tg + 1)
                    y_t, by, xl = ld.pop(tg)
                    m_g, bmg = mg.next()
                    for c in range(8):
                        if tg * 8 + c + 2 < NG * 8:
                            gloads(tg * 8 + c + 2)
                        g_t, bg = gld.pop(tg * 8 + c)
                        m_t, bm = mt.next()
                        for i in range(4):
                            bk, bb = bankrot.next()
                            k0, k1 = kranges[i]
                            for kc in range(k0, k1):
                                Sd.op("pe", lambda e, kc=kc: e.matmul(bk[:, :], lhsT=wbr[:, kc, c * 128:(c + 1) * 128], rhs=y_t[:, kc, :],
                                                                       start=(kc == k0), stop=(kc == k1 - 1)),
                                      reads=[b_wbr, by], writes=[bb])
                            if i == 0:
                                Sd.op("dve", lambda e, i=i: e.tensor_tensor(out=m_t[:], in0=bk[:, :], in1=g_t[:, i, :], op=ALU.mult), reads=[bb, bg], writes=[bm])
                            else:
                                t_t, bt = tt.next()
                                Sd.op("dve", lambda e, i=i: e.tensor_tensor(out=t_t[:], in0=bk[:, :], in1=g_t[:, i, :], op=ALU.mult), reads=[bb, bg], writes=[bt])
                                if i < 3:
                                    Sd.op("pool", lambda e: e.tensor_tensor(out=m_t[:], in0=m_t[:], in1=t_t[:], op=ALU.add), reads=[bm, bt], writes=[bm])
                                else:
                                    Sd.op("pool", lambda e: e.tensor_tensor(out=m_g[:, c, :], in0=m_t[:], in1=t_t[:], op=ALU.add), reads=[bm, bt], writes=[bmg])
                    for q in range(4):
                        i = 4 * tg + q
                        x_t, bx = xl[q]
                        for cg in range(2):
                            bk, bb = bankrot.next()
                            for kc in range(KC):
                                Sd.op("pe", lambda e, kc=kc: e.matmul(bk[:, :], lhsT=m_g[:, kc, q * 128:(q + 1) * 128], rhs=wo[:, kc, cg * 512:(cg + 1) * 512],
                                                                       start=(kc == 0), stop=(kc == KC - 1)),
                                      reads=[bmg, b_wo], writes=[bb])
                            resid_evac(rr, bk, bb, 2 * D, x_t, bx, cg)
                        Sd.dma("sp", xs[i * 128:(i + 1) * 128, :], x_t[:], reads=[bx])
                Sd.barrier()

        def ffn(l):
          with ExitStack() as st0:
            w2t, b_w2 = alloc(st0, "w2t", [128, 32, D], BF16)
            for k0 in range(0, 32, 8):
                Sd.dma("pool", w2t[:, k0:k0 + 8, :], w2[l][k0 * 128:(k0 + 8) * 128, :].rearrange("(kc p) n -> p kc n", p=128), writes=[b_w2])
            with ExitStack() as st:
                hT, b_hT = alloc(st, "hT2", [128, KC, S], BF16)
                with ExitStack() as st2:
                    norm_stage(st2, wmod[:, D:2 * D], modb[:, 3 * D:4 * D], hT, b_hT)
                    Sd.barrier()
                with ExitStack() as st2:
                    r32 = Rot([alloc(st2, f"fr{i}", [128, 512], F32) for i in range(3)])
                    hb = Rot([alloc(st2, f"fh{i}", [128, 512], BF16) for i in range(3)])

                    def ev(bk, bb, rows, cc, tg):
                        r_t, br = r32.next()
                        Sd.op("act", lambda e: e.activation(out=r_t[:], in_=bk[:, :], func=AF.Relu), reads=[bb], writes=[br])
                        h_t, bh = hb.next()
                        Sd.op("dve", lambda e: e.tensor_tensor(out=h_t[:], in0=r_t[:], in1=r_t[:], op=ALU.mult), reads=[br], writes=[bh])
                        Sd.dma("sp", hidT[cc, :, tg * 512:(tg + 1) * 512], h_t[:], reads=[bh])
                    wts = Rot([alloc(st2, f"fpw{i}", [128, KC, 512], BF16) for i in range(2)])
                    proj_fm(wts, hT, b_hT, w1[l], 4 * D, ev)
                    Sd.barrier()
            with ExitStack() as st:
                ht = Rot([alloc(st, f"f2h{i}", [128, 32, 512], BF16) for i in range(2)])
                xt = Rot([alloc(st, f"f2x{i}", [128, D], F32) for i in range(8)])
                rr = Rot([alloc(st, f"f2r{i}", [128, 512], F32) for i in range(3)])
                ld = {}

                def wloads(tg):
                    h_t, bh = ht.next()
                    for k0 in range(0, 32, 8):
                        Sd.dma("sp", h_t[:, k0:k0 + 8, :], hidT[k0:k0 + 8, :, tg * 512:(tg + 1) * 512].rearrange("c p t -> p c t"), writes=[bh])
                    xl = []
                    for q in range(4):
                        i = 4 * tg + q
                        x_t, bx = xt.next()
                        Sd.dma("sp", x_t[:], xs[i * 128:(i + 1) * 128, :], writes=[bx])
                        xl.append((x_t, bx))
                    ld[tg] = (h_t, bh, xl)
                wloads(0)
                for tg in range(NG):
                    if tg + 1 < NG:
                        wloads(tg + 1)
                    h_t, bh, xl = ld.pop(tg)
                    for q in range(4):
                        i = 4 * tg + q
                        x_t, bx = xl[q]
                        for cg in range(2):
                            bk, bb = bankrot.next()
                            for kc in range(32):
                                Sd.op("pe", lambda e, kc=kc: e.matmul(bk[:, :], lhsT=h_t[:, kc, q * 128:(q + 1) * 128], rhs=w2t[:, kc, cg * 512:(cg + 1) * 512],
                                                                       start=(kc == 0), stop=(kc == 31)),
                                      reads=[bh, b_w2], writes=[bb])
                            resid_evac(rr, bk, bb, 5 * D, x_t, bx, cg)
                        Sd.dma("sp", xs[i * 128:(i + 1) * 128, :], x_t[:], reads=[bx])
                Sd.barrier()

        for l in range(depth):
            adaln(l)
            with ExitStack() as st:
                hT, b_hT = alloc(st, "hT", [128, KC, S], BF16)
                with ExitStack() as st2:
                    norm_stage(st2, wmod[:, 0:D], modb[:, 0:D], hT, b_hT)
                    Sd.barrier()
                with ExitStack() as st2:
                    mixer_proj(l, st2, hT, b_hT)
                    Sd.barrier()
            gmlp(l)
            fox(l)
            moba(l)
            ssd(l)
            merge(l)
            ffn(l)
        with ExitStack() as st:
            fw, b_fw = alloc(st, "fw", [128, D], F32)
            Sd.dma("sp", fw[:], final_norm_w.partition_broadcast(128), writes=[b_wmod])
            Sd.barrier()
            norm_stage(st, fw[:], None, None, None, final=True)
            Sd.barrier()
        Sd.finish()
        build.ninst = dict(Sd.ninst)
        build.nsem = Sd.nsem
    return nc


def _perm64():
    return np.array(list(range(0, 8)) + list(range(16, 40)) + list(range(8, 16)) + list(range(40, 64)))


def host_consts(S):
    NT = S // 128
    s = np.arange(128)[:, None]
    t = np.arange(128)[None, :]
    c = {}
    c["c_ident"] = np.eye(128, dtype=np.float32)
    c["c_tri"] = np.where(t >= s, 0.0, NEG).astype(np.float32)
    c["c_m01"] = (s <= t).astype(np.float32)
    half = 8
    inv_freq = (500000.0 ** (-np.arange(half, dtype=np.float32) / half)).astype(np.float32)
    ang = np.arange(S, dtype=np.float32)[None, :] * inv_freq[:, None]
    rope = np.zeros((2, 40, S), np.float32)
    for r0 in (0, 32):
        rope[0, r0:r0 + 8] = np.cos(ang)
        rope[1, r0:r0 + 8] = np.sin(ang)
    rope[0, 8:32] = 1.0
    c["c_rope"] = rope
    rot = np.zeros((64, 64), np.float32)
    for r in range(8):
        rot[32 + r, r] = -1.0
        rot[r, 32 + r] = 1.0
    c["c_rot"] = rot
    oh = np.zeros((32, S), np.float32)
    blk = np.arange(S) // 256
    for n in range(min(16, S // 256)):
        oh[n, blk == n] = 30000.0
    c["c_onehot"] = oh
    e0 = np.zeros((128, 128), np.float32); e0[0, :] = 1.0
    c["c_e0"] = e0
    sel = np.zeros((64, 12, 32), np.float32)
    for h in range(12):
        sel[h, h, 0] = 1.0
        sel[32 + h, h, 1] = 1.0
    c["c_sel"] = sel.reshape(64, 12 * 32)
    selb = np.zeros((64, 12, 128), np.float32)
    for h in range(12):
        selb[h, h, :] = 1.0
        selb[32 + h, h, :] = 1.0
    c["c_selb"] = selb.reshape(64, 12 * 128)
    ownneg = np.zeros((128, NT, 16), np.float32)
    own0 = np.ones((128, NT, 16), np.float32)
    for i in range(NT):
        own = i // 2
        ownneg[:, i, own:] = -1e9
        own0[:, i, own] = 0.0
    c["c_ownneg"] = ownneg.reshape(128, NT * 16)
    c["c_own0"] = own0.reshape(128, NT * 16)
    return c


def host_weights(inp, depth):
    f = lambda a: np.ascontiguousarray(np.asarray(a, dtype=np.float32))
    w = {}
    Ld = depth
    for k in ("ada_w", "ada_b", "norm_mix_w", "norm_mlp_w", "w_in", "gmlp_ln_w", "gmlp_ln_b", "gmlp_bs",
              "fox_f_bias", "w_out", "mlp_w1", "mlp_w2"):
        w[k] = f(inp[k][:Ld])
    w["final_norm_w"] = f(inp["final_norm_w"])
    w_in = np.asarray(inp["w_in"])[:Ld]
    perm = _perm64()
    cols = []
    for base in (C_MQ, C_MK):
        for h in range(8):
            cols.extend(base + h * 64 + perm)
    w["w_mqk"] = f(w_in[:, :, np.array(cols)])
    w["gmlp_wsT"] = f(np.transpose(np.asarray(inp["gmlp_ws"])[:Ld], (0, 1, 3, 2)))
    w["conv_wT"] = f(np.transpose(np.asarray(inp["ssm_conv_w"])[:Ld], (0, 2, 1)).reshape(Ld, 10, 128, 4).transpose(0, 2, 1, 3))
    w["conv_b"] = f(np.asarray(inp["ssm_conv_b"])[:Ld].reshape(Ld, 10, 128).transpose(0, 2, 1))
    w["dt_bias"] = f(inp["ssm_dt_bias"][:Ld])
    w["a_log"] = f(inp["ssm_a_log"][:Ld])
    w["ssm_d_rep"] = f(np.repeat(np.asarray(inp["ssm_d"])[:Ld], 64, axis=1).reshape(Ld, 6, 128).transpose(0, 2, 1))
    w["ssm_norm_w"] = f(np.asarray(inp["ssm_norm_w"])[:Ld].reshape(Ld, 6, 128).transpose(0, 2, 1))
    w["w_br"] = f(np.concatenate([np.asarray(inp[k])[:Ld] for k in ("w_branch_a", "w_branch_b", "w_branch_c", "w_branch_d")], axis=1))
    return w


_CACHE = {}


def run(inp, S, depth, ncores, dbg=False):
    key = (S, depth, dbg)
    if key not in _CACHE:
        _CACHE[key] = build(S, depth, dbg)
    nc = _CACHE[key]
    shared = host_weights(inp, depth)
    shared.update(host_consts(S))
    x = np.asarray(inp["x"], dtype=np.float32)
    c = np.asarray(inp["c"], dtype=np.float32)
    in_maps = []
    for b in range(ncores):
        m = dict(shared)
        m["x"] = np.ascontiguousarray(x[b])
        m["cT"] = np.ascontiguousarray(c[b].reshape(KC, 128).T)
        in_maps.append(m)
    res = run_bass_kernel_spmd(nc, in_maps, core_ids=list(range(ncores)))
    return res.results


def kernel(**inputs):
    B, S, _ = inputs["x"].shape
    depth = inputs["w_in"].shape[0]
    res = run(inputs, S, depth, B)
    return np.stack([np.asarray(r["out"], dtype=np.float32) for r in res], axis=0)
```

```python
import math
from contextlib import ExitStack

import numpy as np
import ml_dtypes
import concourse.bass as bass
import concourse.mybir as mybir
from concourse.bass_utils import run_bass_kernel_spmd

F32 = mybir.dt.float32
BF16 = mybir.dt.bfloat16
AF = mybir.ActivationFunctionType
ALU = mybir.AluOpType
AX = mybir.AxisListType

D = 1024
KC = 8
EPS = 1e-6
NEG = -30000.0
IN_COLS = 10260
C_U, C_V, C_FQ, C_FK, C_FV, C_FF = 0, 512, 1024, 1536, 2048, 2560
C_MQ, C_MK, C_MV = 2568, 3080, 3592
C_Z, C_XBC, C_DT, C_G = 4104, 4872, 6152, 6164


class Buf:
    __slots__ = ("name", "w", "r")

    def __init__(self, name=""):
        self.name = name
        self.w = None
        self.r = {}


class Sched:
    LIMIT = 30000
    NSLOT = 8

    def __init__(self, nc, stack):
        self.nc = nc
        self.stack = stack
        self.eng = dict(pe=nc.tensor, act=nc.scalar, dve=nc.vector, pool=nc.gpsimd, sp=nc.sync)
        self.cur = {}
        self.seen = {e: {} for e in self.eng}
        self.slots = {}
        self.slot_i = {}
        self.nsem = 0
        self.ninst = {e: 0 for e in self.eng}

    def _newsem(self, tag):
        self.nsem += 1
        return self.stack.enter_context(self.nc.semaphore(f"s{tag}_{self.nsem}"))

    def _wait(self, e, deps):
        eng = self.eng[e]
        best = {}
        for d in deps:
            if d is None:
                continue
            sem, val, src = d
            if src == "pe" and e == "pe":
                continue
            if self.seen[e].get(sem.num, 0) >= val:
                continue
            if best.get(sem.num, (None, 0))[1] < val:
                best[sem.num] = (sem, val)
        for sem, val in best.values():
            eng.wait_ge(sem, val)
            self.seen[e][sem.num] = val
            self.ninst[e] += 1

    def _deps(self, reads, writes):
        deps = []
        for b in reads:
            deps.append(b.w)
        for b in writes:
            deps.append(b.w)
            deps.extend(b.r.values())
        return deps

    def _record(self, ticket, reads, writes):
        sem = ticket[0]
        for b in reads:
            b.r[sem.num] = ticket
        for b in writes:
            b.w = ticket
            b.r = {}

    def op(self, e, fn, reads=(), writes=()):
        self._wait(e, self._deps(reads, writes))
        c = self.cur.get(e)
        if c is None or c[1] >= self.LIMIT:
            c = [self._newsem(e), 0]
            self.cur[e] = c
        ins = fn(self.eng[e])
        ins.then_inc(c[0], 1)
        c[1] += 1
        self.ninst[e] += 1
        t = (c[0], c[1], e)
        self._record(t, reads, writes)
        return t

    def dma(self, q, out, in_, reads=(), writes=(), **kw):
        if q not in self.slots:
            self.slots[q] = [None] * self.NSLOT
            self.slot_i[q] = 0
        i = self.slot_i[q]
        self.slot_i[q] = (i + 1) % self.NSLOT
        sl = self.slots[q][i]
        deps = self._deps(reads, writes)
        if sl is not None and sl[1] > 0:
            deps.append((sl[0], sl[1] * 16, "dma"))
        if sl is None or sl[1] * 16 >= self.LIMIT:
            self._wait(q, deps)
            deps = []
            sl = [self._newsem(f"d{q}{i}"), 0]
            self.slots[q][i] = sl
        self._wait(q, deps)
        ins = self.eng[q].dma_start(out=out, in_=in_, **kw)
        ins.then_inc(sl[0], 16)
        sl[1] += 1
        self.ninst[q] += 1
        t = (sl[0], sl[1] * 16, "dma")
        self._record(t, reads, writes)
        return t

    def _all(self):
        deps = []
        for q, sl in self.slots.items():
            for s in sl:
                if s is not None and s[1] > 0:
                    deps.append((s[0], s[1] * 16, "dma"))
        for e, c in self.cur.items():
            deps.append((c[0], c[1], e))
        return deps

    def barrier(self):
        deps = self._all()
        for e in self.eng:
            self._wait(e, deps)

    def finish(self):
        self._wait("sp", self._all())


class Rot:
    def __init__(self, items):
        self.items = items
        self.i = 0

    def next(self):
        it = self.items[self.i]
        self.i = (self.i + 1) % len(self.items)
        return it


def build(S, depth, dbg=False):
    NT = S // 128
    NG = S // 512
    nc = bass.Bass("TRN2", target_bir_lowering=False)

    def din(name, shape, dt=F32):
        return nc.dram_tensor(name, list(shape), dt, kind="ExternalInput").ap()

    def dscr(name, shape, dt):
        return nc.dram_tensor(name, list(shape), dt, kind="ExternalOutput" if dbg else "Internal").ap()

    L = depth
    x_in = din("x", [S, D])
    cT_in = din("cT", [128, KC])
    ada_w = din("ada_w", [L, D, 6 * D]); ada_b = din("ada_b", [L, 6 * D])
    norm_mix_w = din("norm_mix_w", [L, D]); norm_mlp_w = din("norm_mlp_w", [L, D])
    final_norm_w = din("final_norm_w", [D])
    w_in = din("w_in", [L, D, IN_COLS])
    w_mqk = din("w_mqk", [L, D, 1024])
    gmlp_ln_w = din("gmlp_ln_w", [L, 512]); gmlp_ln_b = din("gmlp_ln_b", [L, 512])
    gmlp_wsT = din("gmlp_wsT", [L, 8, 128, 128]); gmlp_bs = din("gmlp_bs", [L, 8, 128])
    fox_f_bias = din("fox_f_bias", [L, 8])
    conv_wT = din("conv_wT", [L, 128, 10, 4]); conv_b = din("conv_b", [L, 128, 10])
    dt_bias = din("dt_bias", [L, 12]); a_log = din("a_log", [L, 12])
    ssm_d_rep = din("ssm_d_rep", [L, 128, 6]); ssm_norm_w = din("ssm_norm_w", [L, 128, 6])
    w_br = din("w_br", [L, 2304, D])
    w_out = din("w_out", [L, D, D])
    w1 = din("mlp_w1", [L, D, 4 * D]); w2 = din("mlp_w2", [L, 4 * D, D])
    c_ident = din("c_ident", [128, 128])
    c_tri = din("c_tri", [128, 128])
    c_m01 = din("c_m01", [128, 128])
    c_rope = din("c_rope", [2, 40, S])
    c_onehot = din("c_onehot", [32, S])
    c_e0 = din("c_e0", [128, 128])
    c_rot = din("c_rot", [64, 64])
    c_sel = din("c_sel", [64, 12 * 32])
    c_selb = din("c_selb", [64, 12 * 128])
    c_ownneg = din("c_ownneg", [128, NT * 16])
    c_own0 = din("c_own0", [128, NT * 16])

    out = nc.dram_tensor("out", [S, D], F32, kind="ExternalOutput").ap()

    xs = dscr("xs", [S, D], F32)
    uT = dscr("uT", [4, 128, S], BF16)
    vg = dscr("vg", [S, 512], F32)
    qTf = dscr("qTf", [8, 64, S], BF16); kTf = dscr("kTf", [8, 64, S], BF16); vf = dscr("vf", [S, 512], BF16)
    fT = dscr("fT", [8, S], F32)
    qTm = dscr("qTm", [8, 64, S], BF16); kTm = dscr("kTm", [8, 64, S], BF16); vm = dscr("vm", [S, 512], BF16)
    szT = dscr("szT", [6, 128, S], BF16)
    xbcT = dscr("xbcT", [10, 128, S], F32)
    dtT = dscr("dtT", [12, S], F32); dtm = dscr("dtm", [S, 12], F32)
    gT = dscr("gT", [32, 128, S], BF16)
    yT = dscr("yT", [18, 128, S], BF16)
    ysT = dscr("ysT", [12, 64, S], F32)
    hidT = dscr("hidT", [32, 128, S], BF16)
    xsTd = dscr("xsTd", [6, 128, S], BF16)

    with ExitStack() as top:
        Sd = Sched(nc, top)

        uid = [0]

        def alloc(stack, name, shape, dt):
            uid[0] += 1
            t = stack.enter_context(nc.sbuf_tensor(f"{name}_{uid[0]}", list(shape), dt))
            return t, Buf(name)

        banks = []
        for i in range(8):
            t = top.enter_context(nc.psum_tensor(f"bank{i}", [128, 512], F32))
            banks.append((t, Buf(f"bank{i}")))
        accrot = Rot(banks[0:2])
        bankrot = Rot(banks[2:8])

        ident, b_ident = alloc(top, "ident", [128, 128], F32)
        identb, b_identb = alloc(top, "identb", [128, 128], BF16)
        trib, b_trib = alloc(top, "trib", [128, 128], BF16)
        ones32, b_ones32 = alloc(top, "ones32", [128, 128], F32)
        e0, b_e0 = alloc(top, "e0", [128, 128], F32)
        sel, b_sel = alloc(top, "sel", [64, 12 * 32], BF16)
        ca, b_ca = alloc(top, "ca", [128, KC], F32)
        modb, b_modb = alloc(top, "modb", [128, 6 * D], F32)
        wmod, b_wmod = alloc(top, "wmod", [128, 2 * D], F32)
        consts_r = [b_ident, b_identb, b_trib, b_ones32, b_e0, b_sel]

        Sd.dma("sp", ident[:], c_ident, writes=[b_ident])
        Sd.dma("pool", identb[:], c_ident, writes=[b_identb])
        Sd.dma("pool", trib[:], c_tri, writes=[b_trib])
        Sd.dma("sp", e0[:], c_e0, writes=[b_e0])
        Sd.dma("pool", sel[:], c_sel, writes=[b_sel])
        Sd.op("dve", lambda e: e.memset(ones32[:], 1.0), writes=[b_ones32])
        with ExitStack() as st:
            ct, b_ct = alloc(st, "ct", [128, KC], F32)
            Sd.dma("sp", ct[:], cT_in, writes=[b_ct])
            Sd.op("act", lambda e: e.activation(out=ca[:], in_=ct[:], func=AF.Silu), reads=[b_ct], writes=[b_ca])
            b_x0 = Buf()
            for i in range(NT):
                Sd.dma("sp", xs[i * 128:(i + 1) * 128, :], x_in[i * 128:(i + 1) * 128, :], writes=[b_x0])
            Sd.barrier()

        dummy, b_dummy = alloc(top, "dummy", [128, 512], BF16)
        Sd.op("pool", lambda e: e.memset(dummy[:], 1.0), writes=[b_dummy])

        def pe_warm(n=24):
            for _ in range(n):
                bk, bb = bankrot.next()
                Sd.op("pe", lambda e: e.matmul(bk[:, :], lhsT=identb[:], rhs=dummy[:], start=True, stop=True),
                      reads=[b_identb, b_dummy], writes=[bb])

        def adaln(l):
            with ExitStack() as st:
                cact_rep, b_cact = alloc(st, "cact_rep", [128, KC, 128], F32)
                Sd.op("dve", lambda e: e.tensor_copy(out=cact_rep[:], in_=ca[:].unsqueeze(2).to_broadcast([128, KC, 128])),
                      reads=[b_ca], writes=[b_cact])
                awt = [alloc(st, f"awt{i}", [128, KC, 512], F32) for i in range(2)]
                awr = Rot(awt)
                abb, b_abb = alloc(st, "abb", [128, 6 * D], F32)
                nw, b_nw = alloc(st, "nw", [128, 2 * D], F32)
                Sd.dma("sp", abb[:], ada_b[l].partition_broadcast(128), writes=[b_abb])
                Sd.dma("sp", nw[:, 0:D], norm_mix_w[l].partition_broadcast(128), writes=[b_nw])
                Sd.dma("sp", nw[:, D:2 * D], norm_mlp_w[l].partition_broadcast(128), writes=[b_nw])
                for g in range(12):
                    w, bw = awr.next()
                    Sd.dma("sp", w[:], ada_w[l][:, g * 512:(g + 1) * 512].rearrange("(kc p) n -> p kc n", p=128), writes=[bw])
                    bk, bb = bankrot.next()
                    for kc in range(KC):
                        Sd.op("pe", lambda e, kc=kc: e.matmul(bk[:, :], lhsT=cact_rep[:, kc, :], rhs=w[:, kc, :],
                                                               start=(kc == 0), stop=(kc == KC - 1)),
                              reads=[b_cact, bw], writes=[bb])
                    Sd.op("dve", lambda e: e.tensor_tensor(out=modb[:, g * 512:(g + 1) * 512], in0=bk[:, :],
                                                           in1=abb[:, g * 512:(g + 1) * 512], op=ALU.add),
                          reads=[bb, b_abb], writes=[b_modb])
                for j, sc_off in enumerate((1 * D, 4 * D)):
                    Sd.op("dve", lambda e, j=j, sc_off=sc_off: e.scalar_tensor_tensor(
                        out=wmod[:, j * D:(j + 1) * D], in0=modb[:, sc_off:sc_off + D], scalar=1.0,
                        in1=nw[:, j * D:(j + 1) * D], op0=ALU.add, op1=ALU.mult),
                        reads=[b_modb, b_nw], writes=[b_wmod])
                Sd.barrier()

        def norm_stage(st, wm_ap, sh_ap, hT, b_hT, final=False):
            xt = Rot([alloc(st, f"nx{i}", [128, D], F32) for i in range(3)])
            h1 = Rot([alloc(st, f"nh{i}", [128, D], F32) for i in range(3)])
            h2 = Rot([alloc(st, f"ng{i}", [128, D], F32) for i in range(3)])
            junk, b_junk = alloc(st, "njunk", [128, D], BF16)
            stat = Rot([alloc(st, f"nst{i}", [128, 4], F32) for i in range(3)])
            for i in range(NT):
                x_t, bx = xt.next()
                Sd.dma("sp", x_t[:], xs[i * 128:(i + 1) * 128, :], writes=[bx])
                s_t, bs = stat.next()
                Sd.op("act", lambda e: e.activation(out=junk[:], in_=x_t[:], func=AF.Square, accum_out=s_t[:, 0:1]),
                      reads=[bx], writes=[b_junk, bs])
                Sd.op("dve", lambda e: e.tensor_scalar(out=s_t[:, 1:2], in0=s_t[:, 0:1], scalar1=1.0 / D, scalar2=EPS,
                                                       op0=ALU.mult, op1=ALU.add), reads=[bs], writes=[bs])
                Sd.op("act", lambda e: e.activation(out=s_t[:, 2:3], in_=s_t[:, 1:2], func=AF.Sqrt), reads=[bs], writes=[bs])
                Sd.op("dve", lambda e: e.reciprocal(out=s_t[:, 3:4], in_=s_t[:, 2:3]), reads=[bs], writes=[bs])
                a_t, ba = h1.next()
                Sd.op("dve", lambda e: e.scalar_tensor_tensor(out=a_t[:], in0=x_t[:], scalar=s_t[:, 3:4], in1=wm_ap,
                                                              op0=ALU.mult, op1=ALU.mult),
                      reads=[bx, bs, b_wmod], writes=[ba])
                if final:
                    Sd.dma("sp", out[i * 128:(i + 1) * 128, :], a_t[:], reads=[ba])
                    continue
                g_t, bg = h2.next()
                Sd.op("pool", lambda e: e.tensor_tensor(out=g_t[:], in0=a_t[:], in1=sh_ap, op=ALU.add),
                      reads=[ba, b_modb], writes=[bg])
                for half in range(2):
                    bk, bb = bankrot.next()
                    for q in range(4):
                        kc = half * 4 + q
                        Sd.op("pe", lambda e, kc=kc, q=q: e.transpose(out=bk[:, q * 128:(q + 1) * 128],
                                                                      in_=g_t[:, kc * 128:(kc + 1) * 128], identity=ident[:]),
                              reads=[bg, b_ident], writes=[bb])
                    eng = "act" if half == 0 else "dve"
                    if eng == "act":
                        Sd.op("act", lambda e: e.copy(out=hT[:, half * 4:half * 4 + 4, i * 128:(i + 1) * 128],
                                                      in_=bk[:, :].rearrange("p (q t) -> p q t", q=4)),
                              reads=[bb], writes=[b_hT])
                    else:
                        Sd.op("dve", lambda e: e.tensor_copy(out=hT[:, half * 4:half * 4 + 4, i * 128:(i + 1) * 128],
                                                             in_=bk[:, :].rearrange("p (q t) -> p q t", q=4)),
                              reads=[bb], writes=[b_hT])

        def proj_fm(wts, hT, b_hT, w_ap, ncols, evac, nk=KC):
            for g0 in range(0, ncols, 512):
                n = min(512, ncols - g0)
                w, bw = wts.next()
                Sd.dma("pool", w[:, :, 0:n], w_ap[:, g0:g0 + n].rearrange("(kc p) n -> p kc n", p=128), writes=[bw])
                for tg in range(NG):
                    for c0 in range(0, n, 128):
                        rows = min(128, n - c0)
                        bk, bb = bankrot.next()
                        for kc in range(nk):
                            Sd.op("pe", lambda e, kc=kc: e.matmul(bk[0:rows, :], lhsT=w[:, kc, c0:c0 + rows],
                                                                   rhs=hT[:, kc, tg * 512:(tg + 1) * 512],
                                                                   start=(kc == 0), stop=(kc == nk - 1)),
                                  reads=[bw, b_hT], writes=[bb])
                        evac(bk, bb, rows, (g0 + c0) // 128, tg)

        def proj_tm(wts, hT, b_hT, w_ap, ncols, evac):
            w, bw = wts.next()
            assert ncols <= 512
            Sd.dma("pool", w[:, :, 0:ncols], w_ap.rearrange("(kc p) n -> p kc n", p=128), writes=[bw])
            for i in range(NT):
                bk, bb = bankrot.next()
                for kc in range(KC):
                    Sd.op("pe", lambda e, kc=kc: e.matmul(bk[:, 0:ncols], lhsT=hT[:, kc, i * 128:(i + 1) * 128],
                                                           rhs=w[:, kc, 0:ncols], start=(kc == 0), stop=(kc == KC - 1)),
                          reads=[bw, b_hT], writes=[bb])
                evac(bk, bb, i)

        def mixer_proj(l, st, hT, b_hT):
            stg = Rot([alloc(st, f"stg{i}", [128, 512], BF16) for i in range(3)])
            stg32 = Rot([alloc(st, f"stgf{i}", [128, 512], F32) for i in range(3)])
            wts = Rot([alloc(st, f"pw{i}", [128, KC, 512], BF16) for i in range(2)])
            wl = w_in[l]

            def ev_act(dst_fn, func, scale=1.0, f32=False):
                def ev(bk, bb, rows, cc, tg):
                    s_t, bs = (stg32 if f32 else stg).next()
                    Sd.op("act", lambda e: e.activation(out=s_t[0:rows, :], in_=bk[0:rows, :], func=func, scale=scale),
                          reads=[bb], writes=[bs])
                    Sd.dma("sp", dst_fn(cc, tg, rows), s_t[0:rows, :], reads=[bs])
                return ev

            def ev_copy(dst_fn, f32=False):
                def ev(bk, bb, rows, cc, tg):
                    s_t, bs = (stg32 if f32 else stg).next()
                    Sd.op("dve", lambda e: e.tensor_copy(out=s_t[0:rows, :], in_=bk[0:rows, :]), reads=[bb], writes=[bs])
                    Sd.dma("sp", dst_fn(cc, tg, rows), s_t[0:rows, :], reads=[bs])
                return ev

            def tgs(tg):
                return slice(tg * 512, (tg + 1) * 512)

            proj_fm(wts, hT, b_hT, wl[:, C_U:C_U + 512], 512,
                    ev_act(lambda cc, tg, rows: uT[cc, :, tgs(tg)], AF.Gelu_apprx_tanh))
            proj_fm(wts, hT, b_hT, wl[:, C_FQ:C_FQ + 512], 512,
                    ev_act(lambda cc, tg, rows: qTf[2 * cc:2 * cc + 2, :, tgs(tg)].rearrange("h d t -> (h d) t"), AF.Copy, 0.125))
            proj_fm(wts, hT, b_hT, wl[:, C_FK:C_FK + 512], 512,
                    ev_copy(lambda cc, tg, rows: kTf[2 * cc:2 * cc + 2, :, tgs(tg)].rearrange("h d t -> (h d) t")))
            proj_fm(wts, hT, b_hT, wl[:, C_FF:C_FF + 8], 8,
                    ev_copy(lambda cc, tg, rows: fT[:, tgs(tg)], f32=True))
            wm = w_mqk[l]
            proj_fm(wts, hT, b_hT, wm[:, 0:512], 512,
                    ev_act(lambda cc, tg, rows: qTm[2 * cc:2 * cc + 2, :, tgs(tg)].rearrange("h d t -> (h d) t"), AF.Copy, 0.125))
            proj_fm(wts, hT, b_hT, wm[:, 512:1024], 512,
                    ev_copy(lambda cc, tg, rows: kTm[2 * cc:2 * cc + 2, :, tgs(tg)].rearrange("h d t -> (h d) t")))
            proj_fm(wts, hT, b_hT, wl[:, C_Z:C_Z + 768], 768,
                    ev_act(lambda cc, tg, rows: szT[cc, :, tgs(tg)], AF.Silu))
            proj_fm(wts, hT, b_hT, wl[:, C_XBC:C_XBC + 1280], 1280,
                    ev_copy(lambda cc, tg, rows: xbcT[cc, :, tgs(tg)], f32=True))
            proj_fm(wts, hT, b_hT, wl[:, C_DT:C_DT + 12], 12,
                    ev_copy(lambda cc, tg, rows: dtT[:, tgs(tg)], f32=True))
            proj_fm(wts, hT, b_hT, wl[:, C_G:C_G + 4096], 4096,
                    ev_act(lambda cc, tg, rows: gT[cc, :, tgs(tg)], AF.Sigmoid))

            def ev_tm(dst, n, func=None, f32=False):
                def ev(bk, bb, i):
                    s_t, bs = (stg32 if f32 else stg).next()
                    if func is None:
                        Sd.op("dve", lambda e: e.tensor_copy(out=s_t[:, 0:n], in_=bk[:, 0:n]), reads=[bb], writes=[bs])
                    else:
                        Sd.op("act", lambda e: e.activation(out=s_t[:, 0:n], in_=bk[:, 0:n], func=func), reads=[bb], writes=[bs])
                    Sd.dma("sp", dst[i * 128:(i + 1) * 128, :], s_t[:, 0:n], reads=[bs])
                return ev
            proj_tm(wts, hT, b_hT, wl[:, C_V:C_V + 512], 512, ev_tm(vg, 512, AF.Gelu_apprx_tanh, True))
            proj_tm(wts, hT, b_hT, wl[:, C_FV:C_FV + 512], 512, ev_tm(vf, 512))
            proj_tm(wts, hT, b_hT, wl[:, C_MV:C_MV + 512], 512, ev_tm(vm, 512))
            proj_tm(wts, hT, b_hT, wl[:, C_DT:C_DT + 12], 12, ev_tm(dtm, 12, None, True))

        def gmlp(l):
            with ExitStack() as st:
                wsT, b_wsT = alloc(st, "wsT", [128, 8, 128], F32)
                wsm, b_wsm = alloc(st, "wsm", [128, 8, 128], BF16)
                m01, b_m01 = alloc(st, "m01", [128, 128], F32)
                bsb, b_bsb = alloc(st, "bsb", [128, 4, 128], F32)
                lnw, b_lnw = alloc(st, "lnw", [128, 512], F32)
                lnb, b_lnb = alloc(st, "lnb", [128, 512], F32)
                Sd.dma("sp", wsT[:], gmlp_wsT[l].rearrange("g s t -> s g t"), writes=[b_wsT])
                Sd.dma("sp", m01[:], c_m01, writes=[b_m01])
                for c in range(4):
                    for hh in range(2):
                        Sd.dma("sp", bsb[hh * 64:(hh + 1) * 64, c, :], gmlp_bs[l, 2 * c + hh].partition_broadcast(64), writes=[b_bsb])
                Sd.dma("sp", lnw[:], gmlp_ln_w[l].partition_broadcast(128), writes=[b_lnw])
                Sd.dma("sp", lnb[:], gmlp_ln_b[l].partition_broadcast(128), writes=[b_lnb])
                Sd.op("dve", lambda e: e.tensor_tensor(out=wsm[:], in0=wsT[:], in1=m01[:].unsqueeze(1).to_broadcast([128, 8, 128]),
                                                       op=ALU.mult), reads=[b_wsT, b_m01], writes=[b_wsm])
                vt = Rot([alloc(st, f"gv{i}", [128, 512], F32) for i in range(4)])
                ut = Rot([alloc(st, f"gu{i}", [128, 4, 128], BF16) for i in range(4)])
                v1 = Rot([alloc(st, f"gw{i}", [128, 512], F32) for i in range(4)])
                v2 = Rot([alloc(st, f"gx{i}", [128, 512], F32) for i in range(4)])
                vn = Rot([alloc(st, f"gn{i}", [128, 512], BF16) for i in range(4)])
                tt = Rot([alloc(st, f"gt{i}", [128, 512], F32) for i in range(4)])
                yo = Rot([alloc(st, f"gy{i}", [128, 4, 128], BF16) for i in range(4)])
                junk, b_junk = alloc(st, "gjunk", [128, 512], BF16)
                stat = Rot([alloc(st, f"gs{i}", [128, 8], F32) for i in range(4)])
                ld = {}

                def gloads(i):
                    ts = slice(i * 128, (i + 1) * 128)
                    v_t, bv = vt.next()
                    u_t, bu = ut.next()
                    Sd.dma("sp", v_t[:], vg[ts, :], writes=[bv])
                    Sd.dma("sp", u_t[:], uT[:, :, ts].rearrange("c p t -> p c t"), writes=[bu])
                    ld[i] = (v_t, bv, u_t, bu)
                gloads(0); gloads(1)
                for i in range(NT):
                    ts = slice(i * 128, (i + 1) * 128)
                    if i + 2 < NT:
                        gloads(i + 2)
                    v_t, bv, u_t, bu = ld.pop(i)
                    s, bs = stat.next()
                    Sd.op("dve", lambda e: e.tensor_reduce(out=s[:, 0:1], in_=v_t[:], axis=AX.X, op=ALU.add), reads=[bv], writes=[bs])
                    Sd.op("act", lambda e: e.activation(out=junk[:], in_=v_t[:], func=AF.Square, accum_out=s[:, 1:2]),
                          reads=[bv], writes=[b_junk, bs])
                    Sd.op("dve", lambda e: e.tensor_scalar(out=s[:, 2:3], in0=s[:, 0:1], scalar1=-1.0 / 512, scalar2=None, op0=ALU.mult),
                          reads=[bs], writes=[bs])
                    Sd.op("dve", lambda e: e.tensor_tensor(out=s[:, 3:4], in0=s[:, 2:3], in1=s[:, 2:3], op=ALU.mult), reads=[bs], writes=[bs])
                    Sd.op("dve", lambda e: e.scalar_tensor_tensor(out=s[:, 4:5], in0=s[:, 1:2], scalar=1.0 / 512, in1=s[:, 3:4],
                                                                  op0=ALU.mult, op1=ALU.subtract), reads=[bs], writes=[bs])
                    Sd.op("dve", lambda e: e.tensor_scalar(out=s[:, 5:6], in0=s[:, 4:5], scalar1=EPS, scalar2=None, op0=ALU.add),
                          reads=[bs], writes=[bs])
                    Sd.op("act", lambda e: e.activation(out=s[:, 6:7], in_=s[:, 5:6], func=AF.Sqrt), reads=[bs], writes=[bs])
                    Sd.op("dve", lambda e: e.reciprocal(out=s[:, 7:8], in_=s[:, 6:7]), reads=[bs], writes=[bs])
                    a1, ba1 = v1.next()
                    Sd.op("dve", lambda e: e.tensor_scalar(out=a1[:], in0=v_t[:], scalar1=s[:, 2:3], scalar2=s[:, 7:8],
                                                           op0=ALU.add, op1=ALU.mult), reads=[bv, bs], writes=[ba1])
                    a2, ba2 = v2.next()
                    Sd.op("pool", lambda e: e.tensor_tensor(out=a2[:], in0=a1[:], in1=lnw[:], op=ALU.mult), reads=[ba1, b_lnw], writes=[ba2])
                    n_t, bn = vn.next()
                    Sd.op("pool", lambda e: e.tensor_tensor(out=n_t[:], in0=a2[:], in1=lnb[:], op=ALU.add), reads=[ba2, b_lnb], writes=[bn])
                    bk, bb = bankrot.next()
                    for g in range(8):
                        po = (g % 2) * 64
                        Sd.op("pe", lambda e, g=g, po=po: e.matmul(bk[po:po + 64, (g // 2) * 128:(g // 2 + 1) * 128],
                                                                   lhsT=n_t[:, g * 64:(g + 1) * 64], rhs=wsm[:, g, :],
                                                                   start=True, stop=True),
                              reads=[bn, b_wsm], writes=[bb])
                    t_t, bt = tt.next()
                    Sd.op("dve", lambda e: e.tensor_tensor(out=t_t[:], in0=bk[:, :], in1=bsb[:].rearrange("p c t -> p (c t)"), op=ALU.add),
                          reads=[bb, b_bsb], writes=[bt])
                    y_t, by = yo.next()
                    Sd.op("pool", lambda e: e.tensor_tensor(out=y_t[:].rearrange("p c t -> p (c t)"), in0=t_t[:],
                                                            in1=u_t[:].rearrange("p c t -> p (c t)"), op=ALU.mult),
                          reads=[bt, bu], writes=[by])
                    Sd.dma("sp", yT[0:4, :, ts].rearrange("c p t -> p c t"), y_t[:], reads=[by])
                Sd.barrier()

        def decay_prep(st, srcT, b_src, nh, tagp):
            cum, b_cum = alloc(st, tagp + "cum", [nh, S], F32)
            one1, b_one1 = alloc(st, tagp + "one1", [nh, 1], F32)
            Sd.op("dve", lambda e: e.memset(one1[:], 1.0), writes=[b_one1])
            Sd.op("dve", lambda e: e.tensor_tensor_scan(out=cum[:], data0=one1[:, 0:1].to_broadcast([nh, S]), data1=srcT[:],
                                                        initial=0.0, op0=ALU.mult, op1=ALU.add),
                  reads=[b_one1, b_src], writes=[b_cum])
            a32, b_a32 = alloc(st, tagp + "a32", [nh, S], F32)
            AH, b_AH = alloc(st, tagp + "AH", [64, S], BF16)
            Sd.op("pool", lambda e: e.memset(AH[:], 0.0), writes=[b_AH])
            for I in range(NG):
                cs = slice(I * 512, (I + 1) * 512)
                Sd.op("dve", lambda e, cs=cs, I=I: e.tensor_scalar(out=a32[:, cs], in0=cum[:, cs], scalar1=cum[:, I * 512:I * 512 + 1],
                                                                   scalar2=None, op0=ALU.subtract),
                      reads=[b_cum], writes=[b_a32])
            Sd.op("dve", lambda e: e.tensor_copy(out=AH[0:nh, :], in_=a32[:]), reads=[b_a32], writes=[b_AH])
            Sd.op("dve", lambda e: e.tensor_tensor(out=AH[32:32 + nh, :], in0=a32[:], in1=AH[0:nh, :], op=ALU.subtract),
                  reads=[b_a32, b_AH], writes=[b_AH])
            cumT, b_cumT = alloc(st, tagp + "cumT", [128, NT, nh], F32)
            bk, bb = bankrot.next()
            for i in range(NT):
                Sd.op("pe", lambda e, i=i: e.transpose(out=bk[:, i * nh:(i + 1) * nh], in_=cum[:, i * 128:(i + 1) * 128],
                                                       identity=ident[0:nh, 0:nh]),
                      reads=[b_cum, b_ident], writes=[bb])
            Sd.op("dve", lambda e: e.tensor_copy(out=cumT[:].rearrange("p i h -> p (i h)"), in_=bk[:, 0:NT * nh]), reads=[bb], writes=[b_cumT])
            refb, b_refb = alloc(st, tagp + "refb", [128, NG, nh], F32)
            bk, bb = bankrot.next()
            for I in range(NG):
                Sd.op("pe", lambda e, I=I: e.matmul(bk[:, I * nh:(I + 1) * nh], lhsT=e0[:], rhs=cumT[:, 4 * I, :], start=True, stop=True),
                      reads=[b_e0, b_cumT], writes=[bb])
            Sd.op("dve", lambda e: e.tensor_copy(out=refb[:].rearrange("p i h -> p (i h)"), in_=bk[:, 0:NG * nh]), reads=[bb], writes=[b_refb])
            fb, b_fb = alloc(st, tagp + "fb", [128, NG, NT, nh], F32)
            for I in range(NG):
                Sd.op("dve", lambda e, I=I: e.tensor_tensor(out=fb[:, I, :, :], in0=refb[:, I, :].unsqueeze(1).to_broadcast([128, NT, nh]),
                                                            in1=cumT[:], op=ALU.subtract),
                      reads=[b_refb, b_cumT], writes=[b_fb])
            return AH, b_AH, fb, b_fb

        def attn_core(st, kind, l, AH=None, b_AH=None, fb=None, b_fb=None):
            qsrc, ksrc, vsrc, ybase = (qTf, kTf, vf, 4) if kind == "fox" else (qTm, kTm, vm, 8)
            Qa = Rot([alloc(st, f"Qa{i}", [96, S], BF16) for i in range(2)])
            Ka = Rot([alloc(st, f"Ka{i}", [96, S], BF16) for i in range(2)])
            Va = Rot([alloc(st, f"Va{i}", [128, NT, 65], BF16) for i in range(2)])
            PT = Rot([alloc(st, f"PT{i}", [128, 512], BF16) for i in range(5)])
            osb = Rot([alloc(st, f"osb{i}", [65, 512], F32) for i in range(2)])
            rsb = Rot([alloc(st, f"rsb{i}", [65, 512], F32) for i in range(2)])
            yst = Rot([alloc(st, f"yst{i}", [64, 512], BF16) for i in range(2)])
            rhl = Rot([alloc(st, f"rhl{i}", [97, 512], BF16) for i in range(2)])
            for (h_t0, bh0) in rhl.items:
                Sd.op("pool", lambda e, h_t0=h_t0: e.memset(h_t0[64:97, :], 0.0), writes=[bh0])
            onesb, b_onesb = alloc(st, "onesb", [128, 64], BF16)
            Sd.op("pool", lambda e: e.memset(onesb[:], 1.0), writes=[b_onesb])
            if kind == "moba":
                rope, b_rope = alloc(st, "rope", [40, 2, S], F32)
                Sd.dma("sp", rope[:, 0, :], c_rope[0], writes=[b_rope])
                Sd.dma("sp", rope[:, 1, :], c_rope[1], writes=[b_rope])
                oneh, b_oneh = alloc(st, "oneh", [32, S], BF16)
                Sd.dma("pool", oneh[:], c_onehot, writes=[b_oneh])
                ownneg, b_ownneg = alloc(st, "ownneg", [128, NT * 16], F32)
                own0, b_own0 = alloc(st, "own0", [128, NT * 16], F32)
                Sd.dma("sp", ownneg[:], c_ownneg, writes=[b_ownneg])
                Sd.dma("sp", own0[:], c_own0, writes=[b_own0])
                rt1 = Rot([alloc(st, f"rt1{i}", [40, 512], F32) for i in range(3)])
                rt2 = Rot([alloc(st, f"rt2{i}", [40, 512], F32) for i in range(3)])
                rotm, b_rotm = alloc(st, "rotm", [64, 64], BF16)
                Sd.dma("pool", rotm[:], c_rot, writes=[b_rotm])
                kmT, b_kmT = alloc(st, "kmT", [64, 16], F32)
                kmb, b_kmb = alloc(st, "kmb", [64, 32], BF16)
                G0, b_G0 = alloc(st, "G0", [128, NT, 16], F32)
                G1, b_G1 = alloc(st, "G1", [128, NT, 16], F32)
                EQ, b_EQ = alloc(st, "EQ", [128, NT, 16], F32)
                MB, b_MB = alloc(st, "MB", [128, NT, 32], F32)
                mx, b_mx = alloc(st, "mx", [128, NT], F32)
                Sd.op("pool", lambda e: e.memset(MB[:], 0.0), writes=[b_MB])
            for (q_t, bq), (k_t, bkk) in zip(Qa.items, Ka.items):
                Sd.op("pool", lambda e, k_t=k_t: e.memset(k_t[64:96, :], 0.0), writes=[bkk])
                if kind == "fox":
                    Sd.op("pool", lambda e, k_t=k_t: e.memset(k_t[64:66, :], 1.0), writes=[bkk])
                else:
                    Sd.op("dve", lambda e, k_t=k_t: e.tensor_copy(out=k_t[64:96, :], in_=oneh[:]), reads=[b_oneh], writes=[bkk])
            for (v_t, bv) in Va.items:
                Sd.op("pool", lambda e, v_t=v_t: e.memset(v_t[:, :, 64:65], 1.0), writes=[bv])

            pb, pbb = banks[7]
            wrot = Rot(banks[2:7])
            hc = {}

            def prolog_steps(h):
                steps = []
                q_t, bq = Qa.next()
                k_t, bkk = Ka.next()
                v_t, bv = Va.next()
                hc[h] = (q_t, bq, k_t, bkk, v_t, bv)

                def loads():
                    Sd.dma("sp", q_t[0:64, :], qsrc[h], writes=[bq])
                    Sd.dma("sp", k_t[0:64, :], ksrc[h], writes=[bkk])
                    Sd.dma("sp", v_t[:, :, 0:64], vsrc[:, h * 64:(h + 1) * 64].rearrange("(i p) d -> p i d", p=128), writes=[bv])
                steps.append(loads)
                if kind == "fox":
                    for I in range(NG):
                        def f(I=I):
                            Sd.op("pe", lambda e: e.matmul(pb[0:32, :], lhsT=sel[0:64, h * 32:(h + 1) * 32], rhs=AH[0:64, I * 512:(I + 1) * 512],
                                                           start=True, stop=True), reads=[b_sel, b_AH], writes=[pbb])
                            Sd.op("dve", lambda e: e.tensor_copy(out=q_t[64:96, I * 512:(I + 1) * 512], in_=pb[0:32, :]), reads=[pbb], writes=[bq])
                        steps.append(f)
                    return steps
                for (T, bT) in ((q_t, bq), (k_t, bkk)):
                    for I in range(NG):
                        def f(T=T, bT=bT, I=I):
                            cs = slice(I * 512, (I + 1) * 512)
                            Sd.op("pe", lambda e: e.matmul(pb[0:64, :], lhsT=rotm[:, :], rhs=T[0:64, cs], start=True, stop=True),
                                  reads=[b_rotm, bT], writes=[pbb])
                            t1, b1 = rt1.next()
                            t2, b2 = rt2.next()
                            Sd.op("dve", lambda e: e.tensor_tensor(out=t1[:, :], in0=pb[0:40, :], in1=rope[:, 1, cs], op=ALU.mult), reads=[pbb, b_rope], writes=[b1])
                            Sd.op("pool", lambda e: e.tensor_tensor(out=t2[:, :], in0=T[0:40, cs], in1=rope[:, 0, cs], op=ALU.mult), reads=[bT, b_rope], writes=[b2])
                            Sd.op("dve", lambda e: e.tensor_tensor(out=T[0:40, cs], in0=t1[:, :], in1=t2[:, :], op=ALU.add), reads=[b1, b2], writes=[bT])
                        steps.append(f)
                nblk = S // 256

                def kmean():
                    Sd.op("dve", lambda e: e.tensor_reduce(out=kmT[:, 0:nblk], in_=k_t[0:64, :].rearrange("p (n t) -> p n t", t=256), axis=AX.X, op=ALU.add),
                          reads=[bkk], writes=[b_kmT])
                    Sd.op("dve", lambda e: e.tensor_copy(out=kmb[:, 0:nblk], in_=kmT[:, 0:nblk]), reads=[b_kmT], writes=[b_kmb])
                    Sd.op("dve", lambda e: e.tensor_tensor(out=kmb[:, 16:16 + nblk], in0=kmT[:, 0:nblk], in1=kmb[:, 0:nblk], op=ALU.subtract),
                          reads=[b_kmT, b_kmb], writes=[b_kmb])
                steps.append(kmean)
                for i0 in range(0, NT, 8):
                    def f(i0=i0):
                        for i in range(i0, min(i0 + 8, NT)):
                            Sd.op("pe", lambda e, i=i: e.matmul(pb[:, i * 16:i * 16 + nblk], lhsT=q_t[0:64, i * 128:(i + 1) * 128], rhs=kmb[:, 0:nblk],
                                                                start=True, stop=False), reads=[bq, b_kmb], writes=[pbb])
                            Sd.op("pe", lambda e, i=i: e.matmul(pb[:, i * 16:i * 16 + nblk], lhsT=q_t[0:64, i * 128:(i + 1) * 128], rhs=kmb[:, 16:16 + nblk],
                                                                start=False, stop=True), reads=[bq, b_kmb], writes=[pbb])
                    steps.append(f)

                def tk0():
                    if nblk < 16:
                        Sd.op("dve", lambda e: e.memset(G0[:], 0.0), writes=[b_G0])
                    Sd.op("dve", lambda e: e.tensor_tensor(out=G0[:, :, 0:nblk], in0=pb[:, 0:NT * 16].rearrange("p (i n) -> p i n", n=16)[:, :, 0:nblk],
                                                           in1=ownneg[:].rearrange("p (i n) -> p i n", n=16)[:, :, 0:nblk], op=ALU.add),
                          reads=[pbb, b_ownneg], writes=[b_G0])
                    if nblk < 16:
                        Sd.op("dve", lambda e: e.memset(G0[:, :, nblk:16], -3e9), writes=[b_G0])
                steps.append(tk0)
                for r in range(2):
                    def f(r=r):
                        src, bsrc = (G0, b_G0) if r == 0 else (G1, b_G1)
                        Sd.op("dve", lambda e: e.tensor_reduce(out=mx[:], in_=src[:], axis=AX.X, op=ALU.max), reads=[bsrc], writes=[b_mx])
                        Sd.op("dve", lambda e: e.tensor_tensor(out=EQ[:], in0=src[:], in1=mx[:].unsqueeze(2).to_broadcast([128, NT, 16]),
                                                               op=ALU.is_equal), reads=[bsrc, b_mx], writes=[b_EQ])
                        Sd.op("dve", lambda e: e.scalar_tensor_tensor(out=G1[:].rearrange("p i n -> p (i n)"), in0=EQ[:].rearrange("p i n -> p (i n)"),
                                                                      scalar=-2e9, in1=src[:].rearrange("p i n -> p (i n)"),
                                                                      op0=ALU.mult, op1=ALU.add),
                              reads=[b_EQ, bsrc], writes=[b_G1])
                    steps.append(f)

                def tk3():
                    Sd.op("dve", lambda e: e.tensor_reduce(out=mx[:], in_=G1[:], axis=AX.X, op=ALU.max), reads=[b_G1], writes=[b_mx])
                    Sd.op("dve", lambda e: e.tensor_tensor(out=EQ[:], in0=G0[:], in1=mx[:].unsqueeze(2).to_broadcast([128, NT, 16]), op=ALU.is_ge),
                          reads=[b_G0, b_mx], writes=[b_EQ])
                    Sd.op("dve", lambda e: e.scalar_tensor_tensor(out=MB[:, :, 0:16], in0=EQ[:], scalar=-1.0,
                                                                  in1=own0[:].rearrange("p (i n) -> p i n", n=16), op0=ALU.add, op1=ALU.mult),
                          reads=[b_EQ, b_own0], writes=[b_MB])
                steps.append(tk3)
                for I in range(NG):
                    def f(I=I):
                        for q in range(4):
                            i = 4 * I + q
                            Sd.op("pe", lambda e, i=i, q=q: e.transpose(out=pb[0:32, q * 128:(q + 1) * 128], in_=MB[:, i, :], identity=ident[:]),
                                  reads=[b_MB, b_ident], writes=[pbb])
                        Sd.op("act", lambda e: e.copy(out=q_t[64:96, I * 512:(I + 1) * 512], in_=pb[0:32, :]), reads=[pbb], writes=[bq])
                    steps.append(f)
                return steps

            items = [(h, I, j) for h in range(8) for I in range(NG) for j in range(4 * I + 4)]
            nper = len(items) // 8
            ic = {}
            gc = {}
            deferred = []

            def stageA(idx):
                h, I, j = items[idx]
                q_t, bq, k_t, bkk, v_t, bv = hc[h]
                m = j - 4 * I
                c0 = max(m, 0) * 128
                sb_, sbb = wrot.next()
                if m < 0:
                    Sd.op("pe", lambda e: e.matmul(sb_[:, 0:512], lhsT=k_t[:, j * 128:(j + 1) * 128],
                                                   rhs=q_t[:, I * 512:(I + 1) * 512], start=True, stop=True),
                          reads=[bkk, bq], writes=[sbb])
                else:
                    Sd.op("pe", lambda e: e.matmul(sb_[:, c0:c0 + 128], lhsT=k_t[:, j * 128:(j + 1) * 128],
                                                   rhs=q_t[:, I * 512 + c0:I * 512 + c0 + 128], start=True, stop=False),
                          reads=[bkk, bq], writes=[sbb])
                    Sd.op("pe", lambda e: e.matmul(sb_[:, c0:c0 + 128], lhsT=identb[:], rhs=trib[:], start=False, stop=True),
                          reads=[b_identb, b_trib], writes=[sbb])
                    if c0 + 128 < 512:
                        Sd.op("pe", lambda e: e.matmul(sb_[:, c0 + 128:512], lhsT=k_t[:, j * 128:(j + 1) * 128],
                                                       rhs=q_t[:, I * 512 + c0 + 128:(I + 1) * 512], start=True, stop=True),
                              reads=[bkk, bq], writes=[sbb])
                p_t, bp = PT.next()
                if kind == "fox":
                    Sd.op("act", lambda e: e.activation(out=p_t[:, c0:512], in_=sb_[:, c0:512], func=AF.Exp, bias=fb[:, I, j, h:h + 1]),
                          reads=[sbb, b_fb], writes=[bp])
                else:
                    Sd.op("act", lambda e: e.activation(out=p_t[:, c0:512], in_=sb_[:, c0:512], func=AF.Exp), reads=[sbb], writes=[bp])
                ic[idx] = (p_t, bp, c0)

            def stageB(idx):
                h, I, j = items[idx]
                q_t, bq, k_t, bkk, v_t, bv = hc[h]
                nj = 4 * I + 4
                if j == 0:
                    gc[(h, I)] = accrot.next()
                ob, obb = gc[(h, I)]
                p_t, bp, c0 = ic.pop(idx)
                Sd.op("pe", lambda e: e.matmul(ob[0:65, c0:512], lhsT=v_t[:, j, :], rhs=p_t[:, c0:512],
                                               start=(j == 0), stop=(j == nj - 1)),
                      reads=[bv, bp], writes=[obb])
                if j == nj - 1:
                    o_t, bo = osb.next()
                    r_t, br = rsb.next()
                    Sd.op("act", lambda e: e.copy(out=o_t[:, :], in_=ob[0:65, :]), reads=[obb], writes=[bo])
                    Sd.op("dve", lambda e: e.reciprocal(out=r_t[64:65, :], in_=o_t[64:65, :]), reads=[bo], writes=[br])

                    h_t, bh_ = rhl.next()
                    Sd.op("dve", lambda e: e.tensor_copy(out=h_t[64:65, :], in_=r_t[64:65, :]), reads=[br], writes=[bh_])
                    Sd.op("dve", lambda e: e.tensor_tensor(out=h_t[96:97, :], in0=r_t[64:65, :], in1=h_t[64:65, :], op=ALU.subtract),
                          reads=[br, bh_], writes=[bh_])

                    def E2():
                        b2, b2b = wrot.next()
                        Sd.op("pe", lambda e: e.matmul(b2[0:64, :], lhsT=onesb[64:97, 0:64], rhs=h_t[64:97, :], start=True, stop=True),
                              reads=[b_onesb, bh_], writes=[b2b])
                        y_t, by = yst.next()
                        Sd.op("dve", lambda e: e.tensor_tensor(out=y_t[:, :], in0=o_t[0:64, :], in1=b2[0:64, :], op=ALU.mult), reads=[bo, b2b], writes=[by])
                        Sd.dma("sp", yT[ybase + h // 2, (h % 2) * 64:(h % 2) * 64 + 64, I * 512:(I + 1) * 512], y_t[:, :], reads=[by])
                    deferred.append([4, E2])

            LA = 3
            for f in prolog_steps(0):
                f()
            n = len(items)
            for idx in range(min(LA, n)):
                stageA(idx)
            pend = []
            for idx in range(n):
                h = items[idx][0]
                loc = idx - h * nper
                if h + 1 < 8:
                    if loc == 0:
                        pend = prolog_steps(h + 1)
                        pend.pop(0)()
                    elif pend and (nper < 60 or (loc >= 8 and loc % 2 == 0) or loc >= nper - LA - 3):
                        pend.pop(0)()
                        while pend and loc >= nper - LA - 3:
                            pend.pop(0)()
                if idx + LA < n:
                    stageA(idx + LA)
                stageB(idx)
                for d in list(deferred):
                    d[0] -= 1
                    if d[0] <= 0:
                        d[1]()
                        deferred.remove(d)
            for d in deferred:
                d[1]()

        def fox(l):
            with ExitStack() as st:
                f_t, bf_ = alloc(st, "fraw", [8, S], F32)
                nfb, b_nfb = alloc(st, "nfb", [8, 1], F32)
                Sd.dma("sp", f_t[:], fT, writes=[bf_])
                Sd.dma("sp", nfb[:], fox_f_bias[l].rearrange("(h o) -> h o", o=1), writes=[b_nfb])
                Sd.op("dve", lambda e: e.tensor_scalar(out=nfb[:], in0=nfb[:], scalar1=-1.0, scalar2=None, op0=ALU.mult), reads=[b_nfb], writes=[b_nfb])
                Sd.op("act", lambda e: e.activation(out=f_t[:], in_=f_t[:], func=AF.Exp, bias=nfb[:, 0:1], scale=-1.0), reads=[bf_, b_nfb], writes=[bf_])
                Sd.op("act", lambda e: e.activation(out=f_t[:], in_=f_t[:], func=AF.Ln, bias=1.0), reads=[bf_], writes=[bf_])
                Sd.op("dve", lambda e: e.tensor_scalar(out=f_t[:], in0=f_t[:], scalar1=-1.0, scalar2=None, op0=ALU.mult), reads=[bf_], writes=[bf_])
                AH, b_AH, fb, b_fb = decay_prep(st, f_t, bf_, 8, "f")
                attn_core(st, "fox", l, AH, b_AH, fb, b_fb)
                Sd.barrier()

        def moba(l):
            with ExitStack() as st:
                attn_core(st, "moba", l)
                Sd.barrier()

        def ssd(l):
            with ExitStack() as st:
                AHp, b_AHp = alloc(st, "sAHp", [64, S], BF16)
                fbp, b_fbp = alloc(st, "sfbp", [128, NG, NT, 12], F32)
                dtk, b_dtk = alloc(st, "dtk", [128, NT, 12], F32)
                with ExitStack() as st2:
                    dT, b_dT = alloc(st2, "dT", [12, S], F32)
                    dtb, b_dtb = alloc(st2, "dtb", [12, 1], F32)
                    alg, b_alg = alloc(st2, "alg", [12, 1], F32)
                    Sd.dma("sp", dT[:], dtT, writes=[b_dT])
                    Sd.dma("sp", dtb[:], dt_bias[l].rearrange("(h o) -> h o", o=1), writes=[b_dtb])
                    Sd.dma("sp", alg[:], a_log[l].rearrange("(h o) -> h o", o=1), writes=[b_alg])
                    Sd.op("act", lambda e: e.activation(out=alg[:], in_=alg[:], func=AF.Exp), reads=[b_alg], writes=[b_alg])
                    Sd.op("act", lambda e: e.activation(out=dT[:], in_=dT[:], func=AF.Exp, bias=dtb[:, 0:1]), reads=[b_dT, b_dtb], writes=[b_dT])
                    Sd.op("act", lambda e: e.activation(out=dT[:], in_=dT[:], func=AF.Ln, bias=1.0), reads=[b_dT], writes=[b_dT])
                    Sd.op("dve", lambda e: e.tensor_scalar(out=dT[:], in0=dT[:], scalar1=alg[:, 0:1], scalar2=-1.0, op0=ALU.mult, op1=ALU.mult),
                          reads=[b_dT, b_alg], writes=[b_dT])
                    AH0, b_AH0, fb0, b_fb0 = decay_prep(st2, dT, b_dT, 12, "s")
                    Sd.op("pool", lambda e: e.tensor_copy(out=AHp[:], in_=AH0[:]), reads=[b_AH0], writes=[b_AHp])
                    Sd.op("pool", lambda e: e.tensor_copy(out=fbp[:].rearrange("p a b c -> p (a b c)"), in_=fb0[:].rearrange("p a b c -> p (a b c)")),
                          reads=[b_fb0], writes=[b_fbp])
                    dtbb, b_dtbb = alloc(st2, "dtbb", [128, 12], F32)
                    Sd.dma("sp", dtk[:], dtm.rearrange("(i p) h -> p i h", p=128), writes=[b_dtk])
                    Sd.dma("sp", dtbb[:], dt_bias[l].partition_broadcast(128), writes=[b_dtbb])
                    Sd.op("dve", lambda e: e.tensor_tensor(out=dtk[:], in0=dtk[:], in1=dtbb[:].unsqueeze(1).to_broadcast([128, NT, 12]), op=ALU.add),
                          reads=[b_dtk, b_dtbb], writes=[b_dtk])
                    Sd.op("act", lambda e: e.activation(out=dtk[:], in_=dtk[:], func=AF.Exp), reads=[b_dtk], writes=[b_dtk])
                    Sd.op("act", lambda e: e.activation(out=dtk[:], in_=dtk[:], func=AF.Ln, bias=1.0), reads=[b_dtk], writes=[b_dtk])
                    Sd.barrier()
                AH, b_AH, fb, b_fb = AHp, b_AHp, fbp, b_fbp
                BC, b_BC = alloc(st, "BC", [128, 4, S], BF16)
                xdtm, b_xdtm = alloc(st, "xdtm", [128, NT, 768], BF16)
                with ExitStack() as st2:
                    cw, b_cw = alloc(st2, "cw", [128, 10, 4], F32)
                    cb, b_cb = alloc(st2, "cb", [128, 10], F32)
                    Sd.dma("sp", cw[:], conv_wT[l], writes=[b_cw])
                    Sd.dma("sp", cb[:], conv_b[l], writes=[b_cb])
                    xin = Rot([alloc(st2, f"cxi{i}", [128, S], F32) for i in range(1)])
                    acc = Rot([alloc(st2, f"cac{i}", [128, S], F32) for i in range(1)])
                    xsf, b_xsf = alloc(st2, "xsf", [128, S], F32)
                    xsb = Rot([alloc(st2, f"xsb{i}", [128, S], BF16) for i in range(1)])
                    for c in range(10):
                        xi, bxi = xin.next()
                        ac, bac = acc.next()
                        Sd.dma("sp", xi[:], xbcT[c], writes=[bxi])
                        Sd.op("dve", lambda e, c=c: e.tensor_scalar(out=ac[:], in0=xi[:], scalar1=cw[:, c, 3:4], scalar2=None, op0=ALU.mult),
                              reads=[bxi, b_cw], writes=[bac])
                        for sh in (1, 2, 3):
                            Sd.op("dve", lambda e, c=c, sh=sh: e.scalar_tensor_tensor(out=ac[:, sh:S], in0=xi[:, 0:S - sh], scalar=cw[:, c, 3 - sh:4 - sh],
                                                                                     in1=ac[:, sh:S], op0=ALU.mult, op1=ALU.add),
                                  reads=[bxi, b_cw, bac], writes=[bac])
                        if c < 6:
                            Sd.op("act", lambda e, c=c: e.activation(out=xsf[:], in_=ac[:], func=AF.Silu, bias=cb[:, c:c + 1]), reads=[bac, b_cb], writes=[b_xsf])
                            xb_, bxb_ = xsb.next()
                            Sd.op("pool", lambda e: e.tensor_copy(out=xb_[:], in_=xsf[:]), reads=[b_xsf], writes=[bxb_])
                            Sd.dma("sp", xsTd[c], xb_[:], reads=[bxb_])
                            for I in range(NG):
                                bk, bb = bankrot.next()
                                for q in range(4):
                                    i = 4 * I + q
                                    Sd.op("pe", lambda e, i=i, q=q: e.transpose(out=bk[:, q * 128:(q + 1) * 128], in_=xsf[:, i * 128:(i + 1) * 128], identity=ident[:]),
                                          reads=[b_xsf, b_ident], writes=[bb])
                                Sd.op("dve", lambda e, I=I, c=c: e.tensor_tensor(
                                    out=xdtm[:, 4 * I:4 * I + 4, c * 128:(c + 1) * 128].rearrange("p i (h d) -> p i h d", h=2),
                                    in0=bk[:, :].rearrange("p (i h d) -> p i h d", i=4, h=2),
                                    in1=dtk[:, 4 * I:4 * I + 4, 2 * c:2 * c + 2].unsqueeze(3).to_broadcast([128, 4, 2, 64]), op=ALU.mult),
                                    reads=[bb, b_dtk], writes=[b_xdtm])
                        else:
                            Sd.op("act", lambda e, c=c: e.activation(out=BC[:, c - 6, :], in_=ac[:], func=AF.Silu, bias=cb[:, c:c + 1]), reads=[bac, b_cb], writes=[b_BC])
                    Sd.barrier()
                selb, b_selb = alloc(st, "selb", [64, 12 * 128], BF16)
                trib4, b_trib4 = alloc(st, "trib4", [128, 512], F32)
                Sd.dma("pool", selb[:], c_selb, writes=[b_selb])
                for q in range(4):
                    Sd.dma("sp", trib4[:, q * 128:(q + 1) * 128], c_tri, writes=[b_trib4])
                m01b, b_m01b = alloc(st, "m01b", [128, 128], BF16)
                Sd.dma("pool", m01b[:], c_m01, writes=[b_m01b])
                ARs = [alloc(st, f"AR_{hh}", [128, 512], F32) for hh in range(6)]
                ARDs = [alloc(st, f"ARD_{hh}", [128, 512], F32) for hh in range(6)]
                Gs = Rot([alloc(st, f"Gs{i}", [128, 512], BF16) for i in range(3)])
                DT_ = Rot([alloc(st, f"DTt{i}", [128, 512], F32) for i in range(3)])
                WT = Rot([alloc(st, f"WT{i}", [128, 512], BF16) for i in range(4)])
                yo = Rot([alloc(st, f"syo{i}", [128, 512], F32) for i in range(2)])
                yoff = [alloc(st, f"yoff{i}", [128, 512], F32) for i in range(3)]
                xsc = Rot([alloc(st, f"xsc{i}", [128, 6, 64], BF16) for i in range(3)])
                Vb = Rot([alloc(st, f"Vb{i}", [128, 512], F32) for i in range(2)])
                U, b_U = alloc(st, "Utab", [128, NG, NT, 12], F32)
                Sd.op("act", lambda e: e.activation(out=U[:].rearrange("p a b c -> p (a b c)"), in_=fb[:].rearrange("p a b c -> p (a b c)"), func=AF.Exp),
                      reads=[b_fb], writes=[b_U])
                accb = banks[0:3]
                gbank = Rot(banks[3:6])
                misc = Rot(banks[6:8])
                groups = [(g, I) for g in range(2) for I in range(NG)]

                def ar_setup(gi):
                    g, I = groups[gi]
                    for hh in range(6):
                        h = 6 * g + hh
                        bk, bb = misc.next()
                        Sd.op("pe", lambda e: e.matmul(bk[:, :], lhsT=selb[0:64, h * 128:(h + 1) * 128], rhs=AH[0:64, I * 512:(I + 1) * 512],
                                                       start=True, stop=True), reads=[b_selb, b_AH], writes=[bb])
                        ar, bar = ARs[hh]
                        ard, bard = ARDs[hh]
                        Sd.op("dve", lambda e: e.tensor_copy(out=ar[:], in_=bk[:, :]), reads=[bb], writes=[bar])
                        Sd.op("pool", lambda e: e.tensor_tensor(out=ard[:], in0=ar[:], in1=trib4[:], op=ALU.add), reads=[bar, b_trib4], writes=[bard])

                for gi, (g, I) in enumerate(groups):
                    ar_setup(gi)
                    gst = {}

                    def emitG(j):
                        m = j - 4 * I
                        c0 = max(m, 0) * 128
                        gb, gbb = gbank.next()
                        Sd.op("pe", lambda e: e.matmul(gb[:, c0:512], lhsT=BC[:, g, j * 128:(j + 1) * 128],
                                                       rhs=BC[:, 2 + g, I * 512 + c0:(I + 1) * 512], start=True, stop=True),
                              reads=[b_BC], writes=[gbb])
                        gst[j] = (gb, gbb, c0, m)

                    def evacG(j):
                        gb, gbb, c0, m = gst[j]
                        gs, bgs = Gs.next()
                        if m < 0:
                            Sd.op("act", lambda e: e.copy(out=gs[:, :], in_=gb[:, :]), reads=[gbb], writes=[bgs])
                        else:
                            Sd.op("dve", lambda e: e.tensor_tensor(out=gs[:, c0:c0 + 128], in0=gb[:, c0:c0 + 128], in1=m01b[:], op=ALU.mult),
                                  reads=[gbb, b_m01b], writes=[bgs])
                            if c0 + 128 < 512:
                                Sd.op("act", lambda e: e.copy(out=gs[:, c0 + 128:512], in_=gb[:, c0 + 128:512]), reads=[gbb], writes=[bgs])
                        gst[j] = (gs, bgs, c0, m)

                    noff = 4 * I
                    if noff > 0:
                        emitG(0); evacG(0)
                        if noff > 1:
                            emitG(1); evacG(1)
                        for j in range(noff):
                            if j + 2 < noff:
                                emitG(j + 2)
                            gs, bgs, c0, m = gst[j]
                            x_t, bxs = xsc.next()
                            Sd.op("dve", lambda e: e.tensor_tensor(out=x_t[:], in0=xdtm[:, j, g * 384:(g + 1) * 384].rearrange("p (h d) -> p h d", h=6),
                                                                   in1=U[:, I, j, 6 * g:6 * g + 6].unsqueeze(2).to_broadcast([128, 6, 64]), op=ALU.mult),
                                  reads=[b_xdtm, b_U], writes=[bxs])
                            for b3 in range(3):
                                ob, obb = accb[b3]
                                Sd.op("pe", lambda e: e.matmul(ob[:, :], lhsT=x_t[:, 2 * b3:2 * b3 + 2, :].rearrange("p h d -> p (h d)"), rhs=gs[:, :],
                                                               start=(j == 0), stop=(j == noff - 1)),
                                      reads=[bxs, bgs], writes=[obb])
                            if j + 2 < noff:
                                evacG(j + 2)
                        for b3 in range(3):
                            ob, obb = accb[b3]
                            yf, byf = yoff[b3]
                            Sd.op("act", lambda e: e.copy(out=yf[:, :], in_=ob[:, :]), reads=[obb], writes=[byf])
                    j0 = 4 * I
                    nj = 4 * I + 4
                    emitG(j0); evacG(j0)
                    emitG(j0 + 1); evacG(j0 + 1)
                    for j in range(j0, nj):
                        if j + 2 < nj:
                            emitG(j + 2)
                        gs, bgs, c0, m = gst[j]
                        for hh in range(6):
                            h = 6 * g + hh
                            ar, bar = ARs[hh]
                            ard, bard = ARDs[hh]
                            d_t, bd = DT_.next()
                            Sd.op("act", lambda e: e.activation(out=d_t[:, c0:c0 + 128], in_=ard[:, c0:c0 + 128], func=AF.Exp, bias=fb[:, I, j, h:h + 1]),
                                  reads=[bard, b_fb], writes=[bd])
                            if c0 + 128 < 512:
                                Sd.op("act", lambda e: e.activation(out=d_t[:, c0 + 128:512], in_=ar[:, c0 + 128:512], func=AF.Exp, bias=fb[:, I, j, h:h + 1]),
                                      reads=[bar, b_fb], writes=[bd])
                            w_t, bw = WT.next()
                            Sd.op("dve", lambda e: e.tensor_tensor(out=w_t[:, c0:512], in0=gs[:, c0:512], in1=d_t[:, c0:512], op=ALU.mult),
                                  reads=[bgs, bd], writes=[bw])
                            ob, obb = accb[hh // 2]
                            po = (hh % 2) * 64
                            Sd.op("pe", lambda e: e.matmul(ob[po:po + 64, c0:512], lhsT=xdtm[:, j, h * 64:(h + 1) * 64], rhs=w_t[:, c0:512],
                                                           start=(j == j0), stop=(j == nj - 1)),
                                  reads=[b_xdtm, bw], writes=[obb])
                        if j + 2 < nj:
                            evacG(j + 2)
                    for b3 in range(3):
                        ob, obb = accb[b3]
                        y_t, by = yo.next()
                        if noff == 0:
                            Sd.op("act", lambda e: e.copy(out=y_t[:, :], in_=ob[:, :]), reads=[obb], writes=[by])
                        else:
                            v_t, bvb = Vb.next()
                            for hf in range(2):
                                ar, bar = ARs[2 * b3 + hf]
                                Sd.op("act", lambda e: e.activation(out=v_t[hf * 64:(hf + 1) * 64, :], in_=ar[hf * 64:(hf + 1) * 64, :], func=AF.Exp),
                                      reads=[bar], writes=[bvb])
                            yf, byf = yoff[b3]
                            Sd.op("pool", lambda e: e.tensor_tensor(out=v_t[:, :], in0=v_t[:, :], in1=yf[:, :], op=ALU.mult), reads=[bvb, byf], writes=[bvb])
                            Sd.op("dve", lambda e: e.tensor_tensor(out=y_t[:, :], in0=ob[:, :], in1=v_t[:, :], op=ALU.add), reads=[obb, bvb], writes=[by])
                        c = 3 * g + b3
                        Sd.dma("sp", ysT[2 * c:2 * c + 2, :, I * 512:(I + 1) * 512].rearrange("h d t -> (h d) t"), y_t[:, :], reads=[by])
                Sd.barrier()

        def ssd_pass2(l):
            with ExitStack() as st:
                xsl = Rot([alloc(st, f"xsl{i}", [128, 512], BF16) for i in range(6)])
                dcol, b_dcol = alloc(st, "dcol", [128, 6], F32)
                nwc, b_nwc = alloc(st, "nwc", [128, 6], F32)
                Sd.dma("sp", dcol[:], ssm_d_rep[l], writes=[b_dcol])
                Sd.dma("sp", nwc[:], ssm_norm_w[l], writes=[b_nwc])
                ysl = Rot([alloc(st, f"ysl{i}", [128, 512], F32) for i in range(6)])
                szl = Rot([alloc(st, f"szl{i}", [128, 512], BF16) for i in range(6)])
                y1 = Rot([alloc(st, f"y1{i}", [128, 512], F32) for i in range(2)])
                y2 = [alloc(st, f"y2{i}", [128, 512], F32) for i in range(6)]
                sq = Rot([alloc(st, f"sq{i}", [128, 512], F32) for i in range(2)])
                rs = Rot([alloc(st, f"rs{i}", [128, 512], F32) for i in range(2)])
                y3 = Rot([alloc(st, f"y3{i}", [128, 512], BF16) for i in range(2)])
                ld = {}

                def ploads(u):
                    I, g = divmod(u, 2)
                    cs = slice(I * 512, (I + 1) * 512)
                    for cc in range(3):
                        c = 3 * g + cc
                        ys_t, bys = ysl.next()
                        xs_t, bxs = xsl.next()
                        sz_t, bsz = szl.next()
                        Sd.dma("sp", xs_t[:], xsTd[c, :, cs], writes=[bxs])
                        Sd.dma("sp", ys_t[:], ysT[2 * c:2 * c + 2, :, cs].rearrange("h d t -> (h d) t"), writes=[bys])
                        Sd.dma("sp", sz_t[:], szT[c, :, cs], writes=[bsz])
                        ld[(u, cc)] = (ys_t, bys, xs_t, bxs, sz_t, bsz)
                ploads(0)
                for I in range(NG):
                    cs = slice(I * 512, (I + 1) * 512)
                    for g in range(2):
                        u = 2 * I + g
                        if u + 1 < 2 * NG:
                            ploads(u + 1)
                        sb_, sbb = accrot.next()
                        for cc in range(3):
                            c = 3 * g + cc
                            ys_t, bys, xs_t, bxs, sz_t, bsz = ld.pop((u, cc))
                            a_t, ba = y1.next()
                            Sd.op("dve", lambda e, c=c: e.scalar_tensor_tensor(out=a_t[:], in0=xs_t[:], scalar=dcol[:, c:c + 1], in1=ys_t[:],
                                                                               op0=ALU.mult, op1=ALU.add), reads=[bxs, b_dcol, bys], writes=[ba])
                            y2t, by2 = y2[c]
                            Sd.op("pool", lambda e: e.tensor_tensor(out=y2t[:], in0=a_t[:], in1=sz_t[:], op=ALU.mult), reads=[ba, bsz], writes=[by2])
                            q_, bq_ = sq.next()
                            Sd.op("act", lambda e: e.activation(out=q_[:], in_=y2t[:], func=AF.Square), reads=[by2], writes=[bq_])
                            Sd.op("pe", lambda e, cc=cc: e.matmul(sb_[:, :], lhsT=ones32[:], rhs=q_[:], start=(cc == 0), stop=(cc == 2)),
                                  reads=[b_ones32, bq_], writes=[sbb])
                        r_t, br = rs.next()
                        Sd.op("dve", lambda e: e.tensor_scalar(out=r_t[:], in0=sb_[:, :], scalar1=1.0 / 384, scalar2=EPS, op0=ALU.mult, op1=ALU.add),
                              reads=[sbb], writes=[br])
                        Sd.op("act", lambda e: e.activation(out=r_t[:], in_=r_t[:], func=AF.Sqrt), reads=[br], writes=[br])
                        Sd.op("dve", lambda e: e.reciprocal(out=r_t[:], in_=r_t[:]), reads=[br], writes=[br])
                        for cc in range(3):
                            c = 3 * g + cc
                            y2t, by2 = y2[c]
                            o_t, bo = y3.next()
                            Sd.op("dve", lambda e, c=c: e.scalar_tensor_tensor(out=o_t[:], in0=y2t[:], scalar=nwc[:, c:c + 1], in1=r_t[:],
                                                                               op0=ALU.mult, op1=ALU.mult), reads=[by2, b_nwc, br], writes=[bo])
                            Sd.dma("sp", yT[12 + c, :, cs], o_t[:], reads=[bo])
                Sd.barrier()

        def resid_evac(st_rot, bk, bb, gofs, xt, bx, cg):
            t_t, bt = st_rot.next()
            Sd.op("dve", lambda e: e.tensor_tensor(out=t_t[:], in0=bk[:, :], in1=modb[:, gofs + cg * 512:gofs + (cg + 1) * 512], op=ALU.mult),
                  reads=[bb, b_modb], writes=[bt])
            Sd.op("pool", lambda e: e.tensor_tensor(out=xt[:, cg * 512:(cg + 1) * 512], in0=xt[:, cg * 512:(cg + 1) * 512], in1=t_t[:], op=ALU.add),
                  reads=[bt, bx], writes=[bx])

        def merge(l):
            with ExitStack() as st:
                wbr, b_wbr = alloc(st, "wbr", [128, 18, D], BF16)
                wo, b_wo = alloc(st, "wo", [128, KC, D], BF16)
                for k0 in range(0, 18, 6):
                    Sd.dma("pool", wbr[:, k0:k0 + 6, :], w_br[l][k0 * 128:(k0 + 6) * 128, :].rearrange("(kc p) n -> p kc n", p=128), writes=[b_wbr])
                Sd.dma("pool", wo[:], w_out[l].rearrange("(kc p) n -> p kc n", p=128), writes=[b_wo])
                ssd_pass2(l)
                yt = Rot([alloc(st, f"myt{i}", [128, 18, 512], BF16) for i in range(2)])
                gt = Rot([alloc(st, f"mgt{i}", [128, 4, 512], BF16) for i in range(3)])
                mt = Rot([alloc(st, f"mmt{i}", [128, 512], F32) for i in range(2)])
                tt = Rot([alloc(st, f"mtt{i}", [128, 512], F32) for i in range(3)])
                mg = Rot([alloc(st, f"mmg{i}", [128, KC, 512], BF16) for i in range(2)])
                pcp = Rot([alloc(st, f"mpc{i}", [128, 512], F32) for i in range(4)])
                xt = Rot([alloc(st, f"mxt{i}", [128, D], F32) for i in range(8)])
                rr = Rot([alloc(st, f"mrr{i}", [128, 512], F32) for i in range(2)])
                kranges = [(0, 4), (4, 8), (8, 12), (12, 18)]
                ld = {}
                gld = {}

                def mloads(tg):
                    cs = slice(tg * 512, (tg + 1) * 512)
                    y_t, by = yt.next()
                    for k0 in range(0, 18, 6):
                        Sd.dma("sp", y_t[:, k0:k0 + 6, :], yT[k0:k0 + 6, :, cs].rearrange("c p t -> p c t"), writes=[by])
                    xl = []
                    for q in range(4):
                        i = 4 * tg + q
                        x_t, bx = xt.next()
                        Sd.dma("sp", x_t[:], xs[i * 128:(i + 1) * 128, :], writes=[bx])
                        xl.append((x_t, bx))
                    ld[tg] = (y_t, by, xl)

                def gloads(k):
                    tg, c = divmod(k, 8)
                    cs = slice(tg * 512, (tg + 1) * 512)
                    g_t, bg = gt.next()
                    Sd.dma("sp", g_t[:], gT[:, :, cs].rearrange("(i c) p t -> c p i t", c=8)[c], writes=[bg])
                    gld[k] = (g_t, bg)
                mloads(0)
                gloads(0); gloads(1)
                for tg in range(NG):
                    cs = slice(tg * 512, (tg + 1) * 512)
                    if tg + 1 < NG:
                        mloads(tg + 1)
                    y_t, by, xl = ld.pop(tg)
                    m_g, bmg = mg.next()
                    for c in range(8):
                        if tg * 8 + c + 2 < NG * 8:
                            gloads(tg * 8 + c + 2)
                        g_t, bg = gld.pop(tg * 8 + c)
                        m_t, bm = mt.next()
                        for i in range(4):
                            bk, bb = bankrot.next()
                            k0, k1 = kranges[i]
                            for kc in range(k0, k1):
                                Sd.op("pe", lambda e, kc=kc: e.matmul(bk[:, :], lhsT=wbr[:, kc, c * 128:(c + 1) * 128], rhs=y_t[:, kc, :],
                                                                       start=(kc == k0), stop=(kc == k1 - 1)),
                                      reads=[b_wbr, by], writes=[bb])
                            pc, bpc = pcp.next()
                            Sd.op("act", lambda e: e.copy(out=pc[:], in_=bk[:, :]), reads=[bb], writes=[bpc])
                            if i == 0:
                                Sd.op("dve", lambda e, i=i: e.tensor_tensor(out=m_t[:], in0=pc[:], in1=g_t[:, i, :], op=ALU.mult), reads=[bpc, bg], writes=[bm])
                            else:
                                t_t, bt = tt.next()
                                Sd.op("dve", lambda e, i=i: e.tensor_tensor(out=t_t[:], in0=pc[:], in1=g_t[:, i, :], op=ALU.mult), reads=[bpc, bg], writes=[bt])
                                if i == 1:
                                    Sd.op("dve", lambda e: e.tensor_tensor(out=m_t[:], in0=m_t[:], in1=t_t[:], op=ALU.add), reads=[bm, bt], writes=[bm])
                                elif i == 2:
                                    Sd.op("pool", lambda e: e.tensor_tensor(out=m_t[:], in0=m_t[:], in1=t_t[:], op=ALU.add), reads=[bm, bt], writes=[bm])
                                else:
                                    Sd.op("pool", lambda e: e.tensor_tensor(out=m_g[:, c, :], in0=m_t[:], in1=t_t[:], op=ALU.add), reads=[bm, bt], writes=[bmg])
                    for q in range(4):
                        i = 4 * tg + q
                        x_t, bx = xl[q]
                        for cg in range(2):
                            bk, bb = bankrot.next()
                            for kc in range(KC):
                                Sd.op("pe", lambda e, kc=kc: e.matmul(bk[:, :], lhsT=m_g[:, kc, q * 128:(q + 1) * 128], rhs=wo[:, kc, cg * 512:(cg + 1) * 512],
                                                                       start=(kc == 0), stop=(kc == KC - 1)),
                                      reads=[bmg, b_wo], writes=[bb])
                            resid_evac(rr, bk, bb, 2 * D, x_t, bx, cg)
                        Sd.dma("sp", xs[i * 128:(i + 1) * 128, :], x_t[:], reads=[bx])
                Sd.barrier()

        def ffn(l):
          with ExitStack() as st0:
            w2t, b_w2 = alloc(st0, "w2t", [128, 32, D], BF16)
            for k0 in range(0, 32, 8):
                Sd.dma("pool", w2t[:, k0:k0 + 8, :], w2[l][k0 * 128:(k0 + 8) * 128, :].rearrange("(kc p) n -> p kc n", p=128), writes=[b_w2])
            with ExitStack() as st:
                hT, b_hT = alloc(st, "hT2", [128, KC, S], BF16)
                with ExitStack() as st2:
                    norm_stage(st2, wmod[:, D:2 * D], modb[:, 3 * D:4 * D], hT, b_hT)
                    Sd.barrier()
                with ExitStack() as st2:
                    r32 = Rot([alloc(st2, f"fr{i}", [128, 512], F32) for i in range(3)])
                    hb = Rot([alloc(st2, f"fh{i}", [128, 512], BF16) for i in range(3)])

                    def ev(bk, bb, rows, cc, tg):
                        r_t, br = r32.next()
                        Sd.op("act", lambda e: e.activation(out=r_t[:], in_=bk[:, :], func=AF.Relu), reads=[bb], writes=[br])
                        h_t, bh = hb.next()
                        Sd.op("dve", lambda e: e.tensor_tensor(out=h_t[:], in0=r_t[:], in1=r_t[:], op=ALU.mult), reads=[br], writes=[bh])
                        Sd.dma("sp", hidT[cc, :, tg * 512:(tg + 1) * 512], h_t[:], reads=[bh])
                    wts = Rot([alloc(st2, f"fpw{i}", [128, KC, 512], BF16) for i in range(2)])
                    proj_fm(wts, hT, b_hT, w1[l], 4 * D, ev)
                    Sd.barrier()
            with ExitStack() as st:
                ht = Rot([alloc(st, f"f2h{i}", [128, 32, 512], BF16) for i in range(2)])
                xt = Rot([alloc(st, f"f2x{i}", [128, D], F32) for i in range(8)])
                rr = Rot([alloc(st, f"f2r{i}", [128, 512], F32) for i in range(3)])
                ld = {}

                def wloads(tg):
                    h_t, bh = ht.next()
                    for k0 in range(0, 32, 8):
                        Sd.dma("sp", h_t[:, k0:k0 + 8, :], hidT[k0:k0 + 8, :, tg * 512:(tg + 1) * 512].rearrange("c p t -> p c t"), writes=[bh])
                    xl = []
                    for q in range(4):
                        i = 4 * tg + q
                        x_t, bx = xt.next()
                        Sd.dma("sp", x_t[:], xs[i * 128:(i + 1) * 128, :], writes=[bx])
                        xl.append((x_t, bx))
                    ld[tg] = (h_t, bh, xl)
                wloads(0)
                for tg in range(NG):
                    if tg + 1 < NG:
                        wloads(tg + 1)
                    h_t, bh, xl = ld.pop(tg)
                    for q in range(4):
                        i = 4 * tg + q
                        x_t, bx = xl[q]
                        for cg in range(2):
                            bk, bb = bankrot.next()
                            for kc in range(32):
                                Sd.op("pe", lambda e, kc=kc: e.matmul(bk[:, :], lhsT=h_t[:, kc, q * 128:(q + 1) * 128], rhs=w2t[:, kc, cg * 512:(cg + 1) * 512],
                                                                       start=(kc == 0), stop=(kc == 31)),
                                      reads=[bh, b_w2], writes=[bb])
                            resid_evac(rr, bk, bb, 5 * D, x_t, bx, cg)
                        Sd.dma("sp", xs[i * 128:(i + 1) * 128, :], x_t[:], reads=[bx])
                Sd.barrier()

        for l in range(depth):
            adaln(l)
            with ExitStack() as st:
                hT, b_hT = alloc(st, "hT", [128, KC, S], BF16)
                with ExitStack() as st2:
                    norm_stage(st2, wmod[:, 0:D], modb[:, 0:D], hT, b_hT)
                    Sd.barrier()
                with ExitStack() as st2:
                    mixer_proj(l, st2, hT, b_hT)
                    Sd.barrier()
            gmlp(l)
            fox(l)
            moba(l)
            ssd(l)
            merge(l)
            ffn(l)
        with ExitStack() as st:
            fw, b_fw = alloc(st, "fw", [128, D], F32)
            Sd.dma("sp", fw[:], final_norm_w.partition_broadcast(128), writes=[b_wmod])
            Sd.barrier()
            norm_stage(st, fw[:], None, None, None, final=True)
            Sd.barrier()
        Sd.finish()
        build.ninst = dict(Sd.ninst)
        build.nsem = Sd.nsem
    return nc


def _perm64():
    return np.array(list(range(0, 8)) + list(range(16, 40)) + list(range(8, 16)) + list(range(40, 64)))


def host_consts(S):
    NT = S // 128
    s = np.arange(128)[:, None]
    t = np.arange(128)[None, :]
    c = {}
    c["c_ident"] = np.eye(128, dtype=np.float32)
    c["c_tri"] = np.where(t >= s, 0.0, NEG).astype(np.float32)
    c["c_m01"] = (s <= t).astype(np.float32)
    half = 8
    inv_freq = (500000.0 ** (-np.arange(half, dtype=np.float32) / half)).astype(np.float32)
    ang = np.arange(S, dtype=np.float32)[None, :] * inv_freq[:, None]
    rope = np.zeros((2, 40, S), np.float32)
    for r0 in (0, 32):
        rope[0, r0:r0 + 8] = np.cos(ang)
        rope[1, r0:r0 + 8] = np.sin(ang)
    rope[0, 8:32] = 1.0
    c["c_rope"] = rope
    rot = np.zeros((64, 64), np.float32)
    for r in range(8):
        rot[32 + r, r] = -1.0
        rot[r, 32 + r] = 1.0
    c["c_rot"] = rot
    oh = np.zeros((32, S), np.float32)
    blk = np.arange(S) // 256
    for n in range(min(16, S // 256)):
        oh[n, blk == n] = 30000.0
    c["c_onehot"] = oh
    e0 = np.zeros((128, 128), np.float32); e0[0, :] = 1.0
    c["c_e0"] = e0
    sel = np.zeros((64, 12, 32), np.float32)
    for h in range(12):
        sel[h, h, 0] = 1.0
        sel[32 + h, h, 1] = 1.0
    c["c_sel"] = sel.reshape(64, 12 * 32)
    selb = np.zeros((64, 12, 128), np.float32)
    for h in range(12):
        selb[h, h, :] = 1.0
        selb[32 + h, h, :] = 1.0
    c["c_selb"] = selb.reshape(64, 12 * 128)
    ownneg = np.zeros((128, NT, 16), np.float32)
    own0 = np.ones((128, NT, 16), np.float32)
    for i in range(NT):
        own = i // 2
        ownneg[:, i, own:] = -1e9
        own0[:, i, own] = 0.0
    c["c_ownneg"] = ownneg.reshape(128, NT * 16)
    c["c_own0"] = own0.reshape(128, NT * 16)
    return c


def host_weights(inp, depth):
    f = lambda a: np.ascontiguousarray(np.asarray(a, dtype=np.float32))
    w = {}
    Ld = depth
    for k in ("ada_w", "ada_b", "norm_mix_w", "norm_mlp_w", "w_in", "gmlp_ln_w", "gmlp_ln_b", "gmlp_bs",
              "fox_f_bias", "w_out", "mlp_w1", "mlp_w2"):
        w[k] = f(inp[k][:Ld])
    w["final_norm_w"] = f(inp["final_norm_w"])
    w_in = np.asarray(inp["w_in"])[:Ld]
    perm = _perm64()
    cols = []
    for base in (C_MQ, C_MK):
        for h in range(8):
            cols.extend(base + h * 64 + perm)
    w["w_mqk"] = f(w_in[:, :, np.array(cols)])
    w["gmlp_wsT"] = f(np.transpose(np.asarray(inp["gmlp_ws"])[:Ld], (0, 1, 3, 2)))
    w["conv_wT"] = f(np.transpose(np.asarray(inp["ssm_conv_w"])[:Ld], (0, 2, 1)).reshape(Ld, 10, 128, 4).transpose(0, 2, 1, 3))
    w["conv_b"] = f(np.asarray(inp["ssm_conv_b"])[:Ld].reshape(Ld, 10, 128).transpose(0, 2, 1))
    w["dt_bias"] = f(inp["ssm_dt_bias"][:Ld])
    w["a_log"] = f(inp["ssm_a_log"][:Ld])
    w["ssm_d_rep"] = f(np.repeat(np.asarray(inp["ssm_d"])[:Ld], 64, axis=1).reshape(Ld, 6, 128).transpose(0, 2, 1))
    w["ssm_norm_w"] = f(np.asarray(inp["ssm_norm_w"])[:Ld].reshape(Ld, 6, 128).transpose(0, 2, 1))
    w["w_br"] = f(np.concatenate([np.asarray(inp[k])[:Ld] for k in ("w_branch_a", "w_branch_b", "w_branch_c", "w_branch_d")], axis=1))
    return w


_CACHE = {}


def run(inp, S, depth, ncores, dbg=False):
    key = (S, depth, dbg)
    if key not in _CACHE:
        _CACHE[key] = build(S, depth, dbg)
    nc = _CACHE[key]
    shared = host_weights(inp, depth)
    shared.update(host_consts(S))
    x = np.asarray(inp["x"], dtype=np.float32)
    c = np.asarray(inp["c"], dtype=np.float32)
    in_maps = []
    for b in range(ncores):
        m = dict(shared)
        m["x"] = np.ascontiguousarray(x[b])
        m["cT"] = np.ascontiguousarray(c[b].reshape(KC, 128).T)
        in_maps.append(m)
    res = run_bass_kernel_spmd(nc, in_maps, core_ids=list(range(ncores)))
    return res.results


def kernel(**inputs):
    B, S, _ = inputs["x"].shape
    depth = inputs["w_in"].shape[0]
    res = run(inputs, S, depth, B)
    return np.stack([np.asarray(r["out"], dtype=np.float32) for r in res], axis=0)
```

```python
import math
from contextlib import ExitStack

import numpy as np
import ml_dtypes
import concourse.bass as bass
import concourse.mybir as mybir
from concourse.bass_utils import run_bass_kernel_spmd

F32 = mybir.dt.float32
BF16 = mybir.dt.bfloat16
AF = mybir.ActivationFunctionType
ALU = mybir.AluOpType
AX = mybir.AxisListType

D = 1024
KC = 8
EPS = 1e-6
NEG = -30000.0
IN_COLS = 10260
C_U, C_V, C_FQ, C_FK, C_FV, C_FF = 0, 512, 1024, 1536, 2048, 2560
C_MQ, C_MK, C_MV = 2568, 3080, 3592
C_Z, C_XBC, C_DT, C_G = 4104, 4872, 6152, 6164


class Buf:
    __slots__ = ("name", "w", "r")

    def __init__(self, name=""):
        self.name = name
        self.w = None
        self.r = {}


class Sched:
    LIMIT = 30000
    NSLOT = 8

    def __init__(self, nc, stack):
        self.nc = nc
        self.stack = stack
        self.eng = dict(pe=nc.tensor, act=nc.scalar, dve=nc.vector, pool=nc.gpsimd, sp=nc.sync)
        self.cur = {}
        self.seen = {e: {} for e in self.eng}
        self.slots = {}
        self.slot_i = {}
        self.nsem = 0
        self.ninst = {e: 0 for e in self.eng}

    def _newsem(self, tag):
        self.nsem += 1
        return self.stack.enter_context(self.nc.semaphore(f"s{tag}_{self.nsem}"))

    def _wait(self, e, deps):
        eng = self.eng[e]
        best = {}
        for d in deps:
            if d is None:
                continue
            sem, val, src = d
            if src == "pe" and e == "pe":
                continue
            if self.seen[e].get(sem.num, 0) >= val:
                continue
            if best.get(sem.num, (None, 0))[1] < val:
                best[sem.num] = (sem, val)
        for sem, val in best.values():
            eng.wait_ge(sem, val)
            self.seen[e][sem.num] = val
            self.ninst[e] += 1

    def _deps(self, reads, writes):
        deps = []
        for b in reads:
            deps.append(b.w)
        for b in writes:
            deps.append(b.w)
            deps.extend(b.r.values())
        return deps

    def _record(self, ticket, reads, writes):
        sem = ticket[0]
        for b in reads:
            b.r[sem.num] = ticket
        for b in writes:
            b.w = ticket
            b.r = {}

    def op(self, e, fn, reads=(), writes=()):
        self._wait(e, self._deps(reads, writes))
        c = self.cur.get(e)
        if c is None or c[1] >= self.LIMIT:
            c = [self._newsem(e), 0]
            self.cur[e] = c
        ins = fn(self.eng[e])
        ins.then_inc(c[0], 1)
        c[1] += 1
        self.ninst[e] += 1
        t = (c[0], c[1], e)
        self._record(t, reads, writes)
        return t

    def dma(self, q, out, in_, reads=(), writes=(), **kw):
        if q not in self.slots:
            self.slots[q] = [None] * self.NSLOT
            self.slot_i[q] = 0
        i = self.slot_i[q]
        self.slot_i[q] = (i + 1) % self.NSLOT
        sl = self.slots[q][i]
        deps = self._deps(reads, writes)
        if sl is not None and sl[1] > 0:
            deps.append((sl[0], sl[1] * 16, "dma"))
        if sl is None or sl[1] * 16 >= self.LIMIT:
            self._wait(q, deps)
            deps = []
            sl = [self._newsem(f"d{q}{i}"), 0]
            self.slots[q][i] = sl
        self._wait(q, deps)
        ins = self.eng[q].dma_start(out=out, in_=in_, **kw)
        ins.then_inc(sl[0], 16)
        sl[1] += 1
        self.ninst[q] += 1
        t = (sl[0], sl[1] * 16, "dma")
        self._record(t, reads, writes)
        return t

    def _all(self):
        deps = []
        for q, sl in self.slots.items():
            for s in sl:
                if s is not None and s[1] > 0:
                    deps.append((s[0], s[1] * 16, "dma"))
        for e, c in self.cur.items():
            deps.append((c[0], c[1], e))
        return deps

    def barrier(self):
        deps = self._all()
        for e in self.eng:
            self._wait(e, deps)

    def finish(self):
        self._wait("sp", self._all())


class Rot:
    def __init__(self, items):
        self.items = items
        self.i = 0

    def next(self):
        it = self.items[self.i]
        self.i = (self.i + 1) % len(self.items)
        return it


def build(S, depth, dbg=False):
    NT = S // 128
    NG = S // 512
    nc = bass.Bass("TRN2", target_bir_lowering=False)

    def din(name, shape, dt=F32):
        return nc.dram_tensor(name, list(shape), dt, kind="ExternalInput").ap()

    def dscr(name, shape, dt):
        return nc.dram_tensor(name, list(shape), dt, kind="ExternalOutput" if dbg else "Internal").ap()

    L = depth
    x_in = din("x", [S, D])
    cT_in = din("cT", [128, KC])
    ada_w = din("ada_w", [L, D, 6 * D]); ada_b = din("ada_b", [L, 6 * D])
    norm_mix_w = din("norm_mix_w", [L, D]); norm_mlp_w = din("norm_mlp_w", [L, D])
    final_norm_w = din("final_norm_w", [D])
    w_in = din("w_in", [L, D, IN_COLS])
    w_mqk = din("w_mqk", [L, D, 1024])
    gmlp_ln_w = din("gmlp_ln_w", [L, 512]); gmlp_ln_b = din("gmlp_ln_b", [L, 512])
    gmlp_wsT = din("gmlp_wsT", [L, 8, 128, 128]); gmlp_bs = din("gmlp_bs", [L, 8, 128])
    fox_f_bias = din("fox_f_bias", [L, 8])
    conv_wT = din("conv_wT", [L, 128, 10, 4]); conv_b = din("conv_b", [L, 128, 10])
    dt_bias = din("dt_bias", [L, 12]); a_log = din("a_log", [L, 12])
    ssm_d_rep = din("ssm_d_rep", [L, 128, 6]); ssm_norm_w = din("ssm_norm_w", [L, 128, 6])
    w_br = din("w_br", [L, 2304, D])
    w_out = din("w_out", [L, D, D])
    w1 = din("mlp_w1", [L, D, 4 * D]); w2 = din("mlp_w2", [L, 4 * D, D])
    c_ident = din("c_ident", [128, 128])
    c_tri = din("c_tri", [128, 128])
    c_m01 = din("c_m01", [128, 128])
    c_rope = din("c_rope", [2, 40, S])
    c_onehot = din("c_onehot", [32, S])
    c_e0 = din("c_e0", [128, 128])
    c_rot = din("c_rot", [64, 64])
    c_sel = din("c_sel", [64, 12 * 32])
    c_selb = din("c_selb", [64, 12 * 128])
    c_ownneg = din("c_ownneg", [128, NT * 16])
    c_own0 = din("c_own0", [128, NT * 16])

    out = nc.dram_tensor("out", [S, D], F32, kind="ExternalOutput").ap()

    xs = dscr("xs", [S, D], F32)
    uT = dscr("uT", [4, 128, S], BF16)
    vg = dscr("vg", [S, 512], F32)
    qTf = dscr("qTf", [8, 64, S], BF16); kTf = dscr("kTf", [8, 64, S], BF16); vf = dscr("vf", [S, 512], BF16)
    fT = dscr("fT", [8, S], F32)
    qTm = dscr("qTm", [8, 64, S], BF16); kTm = dscr("kTm", [8, 64, S], BF16); vm = dscr("vm", [S, 512], BF16)
    szT = dscr("szT", [6, 128, S], BF16)
    xbcT = dscr("xbcT", [10, 128, S], F32)
    dtT = dscr("dtT", [12, S], F32); dtm = dscr("dtm", [S, 12], F32)
    gT = dscr("gT", [32, 128, S], BF16)
    yT = dscr("yT", [18, 128, S], BF16)
    ysT = dscr("ysT", [12, 64, S], F32)
    hidT = dscr("hidT", [32, 128, S], BF16)
    xsTd = dscr("xsTd", [6, 128, S], BF16)

    with ExitStack() as top:
        Sd = Sched(nc, top)

        uid = [0]

        def alloc(stack, name, shape, dt):
            uid[0] += 1
            t = stack.enter_context(nc.sbuf_tensor(f"{name}_{uid[0]}", list(shape), dt))
            return t, Buf(name)

        banks = []
        for i in range(8):
            t = top.enter_context(nc.psum_tensor(f"bank{i}", [128, 512], F32))
            banks.append((t, Buf(f"bank{i}")))
        accrot = Rot(banks[0:2])
        bankrot = Rot(banks[2:8])

        ident, b_ident = alloc(top, "ident", [128, 128], F32)
        identb, b_identb = alloc(top, "identb", [128, 128], BF16)
        trib, b_trib = alloc(top, "trib", [128, 128], BF16)
        ones32, b_ones32 = alloc(top, "ones32", [128, 128], F32)
        e0, b_e0 = alloc(top, "e0", [128, 128], F32)
        sel, b_sel = alloc(top, "sel", [64, 12 * 32], BF16)
        ca, b_ca = alloc(top, "ca", [128, KC], F32)
        modb, b_modb = alloc(top, "modb", [128, 6 * D], F32)
        wmod, b_wmod = alloc(top, "wmod", [128, 2 * D], F32)
        consts_r = [b_ident, b_identb, b_trib, b_ones32, b_e0, b_sel]

        Sd.dma("sp", ident[:], c_ident, writes=[b_ident])
        Sd.dma("pool", identb[:], c_ident, writes=[b_identb])
        Sd.dma("pool", trib[:], c_tri, writes=[b_trib])
        Sd.dma("sp", e0[:], c_e0, writes=[b_e0])
        Sd.dma("pool", sel[:], c_sel, writes=[b_sel])
        Sd.op("dve", lambda e: e.memset(ones32[:], 1.0), writes=[b_ones32])
        with ExitStack() as st:
            ct, b_ct = alloc(st, "ct", [128, KC], F32)
            Sd.dma("sp", ct[:], cT_in, writes=[b_ct])
            Sd.op("act", lambda e: e.activation(out=ca[:], in_=ct[:], func=AF.Silu), reads=[b_ct], writes=[b_ca])
            b_x0 = Buf()
            for i in range(NT):
                Sd.dma("sp", xs[i * 128:(i + 1) * 128, :], x_in[i * 128:(i + 1) * 128, :], writes=[b_x0])
            Sd.barrier()

        dummy, b_dummy = alloc(top, "dummy", [128, 512], BF16)
        Sd.op("pool", lambda e: e.memset(dummy[:], 1.0), writes=[b_dummy])

        def pe_warm(n=24):
            for _ in range(n):
                bk, bb = bankrot.next()
                Sd.op("pe", lambda e: e.matmul(bk[:, :], lhsT=identb[:], rhs=dummy[:], start=True, stop=True),
                      reads=[b_identb, b_dummy], writes=[bb])

        def adaln(l):
            with ExitStack() as st:
                cact_rep, b_cact = alloc(st, "cact_rep", [128, KC, 128], F32)
                Sd.op("dve", lambda e: e.tensor_copy(out=cact_rep[:], in_=ca[:].unsqueeze(2).to_broadcast([128, KC, 128])),
                      reads=[b_ca], writes=[b_cact])
                awt = [alloc(st, f"awt{i}", [128, KC, 512], F32) for i in range(2)]
                awr = Rot(awt)
                abb, b_abb = alloc(st, "abb", [128, 6 * D], F32)
                nw, b_nw = alloc(st, "nw", [128, 2 * D], F32)
                Sd.dma("sp", abb[:], ada_b[l].partition_broadcast(128), writes=[b_abb])
                Sd.dma("sp", nw[:, 0:D], norm_mix_w[l].partition_broadcast(128), writes=[b_nw])
                Sd.dma("sp", nw[:, D:2 * D], norm_mlp_w[l].partition_broadcast(128), writes=[b_nw])
                for g in range(12):
                    w, bw = awr.next()
                    Sd.dma("sp", w[:], ada_w[l][:, g * 512:(g + 1) * 512].rearrange("(kc p) n -> p kc n", p=128), writes=[bw])
                    bk, bb = bankrot.next()
                    for kc in range(KC):
                        Sd.op("pe", lambda e, kc=kc: e.matmul(bk[:, :], lhsT=cact_rep[:, kc, :], rhs=w[:, kc, :],
                                                               start=(kc == 0), stop=(kc == KC - 1)),
                              reads=[b_cact, bw], writes=[bb])
                    Sd.op("dve", lambda e: e.tensor_tensor(out=modb[:, g * 512:(g + 1) * 512], in0=bk[:, :],
                                                           in1=abb[:, g * 512:(g + 1) * 512], op=ALU.add),
                          reads=[bb, b_abb], writes=[b_modb])
                for j, sc_off in enumerate((1 * D, 4 * D)):
                    Sd.op("dve", lambda e, j=j, sc_off=sc_off: e.scalar_tensor_tensor(
                        out=wmod[:, j * D:(j + 1) * D], in0=modb[:, sc_off:sc_off + D], scalar=1.0,
                        in1=nw[:, j * D:(j + 1) * D], op0=ALU.add, op1=ALU.mult),
                        reads=[b_modb, b_nw], writes=[b_wmod])
                Sd.barrier()

        def norm_stage(st, wm_ap, sh_ap, hT, b_hT, final=False):
            xt = Rot([alloc(st, f"nx{i}", [128, D], F32) for i in range(3)])
            h1 = Rot([alloc(st, f"nh{i}", [128, D], F32) for i in range(3)])
            h2 = Rot([alloc(st, f"ng{i}", [128, D], F32) for i in range(3)])
            junk, b_junk = alloc(st, "njunk", [128, D], BF16)
            stat = Rot([alloc(st, f"nst{i}", [128, 4], F32) for i in range(3)])
            for i in range(NT):
                x_t, bx = xt.next()
                Sd.dma("sp", x_t[:], xs[i * 128:(i + 1) * 128, :], writes=[bx])
                s_t, bs = stat.next()
                Sd.op("act", lambda e: e.activation(out=junk[:], in_=x_t[:], func=AF.Square, accum_out=s_t[:, 0:1]),
                      reads=[bx], writes=[b_junk, bs])
                Sd.op("dve", lambda e: e.tensor_scalar(out=s_t[:, 1:2], in0=s_t[:, 0:1], scalar1=1.0 / D, scalar2=EPS,
                                                       op0=ALU.mult, op1=ALU.add), reads=[bs], writes=[bs])
                Sd.op("act", lambda e: e.activation(out=s_t[:, 2:3], in_=s_t[:, 1:2], func=AF.Sqrt), reads=[bs], writes=[bs])
                Sd.op("dve", lambda e: e.reciprocal(out=s_t[:, 3:4], in_=s_t[:, 2:3]), reads=[bs], writes=[bs])
                a_t, ba = h1.next()
                Sd.op("dve", lambda e: e.scalar_tensor_tensor(out=a_t[:], in0=x_t[:], scalar=s_t[:, 3:4], in1=wm_ap,
                                                              op0=ALU.mult, op1=ALU.mult),
                      reads=[bx, bs, b_wmod], writes=[ba])
                if final:
                    Sd.dma("sp", out[i * 128:(i + 1) * 128, :], a_t[:], reads=[ba])
                    continue
                g_t, bg = h2.next()
                Sd.op("pool", lambda e: e.tensor_tensor(out=g_t[:], in0=a_t[:], in1=sh_ap, op=ALU.add),
                      reads=[ba, b_modb], writes=[bg])
                for half in range(2):
                    bk, bb = bankrot.next()
                    for q in range(4):
                        kc = half * 4 + q
                        Sd.op("pe", lambda e, kc=kc, q=q: e.transpose(out=bk[:, q * 128:(q + 1) * 128],
                                                                      in_=g_t[:, kc * 128:(kc + 1) * 128], identity=ident[:]),
                              reads=[bg, b_ident], writes=[bb])
                    eng = "act" if half == 0 else "dve"
                    if eng == "act":
                        Sd.op("act", lambda e: e.copy(out=hT[:, half * 4:half * 4 + 4, i * 128:(i + 1) * 128],
                                                      in_=bk[:, :].rearrange("p (q t) -> p q t", q=4)),
                              reads=[bb], writes=[b_hT])
                    else:
                        Sd.op("dve", lambda e: e.tensor_copy(out=hT[:, half * 4:half * 4 + 4, i * 128:(i + 1) * 128],
                                                             in_=bk[:, :].rearrange("p (q t) -> p q t", q=4)),
                              reads=[bb], writes=[b_hT])

        def proj_fm(wts, hT, b_hT, w_ap, ncols, evac, nk=KC):
            for g0 in range(0, ncols, 512):
                n = min(512, ncols - g0)
                w, bw = wts.next()
                Sd.dma("pool", w[:, :, 0:n], w_ap[:, g0:g0 + n].rearrange("(kc p) n -> p kc n", p=128), writes=[bw])
                for tg in range(NG):
                    for c0 in range(0, n, 128):
                        rows = min(128, n - c0)
                        bk, bb = bankrot.next()
                        for kc in range(nk):
                            Sd.op("pe", lambda e, kc=kc: e.matmul(bk[0:rows, :], lhsT=w[:, kc, c0:c0 + rows],
                                                                   rhs=hT[:, kc, tg * 512:(tg + 1) * 512],
                                                                   start=(kc == 0), stop=(kc == nk - 1)),
                                  reads=[bw, b_hT], writes=[bb])
                        evac(bk, bb, rows, (g0 + c0) // 128, tg)

        def proj_tm(wts, hT, b_hT, w_ap, ncols, evac):
            w, bw = wts.next()
            assert ncols <= 512
            Sd.dma("pool", w[:, :, 0:ncols], w_ap.rearrange("(kc p) n -> p kc n", p=128), writes=[bw])
            for i in range(NT):
                bk, bb = bankrot.next()
                for kc in range(KC):
                    Sd.op("pe", lambda e, kc=kc: e.matmul(bk[:, 0:ncols], lhsT=hT[:, kc, i * 128:(i + 1) * 128],
                                                           rhs=w[:, kc, 0:ncols], start=(kc == 0), stop=(kc == KC - 1)),
                          reads=[bw, b_hT], writes=[bb])
                evac(bk, bb, i)

        def mixer_proj(l, st, hT, b_hT):
            stg = Rot([alloc(st, f"stg{i}", [128, 512], BF16) for i in range(3)])
            stg32 = Rot([alloc(st, f"stgf{i}", [128, 512], F32) for i in range(3)])
            wts = Rot([alloc(st, f"pw{i}", [128, KC, 512], BF16) for i in range(2)])
            wl = w_in[l]

            def ev_act(dst_fn, func, scale=1.0, f32=False):
                def ev(bk, bb, rows, cc, tg):
                    s_t, bs = (stg32 if f32 else stg).next()
                    Sd.op("act", lambda e: e.activation(out=s_t[0:rows, :], in_=bk[0:rows, :], func=func, scale=scale),
                          reads=[bb], writes=[bs])
                    Sd.dma("sp", dst_fn(cc, tg, rows), s_t[0:rows, :], reads=[bs])
                return ev

            def ev_copy(dst_fn, f32=False):
                def ev(bk, bb, rows, cc, tg):
                    s_t, bs = (stg32 if f32 else stg).next()
                    Sd.op("dve", lambda e: e.tensor_copy(out=s_t[0:rows, :], in_=bk[0:rows, :]), reads=[bb], writes=[bs])
                    Sd.dma("sp", dst_fn(cc, tg, rows), s_t[0:rows, :], reads=[bs])
                return ev

            def tgs(tg):
                return slice(tg * 512, (tg + 1) * 512)

            proj_fm(wts, hT, b_hT, wl[:, C_U:C_U + 512], 512,
                    ev_act(lambda cc, tg, rows: uT[cc, :, tgs(tg)], AF.Gelu_apprx_tanh))
            proj_fm(wts, hT, b_hT, wl[:, C_FQ:C_FQ + 512], 512,
                    ev_act(lambda cc, tg, rows: qTf[2 * cc:2 * cc + 2, :, tgs(tg)].rearrange("h d t -> (h d) t"), AF.Copy, 0.125))
            proj_fm(wts, hT, b_hT, wl[:, C_FK:C_FK + 512], 512,
                    ev_copy(lambda cc, tg, rows: kTf[2 * cc:2 * cc + 2, :, tgs(tg)].rearrange("h d t -> (h d) t")))
            proj_fm(wts, hT, b_hT, wl[:, C_FF:C_FF + 8], 8,
                    ev_copy(lambda cc, tg, rows: fT[:, tgs(tg)], f32=True))
            wm = w_mqk[l]
            proj_fm(wts, hT, b_hT, wm[:, 0:512], 512,
                    ev_act(lambda cc, tg, rows: qTm[2 * cc:2 * cc + 2, :, tgs(tg)].rearrange("h d t -> (h d) t"), AF.Copy, 0.125))
            proj_fm(wts, hT, b_hT, wm[:, 512:1024], 512,
                    ev_copy(lambda cc, tg, rows: kTm[2 * cc:2 * cc + 2, :, tgs(tg)].rearrange("h d t -> (h d) t")))
            proj_fm(wts, hT, b_hT, wl[:, C_Z:C_Z + 768], 768,
                    ev_act(lambda cc, tg, rows: szT[cc, :, tgs(tg)], AF.Silu))
            proj_fm(wts, hT, b_hT, wl[:, C_XBC:C_XBC + 1280], 1280,
                    ev_copy(lambda cc, tg, rows: xbcT[cc, :, tgs(tg)], f32=True))
            proj_fm(wts, hT, b_hT, wl[:, C_DT:C_DT + 12], 12,
                    ev_copy(lambda cc, tg, rows: dtT[:, tgs(tg)], f32=True))
            proj_fm(wts, hT, b_hT, wl[:, C_G:C_G + 4096], 4096,
                    ev_act(lambda cc, tg, rows: gT[cc, :, tgs(tg)], AF.Sigmoid))

            def ev_tm(dst, n, func=None, f32=False):
                def ev(bk, bb, i):
                    s_t, bs = (stg32 if f32 else stg).next()
                    if func is None:
                        Sd.op("dve", lambda e: e.tensor_copy(out=s_t[:, 0:n], in_=bk[:, 0:n]), reads=[bb], writes=[bs])
                    else:
                        Sd.op("act", lambda e: e.activation(out=s_t[:, 0:n], in_=bk[:, 0:n], func=func), reads=[bb], writes=[bs])
                    Sd.dma("sp", dst[i * 128:(i + 1) * 128, :], s_t[:, 0:n], reads=[bs])
                return ev
            proj_tm(wts, hT, b_hT, wl[:, C_V:C_V + 512], 512, ev_tm(vg, 512, AF.Gelu_apprx_tanh, True))
            proj_tm(wts, hT, b_hT, wl[:, C_FV:C_FV + 512], 512, ev_tm(vf, 512))
            proj_tm(wts, hT, b_hT, wl[:, C_MV:C_MV + 512], 512, ev_tm(vm, 512))
            proj_tm(wts, hT, b_hT, wl[:, C_DT:C_DT + 12], 12, ev_tm(dtm, 12, None, True))

        gbank = {"rot": None}

        def gmlp_prepare(st, l):
            if True:
                wsT, b_wsT = alloc(st, "wsT", [128, 8, 128], F32)
                wsm, b_wsm = alloc(st, "wsm", [128, 8, 128], BF16)
                m01, b_m01 = alloc(st, "m01", [128, 128], F32)
                bsb, b_bsb = alloc(st, "bsb", [128, 4, 128], F32)
                lnw, b_lnw = alloc(st, "lnw", [128, 512], F32)
                lnb, b_lnb = alloc(st, "lnb", [128, 512], F32)
                Sd.dma("sp", wsT[:], gmlp_wsT[l].rearrange("g s t -> s g t"), writes=[b_wsT])
                Sd.dma("sp", m01[:], c_m01, writes=[b_m01])
                for c in range(4):
                    for hh in range(2):
                        Sd.dma("sp", bsb[hh * 64:(hh + 1) * 64, c, :], gmlp_bs[l, 2 * c + hh].partition_broadcast(64), writes=[b_bsb])
                Sd.dma("sp", lnw[:], gmlp_ln_w[l].partition_broadcast(128), writes=[b_lnw])
                Sd.dma("sp", lnb[:], gmlp_ln_b[l].partition_broadcast(128), writes=[b_lnb])
                Sd.op("dve", lambda e: e.tensor_tensor(out=wsm[:], in0=wsT[:], in1=m01[:].unsqueeze(1).to_broadcast([128, 8, 128]),
                                                       op=ALU.mult), reads=[b_wsT, b_m01], writes=[b_wsm])
                vt = Rot([alloc(st, f"gv{i}", [128, 512], F32) for i in range(3)])
                ut = Rot([alloc(st, f"gu{i}", [128, 4, 128], BF16) for i in range(3)])
                v1 = Rot([alloc(st, f"gw{i}", [128, 512], F32) for i in range(3)])
                v2 = Rot([alloc(st, f"gx{i}", [128, 512], F32) for i in range(3)])
                vn = Rot([alloc(st, f"gn{i}", [128, 512], BF16) for i in range(3)])
                tt = Rot([alloc(st, f"gt{i}", [128, 512], F32) for i in range(3)])
                yo = Rot([alloc(st, f"gy{i}", [128, 4, 128], BF16) for i in range(3)])
                junk, b_junk = alloc(st, "gjunk", [128, 512], BF16)
                stat = Rot([alloc(st, f"gs{i}", [128, 8], F32) for i in range(4)])
                ld = {}

                def gloads(i):
                    ts = slice(i * 128, (i + 1) * 128)
                    v_t, bv = vt.next()
                    u_t, bu = ut.next()
                    Sd.dma("sp", v_t[:], vg[ts, :], writes=[bv])
                    Sd.dma("sp", u_t[:], uT[:, :, ts].rearrange("c p t -> p c t"), writes=[bu])
                    ld[i] = (v_t, bv, u_t, bu)
                gloads(0); gloads(1)

                def gtile(i):
                    ts = slice(i * 128, (i + 1) * 128)
                    if i + 2 < NT:
                        gloads(i + 2)
                    v_t, bv, u_t, bu = ld.pop(i)
                    s, bs = stat.next()
                    Sd.op("dve", lambda e: e.tensor_reduce(out=s[:, 0:1], in_=v_t[:], axis=AX.X, op=ALU.add), reads=[bv], writes=[bs])
                    Sd.op("act", lambda e: e.activation(out=junk[:], in_=v_t[:], func=AF.Square, accum_out=s[:, 1:2]),
                          reads=[bv], writes=[b_junk, bs])
                    Sd.op("dve", lambda e: e.tensor_scalar(out=s[:, 2:3], in0=s[:, 0:1], scalar1=-1.0 / 512, scalar2=None, op0=ALU.mult),
                          reads=[bs], writes=[bs])
                    Sd.op("dve", lambda e: e.tensor_tensor(out=s[:, 3:4], in0=s[:, 2:3], in1=s[:, 2:3], op=ALU.mult), reads=[bs], writes=[bs])
                    Sd.op("dve", lambda e: e.scalar_tensor_tensor(out=s[:, 4:5], in0=s[:, 1:2], scalar=1.0 / 512, in1=s[:, 3:4],
                                                                  op0=ALU.mult, op1=ALU.subtract), reads=[bs], writes=[bs])
                    Sd.op("dve", lambda e: e.tensor_scalar(out=s[:, 5:6], in0=s[:, 4:5], scalar1=EPS, scalar2=None, op0=ALU.add),
                          reads=[bs], writes=[bs])
                    Sd.op("act", lambda e: e.activation(out=s[:, 6:7], in_=s[:, 5:6], func=AF.Sqrt), reads=[bs], writes=[bs])
                    Sd.op("dve", lambda e: e.reciprocal(out=s[:, 7:8], in_=s[:, 6:7]), reads=[bs], writes=[bs])
                    a1, ba1 = v1.next()
                    Sd.op("dve", lambda e: e.tensor_scalar(out=a1[:], in0=v_t[:], scalar1=s[:, 2:3], scalar2=s[:, 7:8],
                                                           op0=ALU.add, op1=ALU.mult), reads=[bv, bs], writes=[ba1])
                    a2, ba2 = v2.next()
                    Sd.op("pool", lambda e: e.tensor_tensor(out=a2[:], in0=a1[:], in1=lnw[:], op=ALU.mult), reads=[ba1, b_lnw], writes=[ba2])
                    n_t, bn = vn.next()
                    Sd.op("pool", lambda e: e.tensor_tensor(out=n_t[:], in0=a2[:], in1=lnb[:], op=ALU.add), reads=[ba2, b_lnb], writes=[bn])
                    bk, bb = gbank["rot"].next()
                    for g in range(8):
                        po = (g % 2) * 64
                        Sd.op("pe", lambda e, g=g, po=po: e.matmul(bk[po:po + 64, (g // 2) * 128:(g // 2 + 1) * 128],
                                                                   lhsT=n_t[:, g * 64:(g + 1) * 64], rhs=wsm[:, g, :],
                                                                   start=True, stop=True),
                              reads=[bn, b_wsm], writes=[bb])
                    t_t, bt = tt.next()
                    Sd.op("dve", lambda e: e.tensor_tensor(out=t_t[:], in0=bk[:, :], in1=bsb[:].rearrange("p c t -> p (c t)"), op=ALU.add),
                          reads=[bb, b_bsb], writes=[bt])
                    y_t, by = yo.next()
                    Sd.op("pool", lambda e: e.tensor_tensor(out=y_t[:].rearrange("p c t -> p (c t)"), in0=t_t[:],
                                                            in1=u_t[:].rearrange("p c t -> p (c t)"), op=ALU.mult),
                          reads=[bt, bu], writes=[by])
                    Sd.dma("sp", yT[0:4, :, ts].rearrange("c p t -> p c t"), y_t[:], reads=[by])
                return [(lambda i=i: gtile(i)) for i in range(NT)]

        def decay_prep(st, srcT, b_src, nh, tagp):
            cum, b_cum = alloc(st, tagp + "cum", [nh, S], F32)
            one1, b_one1 = alloc(st, tagp + "one1", [nh, 1], F32)
            Sd.op("dve", lambda e: e.memset(one1[:], 1.0), writes=[b_one1])
            Sd.op("dve", lambda e: e.tensor_tensor_scan(out=cum[:], data0=one1[:, 0:1].to_broadcast([nh, S]), data1=srcT[:],
                                                        initial=0.0, op0=ALU.mult, op1=ALU.add),
                  reads=[b_one1, b_src], writes=[b_cum])
            a32, b_a32 = alloc(st, tagp + "a32", [nh, S], F32)
            AH, b_AH = alloc(st, tagp + "AH", [64, S], BF16)
            Sd.op("pool", lambda e: e.memset(AH[:], 0.0), writes=[b_AH])
            for I in range(NG):
                cs = slice(I * 512, (I + 1) * 512)
                Sd.op("dve", lambda e, cs=cs, I=I: e.tensor_scalar(out=a32[:, cs], in0=cum[:, cs], scalar1=cum[:, I * 512:I * 512 + 1],
                                                                   scalar2=None, op0=ALU.subtract),
                      reads=[b_cum], writes=[b_a32])
            Sd.op("dve", lambda e: e.tensor_copy(out=AH[0:nh, :], in_=a32[:]), reads=[b_a32], writes=[b_AH])
            Sd.op("dve", lambda e: e.tensor_tensor(out=AH[32:32 + nh, :], in0=a32[:], in1=AH[0:nh, :], op=ALU.subtract),
                  reads=[b_a32, b_AH], writes=[b_AH])
            cumT, b_cumT = alloc(st, tagp + "cumT", [128, NT, nh], F32)
            bk, bb = bankrot.next()
            for i in range(NT):
                Sd.op("pe", lambda e, i=i: e.transpose(out=bk[:, i * nh:(i + 1) * nh], in_=cum[:, i * 128:(i + 1) * 128],
                                                       identity=ident[0:nh, 0:nh]),
                      reads=[b_cum, b_ident], writes=[bb])
            Sd.op("dve", lambda e: e.tensor_copy(out=cumT[:].rearrange("p i h -> p (i h)"), in_=bk[:, 0:NT * nh]), reads=[bb], writes=[b_cumT])
            refb, b_refb = alloc(st, tagp + "refb", [128, NG, nh], F32)
            bk, bb = bankrot.next()
            for I in range(NG):
                Sd.op("pe", lambda e, I=I: e.matmul(bk[:, I * nh:(I + 1) * nh], lhsT=e0[:], rhs=cumT[:, 4 * I, :], start=True, stop=True),
                      reads=[b_e0, b_cumT], writes=[bb])
            Sd.op("dve", lambda e: e.tensor_copy(out=refb[:].rearrange("p i h -> p (i h)"), in_=bk[:, 0:NG * nh]), reads=[bb], writes=[b_refb])
            fb, b_fb = alloc(st, tagp + "fb", [128, NG, NT, nh], F32)
            for I in range(NG):
                Sd.op("dve", lambda e, I=I: e.tensor_tensor(out=fb[:, I, :, :], in0=refb[:, I, :].unsqueeze(1).to_broadcast([128, NT, nh]),
                                                            in1=cumT[:], op=ALU.subtract),
                      reads=[b_refb, b_cumT], writes=[b_fb])
            return AH, b_AH, fb, b_fb

        def attn_core(st, kind, l, AH=None, b_AH=None, fb=None, b_fb=None, extra=None):
            qsrc, ksrc, vsrc, ybase = (qTf, kTf, vf, 4) if kind == "fox" else (qTm, kTm, vm, 8)
            Qa = Rot([alloc(st, f"Qa{i}", [96, S], BF16) for i in range(2)])
            Ka = Rot([alloc(st, f"Ka{i}", [96, S], BF16) for i in range(2)])
            Va = Rot([alloc(st, f"Va{i}", [128, NT, 65], BF16) for i in range(2)])
            PT = Rot([alloc(st, f"PT{i}", [128, 512], BF16) for i in range(5)])
            osb = Rot([alloc(st, f"osb{i}", [65, 512], F32) for i in range(2)])
            rsb = Rot([alloc(st, f"rsb{i}", [65, 512], F32) for i in range(2)])
            yst = Rot([alloc(st, f"yst{i}", [64, 512], BF16) for i in range(2)])
            rhl = Rot([alloc(st, f"rhl{i}", [97, 512], BF16) for i in range(2)])
            for (h_t0, bh0) in rhl.items:
                Sd.op("pool", lambda e, h_t0=h_t0: e.memset(h_t0[64:97, :], 0.0), writes=[bh0])
            onesb, b_onesb = alloc(st, "onesb", [128, 64], BF16)
            Sd.op("pool", lambda e: e.memset(onesb[:], 1.0), writes=[b_onesb])
            if kind == "moba":
                rope, b_rope = alloc(st, "rope", [40, 2, S], F32)
                Sd.dma("sp", rope[:, 0, :], c_rope[0], writes=[b_rope])
                Sd.dma("sp", rope[:, 1, :], c_rope[1], writes=[b_rope])
                oneh, b_oneh = alloc(st, "oneh", [32, S], BF16)
                Sd.dma("pool", oneh[:], c_onehot, writes=[b_oneh])
                ownneg, b_ownneg = alloc(st, "ownneg", [128, NT * 16], F32)
                own0, b_own0 = alloc(st, "own0", [128, NT * 16], F32)
                Sd.dma("sp", ownneg[:], c_ownneg, writes=[b_ownneg])
                Sd.dma("sp", own0[:], c_own0, writes=[b_own0])
                rt1 = Rot([alloc(st, f"rt1{i}", [40, 512], F32) for i in range(3)])
                rt2 = Rot([alloc(st, f"rt2{i}", [40, 512], F32) for i in range(3)])
                rotm, b_rotm = alloc(st, "rotm", [64, 64], BF16)
                Sd.dma("pool", rotm[:], c_rot, writes=[b_rotm])
                kmT, b_kmT = alloc(st, "kmT", [64, 16], F32)
                kmb, b_kmb = alloc(st, "kmb", [64, 32], BF16)
                G0, b_G0 = alloc(st, "G0", [128, NT, 16], F32)
                G1, b_G1 = alloc(st, "G1", [128, NT, 16], F32)
                EQ, b_EQ = alloc(st, "EQ", [128, NT, 16], F32)
                MB, b_MB = alloc(st, "MB", [128, NT, 32], F32)
                mx, b_mx = alloc(st, "mx", [128, NT], F32)
                Sd.op("pool", lambda e: e.memset(MB[:], 0.0), writes=[b_MB])
            for (q_t, bq), (k_t, bkk) in zip(Qa.items, Ka.items):
                Sd.op("pool", lambda e, k_t=k_t: e.memset(k_t[64:96, :], 0.0), writes=[bkk])
                if kind == "fox":
                    Sd.op("pool", lambda e, k_t=k_t: e.memset(k_t[64:66, :], 1.0), writes=[bkk])
                else:
                    Sd.op("dve", lambda e, k_t=k_t: e.tensor_copy(out=k_t[64:96, :], in_=oneh[:]), reads=[b_oneh], writes=[bkk])
            for (v_t, bv) in Va.items:
                Sd.op("pool", lambda e, v_t=v_t: e.memset(v_t[:, :, 64:65], 1.0), writes=[bv])

            pb, pbb = banks[7]
            wrot = Rot(banks[2:7])
            hc = {}

            def prolog_steps(h):
                steps = []
                q_t, bq = Qa.next()
                k_t, bkk = Ka.next()
                v_t, bv = Va.next()
                hc[h] = (q_t, bq, k_t, bkk, v_t, bv)

                def loads():
                    Sd.dma("sp", q_t[0:64, :], qsrc[h], writes=[bq])
                    Sd.dma("sp", k_t[0:64, :], ksrc[h], writes=[bkk])
                    Sd.dma("sp", v_t[:, :, 0:64], vsrc[:, h * 64:(h + 1) * 64].rearrange("(i p) d -> p i d", p=128), writes=[bv])
                steps.append(loads)
                if kind == "fox":
                    for I in range(NG):
                        def f(I=I):
                            Sd.op("pe", lambda e: e.matmul(pb[0:32, :], lhsT=sel[0:64, h * 32:(h + 1) * 32], rhs=AH[0:64, I * 512:(I + 1) * 512],
                                                           start=True, stop=True), reads=[b_sel, b_AH], writes=[pbb])
                            Sd.op("dve", lambda e: e.tensor_copy(out=q_t[64:96, I * 512:(I + 1) * 512], in_=pb[0:32, :]), reads=[pbb], writes=[bq])
                        steps.append(f)
                    return steps
                for (T, bT) in ((q_t, bq), (k_t, bkk)):
                    for I in range(NG):
                        def f(T=T, bT=bT, I=I):
                            cs = slice(I * 512, (I + 1) * 512)
                            Sd.op("pe", lambda e: e.matmul(pb[0:64, :], lhsT=rotm[:, :], rhs=T[0:64, cs], start=True, stop=True),
                                  reads=[b_rotm, bT], writes=[pbb])
                            t1, b1 = rt1.next()
                            t2, b2 = rt2.next()
                            Sd.op("dve", lambda e: e.tensor_tensor(out=t1[:, :], in0=pb[0:40, :], in1=rope[:, 1, cs], op=ALU.mult), reads=[pbb, b_rope], writes=[b1])
                            Sd.op("pool", lambda e: e.tensor_tensor(out=t2[:, :], in0=T[0:40, cs], in1=rope[:, 0, cs], op=ALU.mult), reads=[bT, b_rope], writes=[b2])
                            Sd.op("dve", lambda e: e.tensor_tensor(out=T[0:40, cs], in0=t1[:, :], in1=t2[:, :], op=ALU.add), reads=[b1, b2], writes=[bT])
                        steps.append(f)
                nblk = S // 256

                def kmean():
                    Sd.op("dve", lambda e: e.tensor_reduce(out=kmT[:, 0:nblk], in_=k_t[0:64, :].rearrange("p (n t) -> p n t", t=256), axis=AX.X, op=ALU.add),
                          reads=[bkk], writes=[b_kmT])
                    Sd.op("dve", lambda e: e.tensor_copy(out=kmb[:, 0:nblk], in_=kmT[:, 0:nblk]), reads=[b_kmT], writes=[b_kmb])
                    Sd.op("dve", lambda e: e.tensor_tensor(out=kmb[:, 16:16 + nblk], in0=kmT[:, 0:nblk], in1=kmb[:, 0:nblk], op=ALU.subtract),
                          reads=[b_kmT, b_kmb], writes=[b_kmb])
                steps.append(kmean)
                for i0 in range(0, NT, 8):
                    def f(i0=i0):
                        for i in range(i0, min(i0 + 8, NT)):
                            Sd.op("pe", lambda e, i=i: e.matmul(pb[:, i * 16:i * 16 + nblk], lhsT=q_t[0:64, i * 128:(i + 1) * 128], rhs=kmb[:, 0:nblk],
                                                                start=True, stop=False), reads=[bq, b_kmb], writes=[pbb])
                            Sd.op("pe", lambda e, i=i: e.matmul(pb[:, i * 16:i * 16 + nblk], lhsT=q_t[0:64, i * 128:(i + 1) * 128], rhs=kmb[:, 16:16 + nblk],
                                                                start=False, stop=True), reads=[bq, b_kmb], writes=[pbb])
                    steps.append(f)

                def tk0():
                    if nblk < 16:
                        Sd.op("dve", lambda e: e.memset(G0[:], 0.0), writes=[b_G0])
                    Sd.op("dve", lambda e: e.tensor_tensor(out=G0[:, :, 0:nblk], in0=pb[:, 0:NT * 16].rearrange("p (i n) -> p i n", n=16)[:, :, 0:nblk],
                                                           in1=ownneg[:].rearrange("p (i n) -> p i n", n=16)[:, :, 0:nblk], op=ALU.add),
                          reads=[pbb, b_ownneg], writes=[b_G0])
                    if nblk < 16:
                        Sd.op("dve", lambda e: e.memset(G0[:, :, nblk:16], -3e9), writes=[b_G0])
                steps.append(tk0)
                for r in range(2):
                    def f(r=r):
                        src, bsrc = (G0, b_G0) if r == 0 else (G1, b_G1)
                        Sd.op("dve", lambda e: e.tensor_reduce(out=mx[:], in_=src[:], axis=AX.X, op=ALU.max), reads=[bsrc], writes=[b_mx])
                        Sd.op("dve", lambda e: e.tensor_tensor(out=EQ[:], in0=src[:], in1=mx[:].unsqueeze(2).to_broadcast([128, NT, 16]),
                                                               op=ALU.is_equal), reads=[bsrc, b_mx], writes=[b_EQ])
                        Sd.op("dve", lambda e: e.scalar_tensor_tensor(out=G1[:].rearrange("p i n -> p (i n)"), in0=EQ[:].rearrange("p i n -> p (i n)"),
                                                                      scalar=-2e9, in1=src[:].rearrange("p i n -> p (i n)"),
                                                                      op0=ALU.mult, op1=ALU.add),
                              reads=[b_EQ, bsrc], writes=[b_G1])
                    steps.append(f)

                def tk3():
                    Sd.op("dve", lambda e: e.tensor_reduce(out=mx[:], in_=G1[:], axis=AX.X, op=ALU.max), reads=[b_G1], writes=[b_mx])
                    Sd.op("dve", lambda e: e.tensor_tensor(out=EQ[:], in0=G0[:], in1=mx[:].unsqueeze(2).to_broadcast([128, NT, 16]), op=ALU.is_ge),
                          reads=[b_G0, b_mx], writes=[b_EQ])
                    Sd.op("dve", lambda e: e.scalar_tensor_tensor(out=MB[:, :, 0:16], in0=EQ[:], scalar=-1.0,
                                                                  in1=own0[:].rearrange("p (i n) -> p i n", n=16), op0=ALU.add, op1=ALU.mult),
                          reads=[b_EQ, b_own0], writes=[b_MB])
                steps.append(tk3)
                for I in range(NG):
                    def f(I=I):
                        for q in range(4):
                            i = 4 * I + q
                            Sd.op("pe", lambda e, i=i, q=q: e.transpose(out=pb[0:32, q * 128:(q + 1) * 128], in_=MB[:, i, :], identity=ident[:]),
                                  reads=[b_MB, b_ident], writes=[pbb])
                        Sd.op("act", lambda e: e.copy(out=q_t[64:96, I * 512:(I + 1) * 512], in_=pb[0:32, :]), reads=[pbb], writes=[bq])
                    steps.append(f)
                return steps

            items = [(h, I, j) for h in range(8) for I in range(NG) for j in range(4 * I + 4)]
            nper = len(items) // 8
            ic = {}
            gc = {}
            deferred = []

            def stageA(idx):
                h, I, j = items[idx]
                q_t, bq, k_t, bkk, v_t, bv = hc[h]
                m = j - 4 * I
                c0 = max(m, 0) * 128
                sb_, sbb = wrot.next()
                if m < 0:
                    Sd.op("pe", lambda e: e.matmul(sb_[:, 0:512], lhsT=k_t[:, j * 128:(j + 1) * 128],
                                                   rhs=q_t[:, I * 512:(I + 1) * 512], start=True, stop=True),
                          reads=[bkk, bq], writes=[sbb])
                else:
                    Sd.op("pe", lambda e: e.matmul(sb_[:, c0:c0 + 128], lhsT=k_t[:, j * 128:(j + 1) * 128],
                                                   rhs=q_t[:, I * 512 + c0:I * 512 + c0 + 128], start=True, stop=False),
                          reads=[bkk, bq], writes=[sbb])
                    Sd.op("pe", lambda e: e.matmul(sb_[:, c0:c0 + 128], lhsT=identb[:], rhs=trib[:], start=False, stop=True),
                          reads=[b_identb, b_trib], writes=[sbb])
                    if c0 + 128 < 512:
                        Sd.op("pe", lambda e: e.matmul(sb_[:, c0 + 128:512], lhsT=k_t[:, j * 128:(j + 1) * 128],
                                                       rhs=q_t[:, I * 512 + c0 + 128:(I + 1) * 512], start=True, stop=True),
                              reads=[bkk, bq], writes=[sbb])
                p_t, bp = PT.next()
                if kind == "fox":
                    Sd.op("act", lambda e: e.activation(out=p_t[:, c0:512], in_=sb_[:, c0:512], func=AF.Exp, bias=fb[:, I, j, h:h + 1]),
                          reads=[sbb, b_fb], writes=[bp])
                else:
                    Sd.op("act", lambda e: e.activation(out=p_t[:, c0:512], in_=sb_[:, c0:512], func=AF.Exp), reads=[sbb], writes=[bp])
                ic[idx] = (p_t, bp, c0)

            def stageB(idx):
                h, I, j = items[idx]
                q_t, bq, k_t, bkk, v_t, bv = hc[h]
                nj = 4 * I + 4
                if j == 0:
                    gc[(h, I)] = accrot.next()
                ob, obb = gc[(h, I)]
                p_t, bp, c0 = ic.pop(idx)
                Sd.op("pe", lambda e: e.matmul(ob[0:65, c0:512], lhsT=v_t[:, j, :], rhs=p_t[:, c0:512],
                                               start=(j == 0), stop=(j == nj - 1)),
                      reads=[bv, bp], writes=[obb])
                if j == nj - 1:
                    o_t, bo = osb.next()
                    r_t, br = rsb.next()
                    Sd.op("act", lambda e: e.copy(out=o_t[:, :], in_=ob[0:65, :]), reads=[obb], writes=[bo])
                    Sd.op("dve", lambda e: e.reciprocal(out=r_t[64:65, :], in_=o_t[64:65, :]), reads=[bo], writes=[br])

                    h_t, bh_ = rhl.next()
                    Sd.op("dve", lambda e: e.tensor_copy(out=h_t[64:65, :], in_=r_t[64:65, :]), reads=[br], writes=[bh_])
                    Sd.op("dve", lambda e: e.tensor_tensor(out=h_t[96:97, :], in0=r_t[64:65, :], in1=h_t[64:65, :], op=ALU.subtract),
                          reads=[br, bh_], writes=[bh_])

                    def E2():
                        b2, b2b = wrot.next()
                        Sd.op("pe", lambda e: e.matmul(b2[0:64, :], lhsT=onesb[64:97, 0:64], rhs=h_t[64:97, :], start=True, stop=True),
                              reads=[b_onesb, bh_], writes=[b2b])
                        y_t, by = yst.next()
                        Sd.op("dve", lambda e: e.tensor_tensor(out=y_t[:, :], in0=o_t[0:64, :], in1=b2[0:64, :], op=ALU.mult), reads=[bo, b2b], writes=[by])
                        Sd.dma("sp", yT[ybase + h // 2, (h % 2) * 64:(h % 2) * 64 + 64, I * 512:(I + 1) * 512], y_t[:, :], reads=[by])
                    deferred.append([4, E2])

            LA = 3
            for f in prolog_steps(0):
                f()
            n = len(items)
            for idx in range(min(LA, n)):
                stageA(idx)
            pend = []
            gbank["rot"] = wrot
            for idx in range(n):
                h = items[idx][0]
                loc = idx - h * nper
                if extra and idx >= 6 and idx % 6 == 0:
                    extra.pop(0)()
                if h + 1 < 8:
                    if loc == 0:
                        pend = prolog_steps(h + 1)
                        pend.pop(0)()
                    elif pend and (nper < 60 or (loc >= 8 and loc % 2 == 0) or loc >= nper - LA - 3):
                        pend.pop(0)()
                        while pend and loc >= nper - LA - 3:
                            pend.pop(0)()
                if idx + LA < n:
                    stageA(idx + LA)
                stageB(idx)
                for d in list(deferred):
                    d[0] -= 1
                    if d[0] <= 0:
                        d[1]()
                        deferred.remove(d)
            for d in deferred:
                d[1]()
            while extra:
                extra.pop(0)()

        def fox(l):
            with ExitStack() as st:
                AHp, b_AHp = alloc(st, "fAHp", [64, S], BF16)
                fbp, b_fbp = alloc(st, "ffbp", [128, NG, NT, 8], F32)
                gsteps = gmlp_prepare(st, l)
                with ExitStack() as st2:
                    f_t, bf_ = alloc(st2, "fraw", [8, S], F32)
                    nfb, b_nfb = alloc(st2, "nfb", [8, 1], F32)
                    Sd.dma("sp", f_t[:], fT, writes=[bf_])
                    Sd.dma("sp", nfb[:], fox_f_bias[l].rearrange("(h o) -> h o", o=1), writes=[b_nfb])
                    Sd.op("dve", lambda e: e.tensor_scalar(out=nfb[:], in0=nfb[:], scalar1=-1.0, scalar2=None, op0=ALU.mult), reads=[b_nfb], writes=[b_nfb])
                    Sd.op("act", lambda e: e.activation(out=f_t[:], in_=f_t[:], func=AF.Exp, bias=nfb[:, 0:1], scale=-1.0), reads=[bf_, b_nfb], writes=[bf_])
                    Sd.op("act", lambda e: e.activation(out=f_t[:], in_=f_t[:], func=AF.Ln, bias=1.0), reads=[bf_], writes=[bf_])
                    Sd.op("dve", lambda e: e.tensor_scalar(out=f_t[:], in0=f_t[:], scalar1=-1.0, scalar2=None, op0=ALU.mult), reads=[bf_], writes=[bf_])
                    AH0, b_AH0, fb0, b_fb0 = decay_prep(st2, f_t, bf_, 8, "f")
                    Sd.op("pool", lambda e: e.tensor_copy(out=AHp[:], in_=AH0[:]), reads=[b_AH0], writes=[b_AHp])
                    Sd.op("pool", lambda e: e.tensor_copy(out=fbp[:].rearrange("p a b c -> p (a b c)"), in_=fb0[:].rearrange("p a b c -> p (a b c)")),
                          reads=[b_fb0], writes=[b_fbp])
                    Sd.barrier()
                attn_core(st, "fox", l, AHp, b_AHp, fbp, b_fbp, extra=gsteps)
                Sd.barrier()

        def moba(l):
            with ExitStack() as st:
                attn_core(st, "moba", l)
                Sd.barrier()

        def ssd(l):
            with ExitStack() as st:
                AHp, b_AHp = alloc(st, "sAHp", [64, S], BF16)
                fbp, b_fbp = alloc(st, "sfbp", [128, NG, NT, 12], F32)
                dtk, b_dtk = alloc(st, "dtk", [128, NT, 12], F32)
                with ExitStack() as st2:
                    dT, b_dT = alloc(st2, "dT", [12, S], F32)
                    dtb, b_dtb = alloc(st2, "dtb", [12, 1], F32)
                    alg, b_alg = alloc(st2, "alg", [12, 1], F32)
                    Sd.dma("sp", dT[:], dtT, writes=[b_dT])
                    Sd.dma("sp", dtb[:], dt_bias[l].rearrange("(h o) -> h o", o=1), writes=[b_dtb])
                    Sd.dma("sp", alg[:], a_log[l].rearrange("(h o) -> h o", o=1), writes=[b_alg])
                    Sd.op("act", lambda e: e.activation(out=alg[:], in_=alg[:], func=AF.Exp), reads=[b_alg], writes=[b_alg])
                    Sd.op("act", lambda e: e.activation(out=dT[:], in_=dT[:], func=AF.Exp, bias=dtb[:, 0:1]), reads=[b_dT, b_dtb], writes=[b_dT])
                    Sd.op("act", lambda e: e.activation(out=dT[:], in_=dT[:], func=AF.Ln, bias=1.0), reads=[b_dT], writes=[b_dT])
                    Sd.op("dve", lambda e: e.tensor_scalar(out=dT[:], in0=dT[:], scalar1=alg[:, 0:1], scalar2=-1.0, op0=ALU.mult, op1=ALU.mult),
                          reads=[b_dT, b_alg], writes=[b_dT])
                    AH0, b_AH0, fb0, b_fb0 = decay_prep(st2, dT, b_dT, 12, "s")
                    Sd.op("pool", lambda e: e.tensor_copy(out=AHp[:], in_=AH0[:]), reads=[b_AH0], writes=[b_AHp])
                    Sd.op("pool", lambda e: e.tensor_copy(out=fbp[:].rearrange("p a b c -> p (a b c)"), in_=fb0[:].rearrange("p a b c -> p (a b c)")),
                          reads=[b_fb0], writes=[b_fbp])
                    dtbb, b_dtbb = alloc(st2, "dtbb", [128, 12], F32)
                    Sd.dma("sp", dtk[:], dtm.rearrange("(i p) h -> p i h", p=128), writes=[b_dtk])
                    Sd.dma("sp", dtbb[:], dt_bias[l].partition_broadcast(128), writes=[b_dtbb])
                    Sd.op("dve", lambda e: e.tensor_tensor(out=dtk[:], in0=dtk[:], in1=dtbb[:].unsqueeze(1).to_broadcast([128, NT, 12]), op=ALU.add),
                          reads=[b_dtk, b_dtbb], writes=[b_dtk])
                    Sd.op("act", lambda e: e.activation(out=dtk[:], in_=dtk[:], func=AF.Exp), reads=[b_dtk], writes=[b_dtk])
                    Sd.op("act", lambda e: e.activation(out=dtk[:], in_=dtk[:], func=AF.Ln, bias=1.0), reads=[b_dtk], writes=[b_dtk])
                    Sd.barrier()
                AH, b_AH, fb, b_fb = AHp, b_AHp, fbp, b_fbp
                BC, b_BC = alloc(st, "BC", [128, 4, S], BF16)
                xdtm, b_xdtm = alloc(st, "xdtm", [128, NT, 768], BF16)
                with ExitStack() as st2:
                    cw, b_cw = alloc(st2, "cw", [128, 10, 4], F32)
                    cb, b_cb = alloc(st2, "cb", [128, 10], F32)
                    Sd.dma("sp", cw[:], conv_wT[l], writes=[b_cw])
                    Sd.dma("sp", cb[:], conv_b[l], writes=[b_cb])
                    xin = Rot([alloc(st2, f"cxi{i}", [128, S], F32) for i in range(1)])
                    acc = Rot([alloc(st2, f"cac{i}", [128, S], F32) for i in range(1)])
                    xsf, b_xsf = alloc(st2, "xsf", [128, S], F32)
                    xsb = Rot([alloc(st2, f"xsb{i}", [128, S], BF16) for i in range(1)])
                    for c in range(10):
                        xi, bxi = xin.next()
                        ac, bac = acc.next()
                        Sd.dma("sp", xi[:], xbcT[c], writes=[bxi])
                        Sd.op("dve", lambda e, c=c: e.tensor_scalar(out=ac[:], in0=xi[:], scalar1=cw[:, c, 3:4], scalar2=None, op0=ALU.mult),
                              reads=[bxi, b_cw], writes=[bac])
                        for sh in (1, 2, 3):
                            Sd.op("dve", lambda e, c=c, sh=sh: e.scalar_tensor_tensor(out=ac[:, sh:S], in0=xi[:, 0:S - sh], scalar=cw[:, c, 3 - sh:4 - sh],
                                                                                     in1=ac[:, sh:S], op0=ALU.mult, op1=ALU.add),
                                  reads=[bxi, b_cw, bac], writes=[bac])
                        if c < 6:
                            Sd.op("act", lambda e, c=c: e.activation(out=xsf[:], in_=ac[:], func=AF.Silu, bias=cb[:, c:c + 1]), reads=[bac, b_cb], writes=[b_xsf])
                            xb_, bxb_ = xsb.next()
                            Sd.op("pool", lambda e: e.tensor_copy(out=xb_[:], in_=xsf[:]), reads=[b_xsf], writes=[bxb_])
                            Sd.dma("sp", xsTd[c], xb_[:], reads=[bxb_])
                            for I in range(NG):
                                bk, bb = bankrot.next()
                                for q in range(4):
                                    i = 4 * I + q
                                    Sd.op("pe", lambda e, i=i, q=q: e.transpose(out=bk[:, q * 128:(q + 1) * 128], in_=xsf[:, i * 128:(i + 1) * 128], identity=ident[:]),
                                          reads=[b_xsf, b_ident], writes=[bb])
                                Sd.op("dve", lambda e, I=I, c=c: e.tensor_tensor(
                                    out=xdtm[:, 4 * I:4 * I + 4, c * 128:(c + 1) * 128].rearrange("p i (h d) -> p i h d", h=2),
                                    in0=bk[:, :].rearrange("p (i h d) -> p i h d", i=4, h=2),
                                    in1=dtk[:, 4 * I:4 * I + 4, 2 * c:2 * c + 2].unsqueeze(3).to_broadcast([128, 4, 2, 64]), op=ALU.mult),
                                    reads=[bb, b_dtk], writes=[b_xdtm])
                        else:
                            Sd.op("act", lambda e, c=c: e.activation(out=BC[:, c - 6, :], in_=ac[:], func=AF.Silu, bias=cb[:, c:c + 1]), reads=[bac, b_cb], writes=[b_BC])
                    Sd.barrier()
                selb, b_selb = alloc(st, "selb", [64, 12 * 128], BF16)
                trib4, b_trib4 = alloc(st, "trib4", [128, 512], F32)
                Sd.dma("pool", selb[:], c_selb, writes=[b_selb])
                for q in range(4):
                    Sd.dma("sp", trib4[:, q * 128:(q + 1) * 128], c_tri, writes=[b_trib4])
                m01b, b_m01b = alloc(st, "m01b", [128, 128], BF16)
                Sd.dma("pool", m01b[:], c_m01, writes=[b_m01b])
                ARs = [alloc(st, f"AR_{hh}", [128, 512], F32) for hh in range(6)]
                ARDs = [alloc(st, f"ARD_{hh}", [128, 512], F32) for hh in range(6)]
                Gs = Rot([alloc(st, f"Gs{i}", [128, 512], BF16) for i in range(3)])
                DT_ = Rot([alloc(st, f"DTt{i}", [128, 512], F32) for i in range(3)])
                WT = Rot([alloc(st, f"WT{i}", [128, 512], BF16) for i in range(4)])
                yo = Rot([alloc(st, f"syo{i}", [128, 512], F32) for i in range(2)])
                yoff = [alloc(st, f"yoff{i}", [128, 512], F32) for i in range(3)]
                xsc = Rot([alloc(st, f"xsc{i}", [128, 6, 64], BF16) for i in range(3)])
                Vb = Rot([alloc(st, f"Vb{i}", [128, 512], F32) for i in range(2)])
                U, b_U = alloc(st, "Utab", [128, NG, NT, 12], F32)
                Sd.op("act", lambda e: e.activation(out=U[:].rearrange("p a b c -> p (a b c)"), in_=fb[:].rearrange("p a b c -> p (a b c)"), func=AF.Exp),
                      reads=[b_fb], writes=[b_U])
                accb = banks[0:3]
                gbank = Rot(banks[3:6])
                misc = Rot(banks[6:8])
                groups = [(g, I) for g in range(2) for I in range(NG)]

                def ar_setup(gi):
                    g, I = groups[gi]
                    for hh in range(6):
                        h = 6 * g + hh
                        bk, bb = misc.next()
                        Sd.op("pe", lambda e: e.matmul(bk[:, :], lhsT=selb[0:64, h * 128:(h + 1) * 128], rhs=AH[0:64, I * 512:(I + 1) * 512],
                                                       start=True, stop=True), reads=[b_selb, b_AH], writes=[bb])
                        ar, bar = ARs[hh]
                        ard, bard = ARDs[hh]
                        Sd.op("dve", lambda e: e.tensor_copy(out=ar[:], in_=bk[:, :]), reads=[bb], writes=[bar])
                        Sd.op("pool", lambda e: e.tensor_tensor(out=ard[:], in0=ar[:], in1=trib4[:], op=ALU.add), reads=[bar, b_trib4], writes=[bard])

                for gi, (g, I) in enumerate(groups):
                    ar_setup(gi)
                    gst = {}

                    def emitG(j):
                        m = j - 4 * I
                        c0 = max(m, 0) * 128
                        gb, gbb = gbank.next()
                        Sd.op("pe", lambda e: e.matmul(gb[:, c0:512], lhsT=BC[:, g, j * 128:(j + 1) * 128],
                                                       rhs=BC[:, 2 + g, I * 512 + c0:(I + 1) * 512], start=True, stop=True),
                              reads=[b_BC], writes=[gbb])
                        gst[j] = (gb, gbb, c0, m)

                    def evacG(j):
                        gb, gbb, c0, m = gst[j]
                        gs, bgs = Gs.next()
                        if m < 0:
                            Sd.op("act", lambda e: e.copy(out=gs[:, :], in_=gb[:, :]), reads=[gbb], writes=[bgs])
                        else:
                            Sd.op("dve", lambda e: e.tensor_tensor(out=gs[:, c0:c0 + 128], in0=gb[:, c0:c0 + 128], in1=m01b[:], op=ALU.mult),
                                  reads=[gbb, b_m01b], writes=[bgs])
                            if c0 + 128 < 512:
                                Sd.op("act", lambda e: e.copy(out=gs[:, c0 + 128:512], in_=gb[:, c0 + 128:512]), reads=[gbb], writes=[bgs])
                        gst[j] = (gs, bgs, c0, m)

                    noff = 4 * I
                    if noff > 0:
                        emitG(0); evacG(0)
                        if noff > 1:
                            emitG(1); evacG(1)
                        for j in range(noff):
                            if j + 2 < noff:
                                emitG(j + 2)
                            gs, bgs, c0, m = gst[j]
                            x_t, bxs = xsc.next()
                            Sd.op("dve", lambda e: e.tensor_tensor(out=x_t[:], in0=xdtm[:, j, g * 384:(g + 1) * 384].rearrange("p (h d) -> p h d", h=6),
                                                                   in1=U[:, I, j, 6 * g:6 * g + 6].unsqueeze(2).to_broadcast([128, 6, 64]), op=ALU.mult),
                                  reads=[b_xdtm, b_U], writes=[bxs])
                            for b3 in range(3):
                                ob, obb = accb[b3]
                                Sd.op("pe", lambda e: e.matmul(ob[:, :], lhsT=x_t[:, 2 * b3:2 * b3 + 2, :].rearrange("p h d -> p (h d)"), rhs=gs[:, :],
                                                               start=(j == 0), stop=(j == noff - 1)),
                                      reads=[bxs, bgs], writes=[obb])
                            if j + 2 < noff:
                                evacG(j + 2)
                        for b3 in range(3):
                            ob, obb = accb[b3]
                            yf, byf = yoff[b3]
                            Sd.op("act", lambda e: e.copy(out=yf[:, :], in_=ob[:, :]), reads=[obb], writes=[byf])
                    j0 = 4 * I
                    nj = 4 * I + 4
                    emitG(j0); evacG(j0)
                    emitG(j0 + 1); evacG(j0 + 1)
                    for j in range(j0, nj):
                        if j + 2 < nj:
                            emitG(j + 2)
                        gs, bgs, c0, m = gst[j]
                        for hh in range(6):
                            h = 6 * g + hh
                            ar, bar = ARs[hh]
                            ard, bard = ARDs[hh]
                            d_t, bd = DT_.next()
                            Sd.op("act", lambda e: e.activation(out=d_t[:, c0:c0 + 128], in_=ard[:, c0:c0 + 128], func=AF.Exp, bias=fb[:, I, j, h:h + 1]),
                                  reads=[bard, b_fb], writes=[bd])
                            if c0 + 128 < 512:
                                Sd.op("act", lambda e: e.activation(out=d_t[:, c0 + 128:512], in_=ar[:, c0 + 128:512], func=AF.Exp, bias=fb[:, I, j, h:h + 1]),
                                      reads=[bar, b_fb], writes=[bd])
                            w_t, bw = WT.next()
                            Sd.op("dve", lambda e: e.tensor_tensor(out=w_t[:, c0:512], in0=gs[:, c0:512], in1=d_t[:, c0:512], op=ALU.mult),
                                  reads=[bgs, bd], writes=[bw])
                            ob, obb = accb[hh // 2]
                            po = (hh % 2) * 64
                            Sd.op("pe", lambda e: e.matmul(ob[po:po + 64, c0:512], lhsT=xdtm[:, j, h * 64:(h + 1) * 64], rhs=w_t[:, c0:512],
                                                           start=(j == j0), stop=(j == nj - 1)),
                                  reads=[b_xdtm, bw], writes=[obb])
                        if j + 2 < nj:
                            evacG(j + 2)
                    for b3 in range(3):
                        ob, obb = accb[b3]
                        y_t, by = yo.next()
                        if noff == 0:
                            Sd.op("act", lambda e: e.copy(out=y_t[:, :], in_=ob[:, :]), reads=[obb], writes=[by])
                        else:
                            v_t, bvb = Vb.next()
                            for hf in range(2):
                                ar, bar = ARs[2 * b3 + hf]
                                Sd.op("act", lambda e: e.activation(out=v_t[hf * 64:(hf + 1) * 64, :], in_=ar[hf * 64:(hf + 1) * 64, :], func=AF.Exp),
                                      reads=[bar], writes=[bvb])
                            yf, byf = yoff[b3]
                            Sd.op("pool", lambda e: e.tensor_tensor(out=v_t[:, :], in0=v_t[:, :], in1=yf[:, :], op=ALU.mult), reads=[bvb, byf], writes=[bvb])
                            Sd.op("dve", lambda e: e.tensor_tensor(out=y_t[:, :], in0=ob[:, :], in1=v_t[:, :], op=ALU.add), reads=[obb, bvb], writes=[by])
                        c = 3 * g + b3
                        Sd.dma("sp", ysT[2 * c:2 * c + 2, :, I * 512:(I + 1) * 512].rearrange("h d t -> (h d) t"), y_t[:, :], reads=[by])
                Sd.barrier()

        def ssd_pass2(l):
            with ExitStack() as st:
                xsl = Rot([alloc(st, f"xsl{i}", [128, 512], BF16) for i in range(6)])
                dcol, b_dcol = alloc(st, "dcol", [128, 6], F32)
                nwc, b_nwc = alloc(st, "nwc", [128, 6], F32)
                Sd.dma("sp", dcol[:], ssm_d_rep[l], writes=[b_dcol])
                Sd.dma("sp", nwc[:], ssm_norm_w[l], writes=[b_nwc])
                ysl = Rot([alloc(st, f"ysl{i}", [128, 512], F32) for i in range(6)])
                szl = Rot([alloc(st, f"szl{i}", [128, 512], BF16) for i in range(6)])
                y1 = Rot([alloc(st, f"y1{i}", [128, 512], F32) for i in range(2)])
                y2 = [alloc(st, f"y2{i}", [128, 512], F32) for i in range(6)]
                sq = Rot([alloc(st, f"sq{i}", [128, 512], F32) for i in range(2)])
                rs = Rot([alloc(st, f"rs{i}", [128, 512], F32) for i in range(2)])
                y3 = Rot([alloc(st, f"y3{i}", [128, 512], BF16) for i in range(2)])
                ld = {}

                def ploads(u):
                    I, g = divmod(u, 2)
                    cs = slice(I * 512, (I + 1) * 512)
                    for cc in range(3):
                        c = 3 * g + cc
                        ys_t, bys = ysl.next()
                        xs_t, bxs = xsl.next()
                        sz_t, bsz = szl.next()
                        Sd.dma("sp", xs_t[:], xsTd[c, :, cs], writes=[bxs])
                        Sd.dma("sp", ys_t[:], ysT[2 * c:2 * c + 2, :, cs].rearrange("h d t -> (h d) t"), writes=[bys])
                        Sd.dma("sp", sz_t[:], szT[c, :, cs], writes=[bsz])
                        ld[(u, cc)] = (ys_t, bys, xs_t, bxs, sz_t, bsz)
                ploads(0)
                for I in range(NG):
                    cs = slice(I * 512, (I + 1) * 512)
                    for g in range(2):
                        u = 2 * I + g
                        if u + 1 < 2 * NG:
                            ploads(u + 1)
                        sb_, sbb = accrot.next()
                        for cc in range(3):
                            c = 3 * g + cc
                            ys_t, bys, xs_t, bxs, sz_t, bsz = ld.pop((u, cc))
                            a_t, ba = y1.next()
                            Sd.op("dve", lambda e, c=c: e.scalar_tensor_tensor(out=a_t[:], in0=xs_t[:], scalar=dcol[:, c:c + 1], in1=ys_t[:],
                                                                               op0=ALU.mult, op1=ALU.add), reads=[bxs, b_dcol, bys], writes=[ba])
                            y2t, by2 = y2[c]
                            Sd.op("pool", lambda e: e.tensor_tensor(out=y2t[:], in0=a_t[:], in1=sz_t[:], op=ALU.mult), reads=[ba, bsz], writes=[by2])
                            q_, bq_ = sq.next()
                            Sd.op("act", lambda e: e.activation(out=q_[:], in_=y2t[:], func=AF.Square), reads=[by2], writes=[bq_])
                            Sd.op("pe", lambda e, cc=cc: e.matmul(sb_[:, :], lhsT=ones32[:], rhs=q_[:], start=(cc == 0), stop=(cc == 2)),
                                  reads=[b_ones32, bq_], writes=[sbb])
                        r_t, br = rs.next()
                        Sd.op("dve", lambda e: e.tensor_scalar(out=r_t[:], in0=sb_[:, :], scalar1=1.0 / 384, scalar2=EPS, op0=ALU.mult, op1=ALU.add),
                              reads=[sbb], writes=[br])
                        Sd.op("act", lambda e: e.activation(out=r_t[:], in_=r_t[:], func=AF.Sqrt), reads=[br], writes=[br])
                        Sd.op("dve", lambda e: e.reciprocal(out=r_t[:], in_=r_t[:]), reads=[br], writes=[br])
                        for cc in range(3):
                            c = 3 * g + cc
                            y2t, by2 = y2[c]
                            o_t, bo = y3.next()
                            Sd.op("dve", lambda e, c=c: e.scalar_tensor_tensor(out=o_t[:], in0=y2t[:], scalar=nwc[:, c:c + 1], in1=r_t[:],
                                                                               op0=ALU.mult, op1=ALU.mult), reads=[by2, b_nwc, br], writes=[bo])
                            Sd.dma("sp", yT[12 + c, :, cs], o_t[:], reads=[bo])
                Sd.barrier()

        def resid_evac(st_rot, bk, bb, gofs, xt, bx, cg):
            t_t, bt = st_rot.next()
            Sd.op("dve", lambda e: e.tensor_tensor(out=t_t[:], in0=bk[:, :], in1=modb[:, gofs + cg * 512:gofs + (cg + 1) * 512], op=ALU.mult),
                  reads=[bb, b_modb], writes=[bt])
            Sd.op("pool", lambda e: e.tensor_tensor(out=xt[:, cg * 512:(cg + 1) * 512], in0=xt[:, cg * 512:(cg + 1) * 512], in1=t_t[:], op=ALU.add),
                  reads=[bt, bx], writes=[bx])

        def merge(l):
            with ExitStack() as st:
                wbr, b_wbr = alloc(st, "wbr", [128, 18, D], BF16)
                wo, b_wo = alloc(st, "wo", [128, KC, D], BF16)
                for k0 in range(0, 18, 6):
                    Sd.dma("pool", wbr[:, k0:k0 + 6, :], w_br[l][k0 * 128:(k0 + 6) * 128, :].rearrange("(kc p) n -> p kc n", p=128), writes=[b_wbr])
                Sd.dma("pool", wo[:], w_out[l].rearrange("(kc p) n -> p kc n", p=128), writes=[b_wo])
                ssd_pass2(l)
                yt = Rot([alloc(st, f"myt{i}", [128, 18, 512], BF16) for i in range(2)])
                gt = Rot([alloc(st, f"mgt{i}", [128, 4, 512], BF16) for i in range(4)])
                mt = Rot([alloc(st, f"mmt{i}", [128, 512], F32) for i in range(3)])
                tt = Rot([alloc(st, f"mtt{i}", [128, 512], F32) for i in range(4)])
                mg = Rot([alloc(st, f"mmg{i}", [128, KC, 512], BF16) for i in range(2)])
                xt = Rot([alloc(st, f"mxt{i}", [128, D], F32) for i in range(8)])
                rr = Rot([alloc(st, f"mrr{i}", [128, 512], F32) for i in range(3)])
                kranges = [(0, 4), (4, 8), (8, 12), (12, 18)]
                ld = {}
                gld = {}

                def mloads(tg):
                    cs = slice(tg * 512, (tg + 1) * 512)
                    y_t, by = yt.next()
                    for k0 in range(0, 18, 6):
                        Sd.dma("sp", y_t[:, k0:k0 + 6, :], yT[k0:k0 + 6, :, cs].rearrange("c p t -> p c t"), writes=[by])
                    xl = []
                    for q in range(4):
                        i = 4 * tg + q
                        x_t, bx = xt.next()
                        Sd.dma("sp", x_t[:], xs[i * 128:(i + 1) * 128, :], writes=[bx])
                        xl.append((x_t, bx))
                    ld[tg] = (y_t, by, xl)

                def gloads(k):
                    tg, c = divmod(k, 8)
                    cs = slice(tg * 512, (tg + 1) * 512)
                    g_t, bg = gt.next()
                    Sd.dma("sp", g_t[:], gT[:, :, cs].rearrange("(i c) p t -> c p i t", c=8)[c], writes=[bg])
                    gld[k] = (g_t, bg)
                mloads(0)
                gloads(0); gloads(1)
                for tg in range(NG):
                    cs = slice(tg * 512, (tg + 1) * 512)
                    if tg + 1 < NG:
                        mloads(tg + 1)
                    y_t, by, xl = ld.pop(tg)
                    m_g, bmg = mg.next()
                    for c in range(8):
                        if tg * 8 + c + 2 < NG * 8:
                            gloads(tg * 8 + c + 2)
                        g_t, bg = gld.pop(tg * 8 + c)
                        m_t, bm = mt.next()
                        for i in range(4):
                            bk, bb = bankrot.next()
                            k0, k1 = kranges[i]
                            for kc in range(k0, k1):
                                Sd.op("pe", lambda e, kc=kc: e.matmul(bk[:, :], lhsT=wbr[:, kc, c * 128:(c + 1) * 128], rhs=y_t[:, kc, :],
                                                                       start=(kc == k0), stop=(kc == k1 - 1)),
                                      reads=[b_wbr, by], writes=[bb])
                            if i == 0:
                                Sd.op("dve", lambda e, i=i: e.tensor_tensor(out=m_t[:], in0=bk[:, :], in1=g_t[:, i, :], op=ALU.mult), reads=[bb, bg], writes=[bm])
                            else:
                                t_t, bt = tt.next()
                                Sd.op("dve", lambda e, i=i: e.tensor_tensor(out=t_t[:], in0=bk[:, :], in1=g_t[:, i, :], op=ALU.mult), reads=[bb, bg], writes=[bt])
                                if i < 3:
                                    Sd.op("pool", lambda e: e.tensor_tensor(out=m_t[:], in0=m_t[:], in1=t_t[:], op=ALU.add), reads=[bm, bt], writes=[bm])
                                else:
                                    Sd.op("pool", lambda e: e.tensor_tensor(out=m_g[:, c, :], in0=m_t[:], in1=t_t[:], op=ALU.add), reads=[bm, bt], writes=[bmg])
                    for q in range(4):
                        i = 4 * tg + q
                        x_t, bx = xl[q]
                        for cg in range(2):
                            bk, bb = bankrot.next()
                            for kc in range(KC):
                                Sd.op("pe", lambda e, kc=kc: e.matmul(bk[:, :], lhsT=m_g[:, kc, q * 128:(q + 1) * 128], rhs=wo[:, kc, cg * 512:(cg + 1) * 512],
                                                                       start=(kc == 0), stop=(kc == KC - 1)),
                                      reads=[bmg, b_wo], writes=[bb])
                            resid_evac(rr, bk, bb, 2 * D, x_t, bx, cg)
                        Sd.dma("sp", xs[i * 128:(i + 1) * 128, :], x_t[:], reads=[bx])
                Sd.barrier()

        def ffn(l):
          with ExitStack() as st0:
            w2t, b_w2 = alloc(st0, "w2t", [128, 32, D], BF16)
            for k0 in range(0, 32, 8):
                Sd.dma("pool", w2t[:, k0:k0 + 8, :], w2[l][k0 * 128:(k0 + 8) * 128, :].rearrange("(kc p) n -> p kc n", p=128), writes=[b_w2])
            with ExitStack() as st:
                hT, b_hT = alloc(st, "hT2", [128, KC, S], BF16)
                with ExitStack() as st2:
                    norm_stage(st2, wmod[:, D:2 * D], modb[:, 3 * D:4 * D], hT, b_hT)
                    Sd.barrier()
                with ExitStack() as st2:
                    r32 = Rot([alloc(st2, f"fr{i}", [128, 512], F32) for i in range(3)])
                    hb = Rot([alloc(st2, f"fh{i}", [128, 512], BF16) for i in range(3)])

                    def ev(bk, bb, rows, cc, tg):
                        r_t, br = r32.next()
                        Sd.op("act", lambda e: e.activation(out=r_t[:], in_=bk[:, :], func=AF.Relu), reads=[bb], writes=[br])
                        h_t, bh = hb.next()
                        Sd.op("dve", lambda e: e.tensor_tensor(out=h_t[:], in0=r_t[:], in1=r_t[:], op=ALU.mult), reads=[br], writes=[bh])
                        Sd.dma("sp", hidT[cc, :, tg * 512:(tg + 1) * 512], h_t[:], reads=[bh])
                    wts = Rot([alloc(st2, f"fpw{i}", [128, KC, 512], BF16) for i in range(2)])
                    proj_fm(wts, hT, b_hT, w1[l], 4 * D, ev)
                    Sd.barrier()
            with ExitStack() as st:
                ht = Rot([alloc(st, f"f2h{i}", [128, 32, 512], BF16) for i in range(2)])
                xt = Rot([alloc(st, f"f2x{i}", [128, D], F32) for i in range(8)])
                rr = Rot([alloc(st, f"f2r{i}", [128, 512], F32) for i in range(3)])
                ld = {}

                def wloads(tg):
                    h_t, bh = ht.next()
                    for k0 in range(0, 32, 8):
                        Sd.dma("sp", h_t[:, k0:k0 + 8, :], hidT[k0:k0 + 8, :, tg * 512:(tg + 1) * 512].rearrange("c p t -> p c t"), writes=[bh])
                    xl = []
                    for q in range(4):
                        i = 4 * tg + q
                        x_t, bx = xt.next()
                        Sd.dma("sp", x_t[:], xs[i * 128:(i + 1) * 128, :], writes=[bx])
                        xl.append((x_t, bx))
                    ld[tg] = (h_t, bh, xl)
                wloads(0)
                for tg in range(NG):
                    if tg + 1 < NG:
                        wloads(tg + 1)
                    h_t, bh, xl = ld.pop(tg)
                    for q in range(4):
                        i = 4 * tg + q
                        x_t, bx = xl[q]
                        for cg in range(2):
                            bk, bb = bankrot.next()
                            for kc in range(32):
                                Sd.op("pe", lambda e, kc=kc: e.matmul(bk[:, :], lhsT=h_t[:, kc, q * 128:(q + 1) * 128], rhs=w2t[:, kc, cg * 512:(cg + 1) * 512],
                                                                       start=(kc == 0), stop=(kc == 31)),
                                      reads=[bh, b_w2], writes=[bb])
                            resid_evac(rr, bk, bb, 5 * D, x_t, bx, cg)
                        Sd.dma("sp", xs[i * 128:(i + 1) * 128, :], x_t[:], reads=[bx])
                Sd.barrier()

        for l in range(depth):
            adaln(l)
            with ExitStack() as st:
                hT, b_hT = alloc(st, "hT", [128, KC, S], BF16)
                with ExitStack() as st2:
                    norm_stage(st2, wmod[:, 0:D], modb[:, 0:D], hT, b_hT)
                    Sd.barrier()
                with ExitStack() as st2:
                    mixer_proj(l, st2, hT, b_hT)
                    Sd.barrier()
            fox(l)
            moba(l)
            ssd(l)
            merge(l)
            ffn(l)
        with ExitStack() as st:
            fw, b_fw = alloc(st, "fw", [128, D], F32)
            Sd.dma("sp", fw[:], final_norm_w.partition_broadcast(128), writes=[b_wmod])
            Sd.barrier()
            norm_stage(st, fw[:], None, None, None, final=True)
            Sd.barrier()
        Sd.finish()
        build.ninst = dict(Sd.ninst)
        build.nsem = Sd.nsem
    return nc


def _perm64():
    return np.array(list(range(0, 8)) + list(range(16, 40)) + list(range(8, 16)) + list(range(40, 64)))


def host_consts(S):
    NT = S // 128
    s = np.arange(128)[:, None]
    t = np.arange(128)[None, :]
    c = {}
    c["c_ident"] = np.eye(128, dtype=np.float32)
    c["c_tri"] = np.where(t >= s, 0.0, NEG).astype(np.float32)
    c["c_m01"] = (s <= t).astype(np.float32)
    half = 8
    inv_freq = (500000.0 ** (-np.arange(half, dtype=np.float32) / half)).astype(np.float32)
    ang = np.arange(S, dtype=np.float32)[None, :] * inv_freq[:, None]
    rope = np.zeros((2, 40, S), np.float32)
    for r0 in (0, 32):
        rope[0, r0:r0 + 8] = np.cos(ang)
        rope[1, r0:r0 + 8] = np.sin(ang)
    rope[0, 8:32] = 1.0
    c["c_rope"] = rope
    rot = np.zeros((64, 64), np.float32)
    for r in range(8):
        rot[32 + r, r] = -1.0
        rot[r, 32 + r] = 1.0
    c["c_rot"] = rot
    oh = np.zeros((32, S), np.float32)
    blk = np.arange(S) // 256
    for n in range(min(16, S // 256)):
        oh[n, blk == n] = 30000.0
    c["c_onehot"] = oh
    e0 = np.zeros((128, 128), np.float32); e0[0, :] = 1.0
    c["c_e0"] = e0
    sel = np.zeros((64, 12, 32), np.float32)
    for h in range(12):
        sel[h, h, 0] = 1.0
        sel[32 + h, h, 1] = 1.0
    c["c_sel"] = sel.reshape(64, 12 * 32)
    selb = np.zeros((64, 12, 128), np.float32)
    for h in range(12):
        selb[h, h, :] = 1.0
        selb[32 + h, h, :] = 1.0
    c["c_selb"] = selb.reshape(64, 12 * 128)
    ownneg = np.zeros((128, NT, 16), np.float32)
    own0 = np.ones((128, NT, 16), np.float32)
    for i in range(NT):
        own = i // 2
        ownneg[:, i, own:] = -1e9
        own0[:, i, own] = 0.0
    c["c_ownneg"] = ownneg.reshape(128, NT * 16)
    c["c_own0"] = own0.reshape(128, NT * 16)
    return c


def host_weights(inp, depth):
    f = lambda a: np.ascontiguousarray(np.asarray(a, dtype=np.float32))
    w = {}
    Ld = depth
    for k in ("ada_w", "ada_b", "norm_mix_w", "norm_mlp_w", "w_in", "gmlp_ln_w", "gmlp_ln_b", "gmlp_bs",
              "fox_f_bias", "w_out", "mlp_w1", "mlp_w2"):
        w[k] = f(inp[k][:Ld])
    w["final_norm_w"] = f(inp["final_norm_w"])
    w_in = np.asarray(inp["w_in"])[:Ld]
    perm = _perm64()
    cols = []
    for base in (C_MQ, C_MK):
        for h in range(8):
            cols.extend(base + h * 64 + perm)
    w["w_mqk"] = f(w_in[:, :, np.array(cols)])
    w["gmlp_wsT"] = f(np.transpose(np.asarray(inp["gmlp_ws"])[:Ld], (0, 1, 3, 2)))
    w["conv_wT"] = f(np.transpose(np.asarray(inp["ssm_conv_w"])[:Ld], (0, 2, 1)).reshape(Ld, 10, 128, 4).transpose(0, 2, 1, 3))
    w["conv_b"] = f(np.asarray(inp["ssm_conv_b"])[:Ld].reshape(Ld, 10, 128).transpose(0, 2, 1))
    w["dt_bias"] = f(inp["ssm_dt_bias"][:Ld])
    w["a_log"] = f(inp["ssm_a_log"][:Ld])
    w["ssm_d_rep"] = f(np.repeat(np.asarray(inp["ssm_d"])[:Ld], 64, axis=1).reshape(Ld, 6, 128).transpose(0, 2, 1))
    w["ssm_norm_w"] = f(np.asarray(inp["ssm_norm_w"])[:Ld].reshape(Ld, 6, 128).transpose(0, 2, 1))
    w["w_br"] = f(np.concatenate([np.asarray(inp[k])[:Ld] for k in ("w_branch_a", "w_branch_b", "w_branch_c", "w_branch_d")], axis=1))
    return w


_CACHE = {}


def run(inp, S, depth, ncores, dbg=False):
    key = (S, depth, dbg)
    if key not in _CACHE:
        _CACHE[key] = build(S, depth, dbg)
    nc = _CACHE[key]
    shared = host_weights(inp, depth)
    shared.update(host_consts(S))
    x = np.asarray(inp["x"], dtype=np.float32)
    c = np.asarray(inp["c"], dtype=np.float32)
    in_maps = []
    for b in range(ncores):
        m = dict(shared)
        m["x"] = np.ascontiguousarray(x[b])
        m["cT"] = np.ascontiguousarray(c[b].reshape(KC, 128).T)
        in_maps.append(m)
    res = run_bass_kernel_spmd(nc, in_maps, core_ids=list(range(ncores)))
    return res.results


def kernel(**inputs):
    B, S, _ = inputs["x"].shape
    depth = inputs["w_in"].shape[0]
    res = run(inputs, S, depth, B)
    return np.stack([np.asarray(r["out"], dtype=np.float32) for r in res], axis=0)
```
